# Optimizing a Trainium2 kernel written in Bass

```python
import math
import jax, jax.numpy as jnp
from jax import lax
import numpy as np

D_MODEL = 1024
BATCH = 16
SEQ = 2048
DEPTH = 2
DEC_BATCH = 128
DEC_SEQ = 1
PAST_LEN = 16384
PAGE_SIZE = 128

N_MIXERS = 2
N_MLSTM_LAYERS = (DEPTH + 1) // 2
N_MLA_LAYERS = DEPTH // 2

ML_HEADS = 8
ML_DQK = D_MODEL // 2 // ML_HEADS
ML_DV = D_MODEL // ML_HEADS
ML_CHUNK = 64
ML_IN = 2 * ML_HEADS * ML_DQK + 2 * ML_HEADS * ML_DV + 2 * ML_HEADS

MLA_HEADS = 8
Q_LORA = 384
KV_LORA = 256
QK_NOPE = 128
QK_ROPE = 64
V_DIM = 128
MLA_IN = Q_LORA + KV_LORA + QK_ROPE
MLA_SCALE = (QK_NOPE + QK_ROPE) ** -0.5
ROPE_THETA = 10000.0
Q_BLOCK = 128

D_FF = -(-8 * D_MODEL // (3 * 256)) * 256
EPS = 1e-6

kernel_name = 'hybrid_mlstm_mla_decoder_step'


def rmsnorm(x, g):
    x32 = x.astype(jnp.float32)
    y = x32 * lax.rsqrt(jnp.mean(x32 * x32, axis=-1, keepdims=True) + EPS)
    return (y * g.astype(jnp.float32)).astype(x.dtype)


def swiglu(x, w_gate_up, w_down):
    g, u = jnp.split(x @ w_gate_up, 2, axis=-1)
    return (jax.nn.silu(g) * u) @ w_down


def rope(x, pos):
    d = x.shape[-1]
    inv = ROPE_THETA ** (-jnp.arange(0, d, 2, dtype=jnp.float32) / d)
    ang = pos[:, None] * inv[None, :]
    cos = jnp.cos(ang)[None, :, None, :]
    sin = jnp.sin(ang)[None, :, None, :]
    x32 = x.astype(jnp.float32)
    x1, x2 = x32[..., : d // 2], x32[..., d // 2:]
    return jnp.concatenate([x1 * cos - x2 * sin, x1 * sin + x2 * cos], axis=-1).astype(x.dtype)


def mlstm_chunkwise(q, k, v, i_pre, log_f, C0, n0, m0):
    B, S, H, DK = q.shape
    DV = v.shape[-1]
    L = math.gcd(S, ML_CHUNK)
    NC = S // L

    def to_chunks(a):
        a = a.reshape((B, NC, L) + a.shape[2:])
        return jnp.moveaxis(a, [1, 3], [0, 2])

    qc, kc, vc = (to_chunks(a.astype(jnp.float32)) for a in (q, k, v))
    ic, fc = to_chunks(i_pre), to_chunks(log_f)
    causal = jnp.tril(jnp.ones((L, L), dtype=bool))

    def step(carry, xs):
        C, n, m = carry
        qb, kb, vb, ib, fb = xs
        b = jnp.cumsum(fb, axis=-1)
        log_inter = b + m[..., None]
        log_intra = b[..., :, None] - b[..., None, :] + ib[..., None, :]
        log_intra = jnp.where(causal, log_intra, -jnp.inf)
        m_t = jnp.maximum(log_inter, jnp.max(log_intra, axis=-1))
        w_inter = jnp.exp(log_inter - m_t)
        w_intra = jnp.exp(log_intra - m_t[..., None])
        s = jnp.einsum('bhtd,bhsd->bhts', qb, kb) * w_intra
        num = w_inter[..., None] * jnp.einsum('bhtd,bhde->bhte', qb, C) + jnp.einsum('bhts,bhse->bhte', s, vb)
        den = w_inter * jnp.einsum('bhtd,bhd->bht', qb, n) + jnp.sum(s, axis=-1)
        h = num / jnp.maximum(jnp.abs(den), jnp.exp(-m_t))[..., None]
        m_new = m_t[..., -1]
        bL = b[..., -1]
        w_C = jnp.exp(bL + m - m_new)
        w_s = jnp.exp(bL[..., None] - b + ib - m_new[..., None])
        C_new = w_C[..., None, None] * C + jnp.einsum('bhs,bhsd,bhse->bhde', w_s, kb, vb)
        n_new = w_C[..., None] * n + jnp.einsum('bhs,bhsd->bhd', w_s, kb)
        return (C_new, n_new, m_new), h

    carry0 = (C0.astype(jnp.float32), n0.astype(jnp.float32), m0.astype(jnp.float32))
    (C, n, m), hc = lax.scan(step, carry0, (qc, kc, vc, ic, fc))
    h = jnp.moveaxis(hc, [0, 2], [1, 3]).reshape(B, S, H, DV)
    return h, C, n, m


def mlstm_mixer(x, w_in, b_gates, g_head, w_out, C0, n0, m0):
    B, S, _ = x.shape
    dqk = ML_HEADS * ML_DQK
    dv = ML_HEADS * ML_DV
    q, k, v, o, gates = jnp.split(x @ w_in, [dqk, 2 * dqk, 2 * dqk + dv, 2 * dqk + 2 * dv], axis=-1)
    q = q.reshape(B, S, ML_HEADS, ML_DQK)
    k = k.reshape(B, S, ML_HEADS, ML_DQK) * (ML_DQK ** -0.5)
    v = v.reshape(B, S, ML_HEADS, ML_DV)
    gates = gates.astype(jnp.float32) + b_gates.astype(jnp.float32)
    i_pre = gates[..., :ML_HEADS]
    log_f = jax.nn.log_sigmoid(gates[..., ML_HEADS:])
    h, C, n, m = mlstm_chunkwise(q, k, v, i_pre, log_f, C0, n0, m0)
    h = h * lax.rsqrt(jnp.mean(h * h, axis=-1, keepdims=True) + EPS)
    h = h.reshape(B, S, dv) * g_head.astype(jnp.float32)
    y = (jax.nn.sigmoid(o.astype(jnp.float32)) * h).astype(x.dtype) @ w_out
    return y, C, n, m


def mla_project(x, w_in, g_q, g_kv, w_uq, w_uk, pos):
    B, S, _ = x.shape
    c_q, c_kv, k_r = jnp.split(x @ w_in, [Q_LORA, Q_LORA + KV_LORA], axis=-1)
    q = (rmsnorm(c_q, g_q) @ w_uq).reshape(B, S, MLA_HEADS, QK_NOPE + QK_ROPE)
    q_nope = q[..., :QK_NOPE]
    q_rope = rope(q[..., QK_NOPE:], pos)
    c_kv = rmsnorm(c_kv, g_kv)
    k_r = rope(k_r[:, :, None, :], pos)[:, :, 0, :]
    q_lat = jnp.einsum('bshn,chn->bshc', q_nope, w_uk.reshape(KV_LORA, MLA_HEADS, QK_NOPE))
    return q_lat, q_rope, c_kv, k_r


def mla_output(o_lat, w_uv, w_o):
    B, S = o_lat.shape[:2]
    v = jnp.einsum('bshc,chv->bshv', o_lat, w_uv.reshape(KV_LORA, MLA_HEADS, V_DIM))
    return v.reshape(B, S, MLA_HEADS * V_DIM) @ w_o


def mla_attend_prompt(q_lat, q_rope, c_kv, k_r):
    B, S, H, _ = q_lat.shape
    NB = S // Q_BLOCK
    qlb = jnp.moveaxis(q_lat.reshape(B, NB, Q_BLOCK, H, KV_LORA), 1, 0)
    qrb = jnp.moveaxis(q_rope.reshape(B, NB, Q_BLOCK, H, QK_ROPE), 1, 0)
    kpos = jnp.arange(S)

    def block(args):
        ql, qr, start = args
        s = jnp.einsum('bqhc,bkc->bhqk', ql, c_kv) + jnp.einsum('bqhr,bkr->bhqk', qr, k_r)
        s = s.astype(jnp.float32) * MLA_SCALE
        qpos = start + jnp.arange(Q_BLOCK)
        s = jnp.where(kpos[None, :] <= qpos[:, None], s, -jnp.inf)
        p = jax.nn.softmax(s, axis=-1)
        return jnp.einsum('bhqk,bkc->bqhc', p.astype(c_kv.dtype), c_kv)

    out = lax.map(block, (qlb, qrb, jnp.arange(NB) * Q_BLOCK))
    return jnp.moveaxis(out, 0, 1).reshape(B, S, H, KV_LORA)


def mla_attend_sample(q_lat, q_rope, c_new, kr_new, cache_latent, cache_k_rope, page_table, layer_idx):
    T = q_lat.shape[1]
    past = page_table.shape[1] * cache_latent.shape[2]
    kpos = jnp.arange(past + T)
    qpos = past + jnp.arange(T)
    mask = kpos[None, :] <= qpos[:, None]

    def one_seq(args):
        ql, qr, cn, krn, pages = args
        c_all = jnp.concatenate([cache_latent[layer_idx, pages].reshape(past, KV_LORA), cn], axis=0)
        kr_all = jnp.concatenate([cache_k_rope[layer_idx, pages].reshape(past, QK_ROPE), krn], axis=0)
        s = jnp.einsum('qhc,kc->hqk', ql, c_all) + jnp.einsum('qhr,kr->hqk', qr, kr_all)
        s = jnp.where(mask, s.astype(jnp.float32) * MLA_SCALE, -jnp.inf)
        p = jax.nn.softmax(s, axis=-1)
        return jnp.einsum('hqk,kc->qhc', p.astype(c_all.dtype), c_all)

    return lax.map(one_seq, (q_lat, q_rope, c_new, kr_new, page_table))


def setup_inputs(seed: int = 0) -> dict:
    key = jax.random.key(seed)
    ks = iter(jax.random.split(key, 32))
    nrm = lambda shape, scale: jax.random.normal(next(ks), shape, jnp.float32) * scale
    n_pages = PAST_LEN // PAGE_SIZE
    n_used = DEC_BATCH * n_pages
    n_phys = n_used + n_used // 4
    page_table = jax.random.permutation(next(ks), n_phys)[:n_used].reshape(DEC_BATCH, n_pages).astype(jnp.int32)
    NA, NB = N_MLSTM_LAYERS, N_MLA_LAYERS
    b_i = nrm((NA, ML_HEADS), 0.1)
    b_f = 3.0 + 3.0 * jax.random.uniform(next(ks), (NA, ML_HEADS), jnp.float32)
    return {
        'x_prompt': nrm((BATCH, SEQ, D_MODEL), 1.0),
        'x_sample': nrm((DEC_BATCH, DEC_SEQ, D_MODEL), 1.0),
        'state_mlstm_C': nrm((NA, DEC_BATCH, ML_HEADS, ML_DQK, ML_DV), 0.5),
        'state_mlstm_n': nrm((NA, DEC_BATCH, ML_HEADS, ML_DQK), 0.5),
        'state_mlstm_m': nrm((NA, DEC_BATCH, ML_HEADS), 1.0),
        'cache_latent': nrm((NB, n_phys, PAGE_SIZE, KV_LORA), 1.0),
        'cache_k_rope': nrm((NB, n_phys, PAGE_SIZE, QK_ROPE), 1.0),
        'page_table': page_table,
        'norm_mix': 1.0 + nrm((DEPTH, D_MODEL), 0.01),
        'norm_ffn': 1.0 + nrm((DEPTH, D_MODEL), 0.01),
        'norm_final': 1.0 + nrm((D_MODEL,), 0.01),
        'mlstm_w_in': nrm((NA, D_MODEL, ML_IN), D_MODEL ** -0.5),
        'mlstm_b_gates': jnp.concatenate([b_i, b_f], axis=-1),
        'mlstm_g_head': 1.0 + nrm((NA, ML_HEADS * ML_DV), 0.01),
        'mlstm_w_out': nrm((NA, ML_HEADS * ML_DV, D_MODEL), (ML_HEADS * ML_DV) ** -0.5),
        'mla_w_in': nrm((NB, D_MODEL, MLA_IN), D_MODEL ** -0.5),
        'mla_g_q': 1.0 + nrm((NB, Q_LORA), 0.01),
        'mla_g_kv': 1.0 + nrm((NB, KV_LORA), 0.01),
        'mla_w_uq': nrm((NB, Q_LORA, MLA_HEADS * (QK_NOPE + QK_ROPE)), Q_LORA ** -0.5),
        'mla_w_uk': nrm((NB, KV_LORA, MLA_HEADS * QK_NOPE), KV_LORA ** -0.5),
        'mla_w_uv': nrm((NB, KV_LORA, MLA_HEADS * V_DIM), KV_LORA ** -0.5),
        'mla_w_o': nrm((NB, MLA_HEADS * V_DIM, D_MODEL), (MLA_HEADS * V_DIM) ** -0.5),
        'ffn_w_gate_up': nrm((DEPTH, D_MODEL, 2 * D_FF), D_MODEL ** -0.5),
        'ffn_w_down': nrm((DEPTH, D_FF, D_MODEL), D_FF ** -0.5),
    }


def reference(x_prompt, x_sample, state_mlstm_C, state_mlstm_n, state_mlstm_m, cache_latent, cache_k_rope,
              page_table, norm_mix, norm_ffn, norm_final, mlstm_w_in, mlstm_b_gates, mlstm_g_head, mlstm_w_out,
              mla_w_in, mla_g_q, mla_g_kv, mla_w_uq, mla_w_uk, mla_w_uv, mla_w_o, ffn_w_gate_up, ffn_w_down):
    B, S, _ = x_prompt.shape
    T = x_sample.shape[1]
    past_len = page_table.shape[1] * cache_latent.shape[2]
    pos_p = jnp.arange(S, dtype=jnp.float32)
    pos_s = jnp.arange(T, dtype=jnp.float32) + past_len
    hp, hs = x_prompt, x_sample
    C_p, n_p, m_p, C_s, n_s, m_s = [], [], [], [], [], []
    lat_p, kr_p, lat_s, kr_s = [], [], [], []
    for layer in range(DEPTH):
        j = layer // N_MIXERS
        up = rmsnorm(hp, norm_mix[layer])
        us = rmsnorm(hs, norm_mix[layer])
        if layer % N_MIXERS == 0:
            zC = jnp.zeros((B, ML_HEADS, ML_DQK, ML_DV), jnp.float32)
            zn = jnp.zeros((B, ML_HEADS, ML_DQK), jnp.float32)
            zm = jnp.zeros((B, ML_HEADS), jnp.float32)
            yp, Cp, np_, mp = mlstm_mixer(up, mlstm_w_in[j], mlstm_b_gates[j], mlstm_g_head[j], mlstm_w_out[j], zC, zn, zm)
            ys, Cs, ns_, ms = mlstm_mixer(us, mlstm_w_in[j], mlstm_b_gates[j], mlstm_g_head[j], mlstm_w_out[j],
                                          state_mlstm_C[j], state_mlstm_n[j], state_mlstm_m[j])
            C_p.append(Cp); n_p.append(np_); m_p.append(mp)
            C_s.append(Cs); n_s.append(ns_); m_s.append(ms)
        else:
            qlp, qrp, cp, krp = mla_project(up, mla_w_in[j], mla_g_q[j], mla_g_kv[j], mla_w_uq[j], mla_w_uk[j], pos_p)
            yp = mla_output(mla_attend_prompt(qlp, qrp, cp, krp), mla_w_uv[j], mla_w_o[j])
            qls, qrs, cs, krs = mla_project(us, mla_w_in[j], mla_g_q[j], mla_g_kv[j], mla_w_uq[j], mla_w_uk[j], pos_s)
            ys = mla_output(mla_attend_sample(qls, qrs, cs, krs, cache_latent, cache_k_rope, page_table, j),
                            mla_w_uv[j], mla_w_o[j])
            lat_p.append(cp); kr_p.append(krp); lat_s.append(cs); kr_s.append(krs)
        hp = hp + yp
        hs = hs + ys
        hp = hp + swiglu(rmsnorm(hp, norm_ffn[layer]), ffn_w_gate_up[layer], ffn_w_down[layer])
        hs = hs + swiglu(rmsnorm(hs, norm_ffn[layer]), ffn_w_gate_up[layer], ffn_w_down[layer])
    y_prompt = rmsnorm(hp, norm_final)
    y_sample = rmsnorm(hs, norm_final)
    return (y_prompt, y_sample,
            jnp.stack(C_p), jnp.stack(n_p), jnp.stack(m_p),
            jnp.stack(C_s), jnp.stack(n_s), jnp.stack(m_s),
            jnp.stack(lat_p), jnp.stack(kr_p), jnp.stack(lat_s), jnp.stack(kr_s))
```

```python
import numpy as np
from contextlib import ExitStack
import concourse.bass as bass
import concourse.mybir as mybir
from concourse.bass_utils import run_bass_kernel_spmd

F32, BF16, I32 = mybir.dt.float32, mybir.dt.bfloat16, mybir.dt.int32
ALU = mybir.AluOpType
AF = mybir.ActivationFunctionType
AX = mybir.AxisListType

D = 1024
KC = 8
NH = 8
DQK = 64
DV = 128
ML_IN = 3088
DFF = 2816
EPS = 1e-6
QL, KVL, RP = 384, 256, 64
MLA_SCALE = float((128 + 64) ** -0.5)


class Buf:
    __slots__ = ("name", "writer", "readers")

    def __init__(self, name):
        self.name = name
        self.writer = None
        self.readers = []


class Prog:
    ENG = ["pe", "act", "dve", "pool", "sp"]
    EMAP = {"pe": "tensor", "act": "scalar", "dve": "vector", "pool": "gpsimd", "sp": "sync"}

    def __init__(self, nc, es, ndma=14):
        self.nc = nc
        self.ops = {e: [] for e in self.ENG}
        self.cnt = {e: 0 for e in ["pe", "act", "dve", "pool"]}
        self.semh = {}
        for e in ["pe", "act", "dve", "pool"]:
            self.semh["c_" + e] = es.enter_context(nc.semaphore("c_" + e))
        self.ndma = ndma
        self.dma_cnt = {}
        self.dma_rr = {"sp": 0, "pool": 0}
        for q in ["sp", "pool"]:
            for i in range(ndma):
                k = f"d_{q}{i}"
                self.semh[k] = es.enter_context(nc.semaphore(k))
                self.dma_cnt[k] = 0
        self.seen = {e: {} for e in self.ENG}

    def _deps(self, eng, reads, writes):
        need = {}

        def add(tok):
            if tok is None:
                return
            k, v = tok
            if need.get(k, 0) < v:
                need[k] = v

        for b in reads:
            add(b.writer)
        for b in writes:
            add(b.writer)
            for t in b.readers:
                add(t)
        waits = []
        for k, v in need.items():
            if k == "c_pe" and eng == "pe":
                continue
            if self.seen[eng].get(k, 0) < v:
                self.seen[eng][k] = v
                waits.append((k, v))
        return waits

    def _upd(self, tok, reads, writes):
        for b in reads:
            b.readers.append(tok)
        for b in writes:
            b.writer = tok
            b.readers = []

    def op(self, eng, fn, reads=(), writes=()):
        waits = self._deps(eng, reads, writes)
        self.cnt[eng] += 1
        tok = ("c_" + eng, self.cnt[eng])
        self.ops[eng].append((waits, fn, ("c_" + eng, 1)))
        self._upd(tok, reads, writes)

    def dma(self, q, fn, reads=(), writes=()):
        i = self.dma_rr[q]
        self.dma_rr[q] = (i + 1) % self.ndma
        key = f"d_{q}{i}"
        waits = self._deps(q, reads, writes)
        prev = self.dma_cnt[key]
        if prev > 0 and self.seen[q].get(key, 0) < prev:
            self.seen[q][key] = prev
            waits.append((key, prev))
        self.dma_cnt[key] += 16
        tok = (key, self.dma_cnt[key])
        self.ops[q].append((waits, fn, (key, 16)))
        self._upd(tok, reads, writes)

    def emit(self):
        nc = self.nc
        with nc.Block() as block:
            for e in self.ENG:
                def body(eng, e=e):
                    for waits, fn, inc in self.ops[e]:
                        for k, v in waits:
                            eng.wait_ge(self.semh[k], v)
                        ins = fn(eng)
                        ins.then_inc(self.semh[inc[0]], inc[1])
                    if e in ("sp", "pool"):
                        for k, c in self.dma_cnt.items():
                            if k.startswith(f"d_{e}") and c > 0:
                                eng.wait_ge(self.semh[k], c)
                getattr(block, self.EMAP[e])(body)


def build(cfg):
    S = cfg["S"]
    NSP = cfg["NSP"]
    GT = cfg["GT"]
    NS = cfg["NS"]
    NPG = cfg["NPG"]
    NPHYS = cfg["NPHYS"]
    stage = cfg.get("stage", 99)
    NT = GT // 128
    NG = S // GT
    SLAB = min(512, GT)

    nc = bass.Bass("TRN2", target_bir_lowering=False)
    es = ExitStack()
    P = Prog(nc, es)

    def din(name, shape, dt=F32):
        return nc.dram_tensor(name, list(shape), dt, kind="ExternalInput").ap()

    def dout(name, shape, dt=F32):
        return nc.dram_tensor(name, list(shape), dt, kind="ExternalOutput").ap()

    xp = din("xp", [NSP * S, D])
    xs = din("xs", [NS, D])
    stC = din("stC", [NS, NH, DQK, DV])
    stn = din("stn", [NS, NH, DQK])
    stm = din("stm", [NS, NH])
    USE_CACHE = cfg.get("use_cache", False)
    if USE_CACHE:
        lat = din("lat", [NPHYS * 128, KVL])
        kr = din("kr", [NPHYS * 128, RP])
        ptb = din("ptb", [1, NS * NPG], I32)
    gfm = din("gfm", [128, 5, KC])
    w_in_ml = din("w_in_ml", [D, ML_IN])
    b_gates = din("b_gates", [1, 16])
    g_head = din("g_head", [1, D])
    w_out_ml = din("w_out_ml", [D, D])
    w_in_mla = din("w_in_mla", [D, QL + KVL + RP])
    g_q = din("g_q", [1, QL])
    g_kv = din("g_kv", [1, KVL])
    w_uq = din("w_uq", [QL, 8 * 192])
    w_uk = din("w_uk", [KVL, 1024])
    w_uv = din("w_uv", [KVL, 1024])
    w_o = din("w_o", [D, D])
    w_gu = din("w_gu", [2, D, 2 * DFF])
    w_dn = din("w_dn", [2, DFF, D])
    ident_d = din("ident", [128, 128])
    maskT_d = din("maskT", [128, 128])
    parsel_d = din("parsel", [8, 128])
    pairsel_d = din("pairsel", [8, 4])
    gqfm = din("gqfm", [128, 3])
    g_fin = din("g_fin", [1, D])
    oh16_d = din("oh16", [128, NS, NS])
    cs_tm = din("cs_tm", [S + 1, 64])
    cs_fm = din("cs_fm", [128, 2, S + 1])

    yp = dout("yp", [NSP * S, D])
    ys = dout("ys", [NS, D])
    Cp = dout("Cp", [NSP, NH, DQK, DV])
    np_ = dout("np", [NSP, NH, DQK])
    mp = dout("mp", [NSP, NH])
    Cs = dout("Cs", [NS, NH, DQK, DV])
    ns_ = dout("ns", [NS, NH, DQK])
    ms_ = dout("ms", [NS, NH])
    latp = dout("latp", [NSP * S, KVL])
    krp = dout("krp", [NSP * S, RP])
    lats = dout("lats", [NS, KVL])
    krs = dout("krs", [NS, RP])

    def sb(name, shape, dt=F32):
        return es.enter_context(nc.sbuf_tensor("s_" + name, list(shape), dt))

    def ps(name, shape, dt=F32):
        return es.enter_context(nc.psum_tensor("p_" + name, list(shape), dt))

    ident = sb("ident", [128, 128]); B_ident = Buf("ident")
    identb = sb("identb", [128, 128], BF16); B_identb = Buf("identb")
    maskT = sb("maskT", [128, 128]); B_maskT = Buf("maskT")
    maskTb = sb("maskTb", [128, 128], BF16)
    parsel = sb("parsel", [8, 128]); pairsel = sb("pairsel", [8, 4]); B_sel = Buf("sel")
    gfm_s = sb("gfm_s", [128, 5, KC]); B_gfm = Buf("gfm")
    bg_s = sb("bg_s", [128, 16]); ghead_s = sb("ghead_s", [128, D]); B_vec = Buf("vec")
    gkv_s = sb("gkv_s", [128, KVL])
    zeros8 = sb("zeros8", [8, GT if GT > 128 else 128]); B_z8 = Buf("z8")
    ones1 = sb("ones1", [128, 1], BF16)

    P.dma("sp", lambda e: e.dma_start(out=ident[:], in_=ident_d[:, :]), writes=[B_ident])
    P.dma("sp", lambda e: e.dma_start(out=maskT[:], in_=maskT_d[:, :]), writes=[B_maskT])
    P.dma("sp", lambda e: e.dma_start(out=parsel[:], in_=parsel_d[:, :]), writes=[B_sel])
    P.dma("sp", lambda e: e.dma_start(out=pairsel[:], in_=pairsel_d[:, :]), writes=[B_sel])
    P.dma("sp", lambda e: e.dma_start(out=gfm_s[:], in_=gfm[:, :, :]), writes=[B_gfm])
    P.dma("sp", lambda e: e.dma_start(out=bg_s[:], in_=b_gates[0:1, :].partition_broadcast(128)), writes=[B_vec])
    B_gh = Buf("gh")
    P.dma("sp", lambda e: e.dma_start(out=gkv_s[:], in_=g_kv[0:1, :].partition_broadcast(128)), writes=[B_vec])
    P.op("pool", lambda e: e.tensor_copy(identb[:], ident[:]), reads=[B_ident], writes=[B_identb])
    P.op("pool", lambda e: e.tensor_copy(maskTb[:], maskT[:]), reads=[B_maskT], writes=[B_maskT])
    P.op("pool", lambda e: e.memset(zeros8[:], 0.0), writes=[B_z8])
    P.op("pool", lambda e: e.memset(ones1[:], 1.0), writes=[B_z8])

    h = sb("h", [128, NT, D])
    B_h = [Buf(f"h{t}") for t in range(NT)]
    xnT = sb("xnT", [128, KC, GT], BF16)
    B_xnT = [Buf(f"xnT{t}") for t in range(NT)]

    WCH = 2048
    NSTG, NWB = 2, 2
    stg = [sb(f"stg{i}", [128, WCH]) for i in range(NSTG)]
    B_stg = [Buf(f"stg{i}") for i in range(NSTG)]
    wbuf = [sb(f"wb{i}", [128, WCH], BF16) for i in range(NWB)]
    B_wb = [Buf(f"wb{i}") for i in range(NWB)]
    wctr = [0, 0]

    def wload(dram2d, kc, ncols):
        assert kc * ncols <= WCH
        si = wctr[0] % NSTG; wctr[0] += 1
        wi = wctr[1] % NWB; wctr[1] += 1
        sv = stg[si][:, 0:kc * ncols].rearrange("p (k n) -> p k n", k=kc)
        wv = wbuf[wi][:, 0:kc * ncols].rearrange("p (k n) -> p k n", k=kc)
        src = dram2d.rearrange("(k p) n -> p k n", p=128)
        P.dma("sp", lambda e: e.dma_start(out=sv, in_=src), writes=[B_stg[si]])
        P.op("pool", lambda e: e.tensor_copy(wv, sv), reads=[B_stg[si]], writes=[B_wb[wi]])
        return wv, B_wb[wi]

    pA = [ps(f"pA{i}", [128, 512]) for i in range(2)]; B_pA = [Buf(f"pA{i}") for i in range(2)]
    pT = ps("pT", [128, 1024], BF16); B_pT = Buf("pT")
    pM = [ps(f"pM{i}", [128, 512]) for i in range(5)]; B_pM = [Buf(f"pM{i}") for i in range(5)]
    pactr = [0]

    def next_pA():
        i = pactr[0] % 2; pactr[0] += 1
        return pA[i], B_pA[i]

    xnb = sb("xnb", [128, D], BF16); B_xnb = Buf("xnb")
    junk = xnb; B_junk = B_xnb
    st4 = sb("st4", [128, 8]); B_st4 = Buf("st4")

    def rmsnorm_T(hap, Bh, rows, gi, dstT, Bdst, width=D, gap=None):
        nk = width // 128
        P.op("dve", lambda e: e.scalar_tensor_tensor(out=junk[0:rows, 0:width], in0=hap, scalar=1.0, in1=hap,
                                                     op0=ALU.mult, op1=ALU.mult, accum_out=st4[0:rows, 0:1]),
             reads=[Bh], writes=[B_junk, B_st4])
        P.op("dve", lambda e: e.tensor_scalar(st4[0:rows, 1:2], st4[0:rows, 0:1], 1.0 / width, EPS, ALU.mult, ALU.add),
             reads=[B_st4], writes=[B_st4])
        P.op("act", lambda e: e.activation(out=st4[0:rows, 2:3], in_=st4[0:rows, 1:2], func=AF.Ln), reads=[B_st4], writes=[B_st4])
        P.op("act", lambda e: e.activation(out=st4[0:rows, 3:4], in_=st4[0:rows, 2:3], func=AF.Exp, scale=-0.5), reads=[B_st4], writes=[B_st4])
        P.op("dve", lambda e: e.tensor_scalar(xnb[0:rows, 0:width], hap, st4[0:rows, 3:4], None, ALU.mult),
             reads=[Bh, B_st4], writes=[B_xnb])
        for k in range(nk):
            P.op("pe", lambda e, k=k: e.transpose(pT[:, k * 128:k * 128 + rows], xnb[0:rows, k * 128:(k + 1) * 128], identb[0:rows, 0:rows]),
                 reads=[B_xnb, B_identb], writes=[B_pT])
        src = pT[:, 0:nk * 128].rearrange("p (k t) -> p k t", k=nk)[:, :, 0:rows]
        if gap is None:
            gap_ = gfm_s[:, gi, 0:nk]
        else:
            gap_ = gap
        P.op("dve", lambda e: e.tensor_tensor(out=dstT, in0=src, in1=gap_.unsqueeze(2).to_broadcast([128, nk, rows]), op=ALU.mult),
             reads=[B_pT, B_gfm], writes=[Bdst])

    def final_norm(hap, Bh, rows):
        P.op("dve", lambda e: e.scalar_tensor_tensor(out=junk[0:rows, :], in0=hap, scalar=1.0, in1=hap, op0=ALU.mult, op1=ALU.mult,
                                                     accum_out=st4[0:rows, 0:1]), reads=[Bh], writes=[B_junk, B_st4])
        P.op("dve", lambda e: e.tensor_scalar(st4[0:rows, 1:2], st4[0:rows, 0:1], 1.0 / D, EPS, ALU.mult, ALU.add), reads=[B_st4], writes=[B_st4])
        P.op("act", lambda e: e.activation(out=st4[0:rows, 2:3], in_=st4[0:rows, 1:2], func=AF.Ln), reads=[B_st4], writes=[B_st4])
        P.op("act", lambda e: e.activation(out=st4[0:rows, 3:4], in_=st4[0:rows, 2:3], func=AF.Exp, scale=-0.5), reads=[B_st4], writes=[B_st4])
        P.op("dve", lambda e: e.scalar_tensor_tensor(out=hap, in0=hap, scalar=st4[0:rows, 3:4], in1=gfin_s[0:rows, :], op0=ALU.mult, op1=ALU.mult),
             reads=[Bh, B_st4, B_gh], writes=[Bh])

    def load_gfin():
        P.dma("sp", lambda e: e.dma_start(out=gfin_s[:], in_=g_fin[0:1, :].partition_broadcast(128)), writes=[B_gh])

    def proj_fm(wv, Bw, kc, ncols, xT, Bx_list, ntok, evac):
        for cb in range((ncols + 127) // 128):
            m = min(128, ncols - cb * 128)
            for t0 in range(0, ntok, 512):
                n = min(512, ntok - t0)
                pa, Bp = next_pA()
                for k in range(kc):
                    P.op("pe", lambda e, k=k, pa=pa, cb=cb, m=m, t0=t0, n=n: e.matmul(
                        pa[0:m, 0:n], lhsT=wv[:, k, cb * 128:cb * 128 + m], rhs=xT[:, k, t0:t0 + n],
                        start=(k == 0), stop=(k == kc - 1)), reads=[Bw] + Bx_list, writes=[Bp])
                evac(pa[0:m, 0:n], Bp, cb, t0, n)

    def proj_tm(wv, Bw, kc, ncols, xT, Bx_list, tiles, evac):
        for ti, (c0, rows) in enumerate(tiles):
            pa, Bp = next_pA()
            for k in range(kc):
                P.op("pe", lambda e, k=k, pa=pa, c0=c0, rows=rows: e.matmul(
                    pa[0:rows, 0:ncols], lhsT=xT[:, k, c0:c0 + rows], rhs=wv[:, k, 0:ncols],
                    start=(k == 0), stop=(k == kc - 1)), reads=[Bw] + Bx_list, writes=[Bp])
            evac(pa[0:rows, 0:ncols], Bp, ti, rows)

    NF_ = DFF // 128
    ASZ = max(NF_ * GT, 8 * GT + NT * NH * 129 + NT * D, 21 * GT + 2 * NH * 128 + D + 192)
    arena = sb("arena", [128, ASZ], BF16)
    qT = arena[:, 0:4 * GT].rearrange("p (c t) -> p c t", c=4); B_qT = Buf("qT")
    kT = arena[:, 4 * GT:8 * GT].rearrange("p (c t) -> p c t", c=4); B_kT = Buf("kT")
    o0 = 8 * GT
    vaug = arena[:, o0:o0 + NT * NH * 129].rearrange("p (t h e) -> p t h e", t=NT, h=NH); B_v = [Buf(f"v{t}") for t in range(NT)]
    o1 = o0 + NT * NH * 129
    so = arena[:, o1:o1 + NT * D].rearrange("p (t d) -> p t d", t=NT); B_so = [Buf(f"so{t}") for t in range(NT)]
    gates = sb("gates", [128, NT, 16]); B_g = Buf("gates")
    gtmp = sb("gtmp", [128, NT, 8, 4]); B_gt = Buf("gtmp")
    logf = sb("logf", [128, NT, 8])
    irow = sb("irow", [8, GT]); frow = sb("frow", [8, GT]); B_rows = Buf("rows")
    Bext = sb("Bext", [8, GT + 1]); mext = sb("mext", [8, GT + 1]); B_scan = Buf("scan")
    mu = sb("mu", [8, 1]); B_mu = Buf("mu")
    rs8 = sb("rs8", [8, 16]); B_rs8 = Buf("rs8")
    zrow = sb("zrow", [8, 128]); erow = sb("erow", [8, 128]); trow = sb("trow", [8, 128]); B_zr = Buf("zr")
    dg8 = sb("dg8", [8, 4]); B_dg8 = Buf("dg8")
    et = sb("et", [128, 16]); B_et = Buf("et")
    decb = sb("decb", [128, 4]); B_decb = Buf("decb")
    Chat = sb("Chat", [128, 4, 129]); B_Ch = Buf("Chat")
    Csb = sb("Csb", [128, 4, 129], BF16); B_Cs = Buf("Csb")
    vp = sb("vp", [128, NH, 129], BF16); B_vp = Buf("vp")
    ktok = sb("ktok", [128, 512], BF16); B_kt = Buf("ktok")
    STb = sb("STb", [128, 8, 128], BF16); B_ST = [Buf("ST0"), Buf("ST1")]
    mscr = sb("mscr", [128, max(NT * 704, 2816)])
    nd = mscr[:, 0:NH * 129].rearrange("p (a b) -> p a b", a=NH); B_nd = Buf("nd")
    sqs = mscr[:, NH * 129:NH * 129 + NH * 128].rearrange("p (a b) -> p a b", a=NH); B_sq = Buf("sqs")
    p8 = sb("p8", [128, 8, 8]); B_p8 = Buf("p8")
    ogb = junk; B_og = B_junk
    hmT = sb("hmT", [128, KC, GT], BF16); B_hm = [Buf(f"hm{t}") for t in range(NT)]

    def mlstm_post(R, soap, Bso, hmT_dst, Bhm, thr_ap, Bthr):
        den = nd[0:R, :, 128]
        P.op("dve", lambda e: e.scalar_tensor_tensor(out=p8[0:R, 0, :], in0=den, scalar=-1.0, in1=den, op0=ALU.mult, op1=ALU.max), reads=[B_nd], writes=[B_p8])
        P.op("dve", lambda e: e.tensor_tensor(out=p8[0:R, 1, :], in0=p8[0:R, 0, :], in1=thr_ap, op=ALU.max), reads=[B_p8, Bthr], writes=[B_p8])
        P.op("dve", lambda e: e.reciprocal(p8[0:R, 2, :], p8[0:R, 1, :]), reads=[B_p8], writes=[B_p8])
        P.op("pool", lambda e: e.tensor_tensor(out=sqs[0:R], in0=nd[0:R, :, 0:128], in1=nd[0:R, :, 0:128], op=ALU.mult), reads=[B_nd], writes=[B_sq])
        P.op("dve", lambda e: e.tensor_reduce(out=p8[0:R, 3, :], in_=sqs[0:R], axis=AX.X, op=ALU.add), reads=[B_sq], writes=[B_p8])
        P.op("dve", lambda e: e.tensor_tensor(out=p8[0:R, 4, :], in0=p8[0:R, 2, :], in1=p8[0:R, 2, :], op=ALU.mult), reads=[B_p8], writes=[B_p8])
        P.op("dve", lambda e: e.tensor_tensor(out=p8[0:R, 4, :], in0=p8[0:R, 4, :], in1=p8[0:R, 3, :], op=ALU.mult), reads=[B_p8], writes=[B_p8])
        P.op("dve", lambda e: e.tensor_scalar(p8[0:R, 4, :], p8[0:R, 4, :], 1.0 / 128, EPS, ALU.mult, ALU.add), reads=[B_p8], writes=[B_p8])
        P.op("act", lambda e: e.activation(out=p8[0:R, 5, :], in_=p8[0:R, 4, :], func=AF.Ln), reads=[B_p8], writes=[B_p8])
        P.op("act", lambda e: e.activation(out=p8[0:R, 6, :], in_=p8[0:R, 5, :], func=AF.Exp, scale=-0.5), reads=[B_p8], writes=[B_p8])
        P.op("dve", lambda e: e.tensor_tensor(out=p8[0:R, 7, :], in0=p8[0:R, 6, :], in1=p8[0:R, 2, :], op=ALU.mult), reads=[B_p8], writes=[B_p8])
        P.op("dve", lambda e: e.tensor_tensor(out=sqs[0:R], in0=nd[0:R, :, 0:128], in1=p8[0:R, 7, :].unsqueeze(2).to_broadcast([R, NH, 128]), op=ALU.mult),
             reads=[B_nd, B_p8, B_sq], writes=[B_sq])
        P.op("dve", lambda e: e.tensor_tensor(out=ogb[0:R, :], in0=sqs[0:R].rearrange("p a b -> p (a b)"), in1=soap, op=ALU.mult),
             reads=[B_sq, Bso], writes=[B_og])
        for k in range(KC):
            P.op("pe", lambda e, k=k: e.transpose(pT[:, k * 128:k * 128 + R], ogb[0:R, k * 128:(k + 1) * 128], identb[0:R, 0:R]),
                 reads=[B_og, B_identb], writes=[B_pT])
        P.op("act", lambda e: e.copy(hmT_dst, pT[:, :].rearrange("p (k t) -> p k t", k=KC)[:, :, 0:R]), reads=[B_pT], writes=[Bhm])


    def mlstm_chunk(rows, tcol, vap, Bv, soap, Bso, hmT_dst, Bhm, Bprev_col, Bslice, irow_sl, mask_ap, seq_first):
        R = rows
        P.op("dve", lambda e: e.scalar_tensor_tensor(out=zrow[:, 0:R], in0=irow_sl, scalar=Bprev_col, in1=Bslice,
                                                     op0=ALU.add, op1=ALU.subtract), reads=[B_rows, B_scan], writes=[B_zr])
        P.op("dve", lambda e: e.reduce_max(out=rs8[:, 0:1], in_=zrow[:, 0:R], axis=AX.X), reads=[B_zr], writes=[B_rs8])
        P.op("dve", lambda e: e.tensor_tensor(out=rs8[:, 1:2], in0=rs8[:, 0:1], in1=mu[:, 0:1], op=ALU.max), reads=[B_rs8, B_mu], writes=[B_rs8])
        P.op("dve", lambda e: e.tensor_scalar(rs8[:, 2:3], rs8[:, 1:2], -1.0, None, ALU.mult), reads=[B_rs8], writes=[B_rs8])
        P.op("dve", lambda e: e.tensor_tensor(out=rs8[:, 3:4], in0=Bprev_col, in1=rs8[:, 1:2], op=ALU.subtract), reads=[B_rs8, B_scan], writes=[B_rs8])
        P.op("act", lambda e: e.activation(out=erow[:, 0:R], in_=zrow[:, 0:R], func=AF.Exp, bias=rs8[:, 2:3], scale=1.0), reads=[B_zr, B_rs8], writes=[B_zr])
        P.op("act", lambda e: e.activation(out=trow[:, 0:R], in_=Bslice, func=AF.Exp, bias=rs8[:, 3:4], scale=-1.0), reads=[B_scan, B_rs8], writes=[B_zr])
        P.op("act", lambda e: e.activation(out=rs8[:, 4:5], in_=mu[:, 0:1], func=AF.Exp, bias=rs8[:, 2:3], scale=1.0), reads=[B_mu, B_rs8], writes=[B_rs8])
        P.op("dve", lambda e: e.scalar_tensor_tensor(out=mu[:, 0:1], in0=Bslice[:, R - 1:R], scalar=Bprev_col, in1=rs8[:, 1:2],
                                                     op0=ALU.subtract, op1=ALU.add), reads=[B_scan, B_rs8, B_mu], writes=[B_mu])
        pm0, Bm0 = pM[0], B_pM[0]
        P.op("pe", lambda e: e.transpose(pm0[0:R, 0:8], erow[:, 0:R], ident[0:8, 0:8]), reads=[B_zr, B_ident], writes=[Bm0])
        P.op("pe", lambda e: e.transpose(pm0[0:R, 8:16], trow[:, 0:R], ident[0:8, 0:8]), reads=[B_zr, B_ident], writes=[Bm0])
        P.op("act", lambda e: e.copy(et[0:R, :], pm0[0:R, 0:16]), reads=[Bm0], writes=[B_et])
        P.op("dve", lambda e: e.tensor_scalar(dg8[:, :], pairsel[:, :], rs8[:, 4:5], None, ALU.mult), reads=[B_rs8, B_sel], writes=[B_dg8])
        P.op("pe", lambda e: e.matmul(pm0[:, 16:20], lhsT=parsel[:, :], rhs=dg8[:, :], start=True, stop=True), reads=[B_dg8, B_sel], writes=[Bm0])
        P.op("act", lambda e: e.copy(decb[:, :], pm0[:, 16:20]), reads=[Bm0], writes=[B_decb])
        P.op("dve", lambda e: e.tensor_tensor(out=Chat[:], in0=Chat[:], in1=decb[:, :].unsqueeze(2).to_broadcast([128, 4, 129]), op=ALU.mult),
             reads=[B_Ch, B_decb], writes=[B_Ch])
        P.op("act", lambda e: e.copy(Csb[:], Chat[:]), reads=[B_Ch], writes=[B_Cs])
        P.op("dve", lambda e: e.tensor_tensor(out=vp[0:R], in0=vap, in1=et[0:R, 0:8].unsqueeze(2).to_broadcast([R, NH, 129]), op=ALU.mult),
             reads=[Bv, B_et], writes=[B_vp])
        for cb in range(4):
            P.op("pe", lambda e, cb=cb: e.transpose(pT[0:R, cb * 128:(cb + 1) * 128], kT[:, cb, tcol:tcol + R], identb[:, :]),
                 reads=[B_kT, B_identb], writes=[B_pT])
        P.op("act", lambda e: e.copy(ktok[0:R, :], pT[0:R, 0:512]), reads=[B_pT], writes=[B_kt])
        for par in range(2):
            pm, Bm = pM[1 + par], B_pM[1 + par]
            for hh in range(4):
                hd = hh * 2 + par
                cb, base = hd // 2, par * 64
                P.op("pe", lambda e, pm=pm, hh=hh, cb=cb, base=base: e.matmul(
                    pm[0:R, hh * 128:hh * 128 + R], lhsT=kT[base:base + 64, cb, tcol:tcol + R], rhs=qT[base:base + 64, cb, tcol:tcol + R],
                    start=True, stop=True), reads=[B_kT, B_qT], writes=[Bm])
            P.op("dve", lambda e, pm=pm, par=par: e.tensor_tensor(
                out=STb[0:R, par * 4:(par + 1) * 4, 0:R], in0=pm[0:R, :].rearrange("p (a b) -> p a b", a=4)[:, :, 0:R],
                in1=mask_ap.unsqueeze(1).to_broadcast([R, 4, R]), op=ALU.mult), reads=[Bm, B_maskT], writes=[B_ST[par]])
        for ph, hhs in enumerate([(0, 1, 2), (3,)]):
            for par in range(2):
                pm, Bm = pM[3 + par], B_pM[3 + par]
                for sl, hh in enumerate(hhs):
                    hd = hh * 2 + par
                    cb, base = hd // 2, par * 64
                    P.op("pe", lambda e, pm=pm, sl=sl, hd=hd, par=par, hh=hh: e.matmul(
                        pm[0:R, sl * 129:(sl + 1) * 129], lhsT=STb[0:R, par * 4 + hh, 0:R], rhs=vp[0:R, hd, :], start=True, stop=False),
                        reads=[B_ST[par], B_vp], writes=[Bm])
                    P.op("pe", lambda e, pm=pm, sl=sl, cb=cb, base=base, hd=hd: e.matmul(
                        pm[0:R, sl * 129:(sl + 1) * 129], lhsT=qT[base:base + 64, cb, tcol:tcol + R], rhs=Csb[base:base + 64, hd // 2, :],
                        start=False, stop=True), reads=[B_qT, B_Cs], writes=[Bm])
                for sl, hh in enumerate(hhs):
                    hd = hh * 2 + par
                    P.op("act", lambda e, pm=pm, sl=sl, hd=hd: e.copy(nd[0:R, hd, :], pm[0:R, sl * 129:(sl + 1) * 129]),
                         reads=[Bm], writes=[B_nd])
        for hp2 in range(2):
            pm, Bm = pM[1 + hp2], B_pM[1 + hp2]
            for hq in range(2):
                for par in range(2):
                    hd = (hp2 * 2 + hq) * 2 + par
                    P.op("pe", lambda e, pm=pm, hq=hq, par=par, hd=hd: e.matmul(
                        pm[par * 64:(par + 1) * 64, hq * 129:(hq + 1) * 129], lhsT=ktok[0:R, hd * 64:(hd + 1) * 64], rhs=vp[0:R, hd, :],
                        start=True, stop=True), reads=[B_kt, B_vp], writes=[Bm])
            P.op("dve", lambda e, pm=pm, hp2=hp2: e.tensor_tensor(
                out=Chat[:, hp2 * 2:hp2 * 2 + 2, :], in0=Chat[:, hp2 * 2:hp2 * 2 + 2, :],
                in1=pm[:, 0:258].rearrange("p (a b) -> p a b", a=2), op=ALU.add), reads=[Bm, B_Ch], writes=[B_Ch])
        mlstm_post(R, soap, Bso, hmT_dst, Bhm, et[0:R, 8:16], B_et)

    def mlstm_project(tiles, ntok):
        allx = B_xnT
        P.dma("sp", lambda e: e.dma_start(out=ghead_s[:], in_=g_head[0:1, :].partition_broadcast(128)), writes=[B_gh])
        P.op("pool", lambda e: e.memset(vaug[:, :, :, 128:129], 1.0), writes=B_v)
        for c in range(2):
            wv, Bw = wload(w_in_ml[:, c * 256:(c + 1) * 256], KC, 256)
            proj_fm(wv, Bw, KC, 256, xnT, allx, ntok,
                    lambda pa, Bp, cb, t0, n, c=c: P.op("act", lambda e: e.copy(qT[:, c * 2 + cb, t0:t0 + n], pa), reads=[Bp], writes=[B_qT]))
        for c in range(2):
            wv, Bw = wload(w_in_ml[:, 512 + c * 256:512 + (c + 1) * 256], KC, 256)
            proj_fm(wv, Bw, KC, 256, xnT, allx, ntok,
                    lambda pa, Bp, cb, t0, n, c=c: P.op("act", lambda e: e.mul(kT[:, c * 2 + cb, t0:t0 + n], pa, DQK ** -0.5), reads=[Bp], writes=[B_kT]))
        for c in range(4):
            wv, Bw = wload(w_in_ml[:, 1024 + c * 256:1024 + (c + 1) * 256], KC, 256)
            proj_tm(wv, Bw, KC, 256, xnT, allx, tiles,
                    lambda pa, Bp, ti, rows, c=c: P.op("act", lambda e: e.copy(
                        vaug[0:rows, ti, 2 * c:2 * c + 2, 0:128], pa.rearrange("p (a b) -> p a b", a=2)), reads=[Bp], writes=[B_v[ti]]))
        for c in range(4):
            wv, Bw = wload(w_in_ml[:, 2048 + c * 256:2048 + (c + 1) * 256], KC, 256)

            def ev(pa, Bp, ti, rows, c=c):
                P.op("act", lambda e: e.activation(out=junk[0:rows, 0:256], in_=pa, func=AF.Sigmoid), reads=[Bp], writes=[B_junk])
                P.op("dve", lambda e: e.tensor_tensor(out=so[0:rows, ti, c * 256:(c + 1) * 256], in0=junk[0:rows, 0:256],
                                                      in1=ghead_s[0:rows, c * 256:(c + 1) * 256], op=ALU.mult), reads=[B_junk, B_gh], writes=[B_so[ti]])
            proj_tm(wv, Bw, KC, 256, xnT, allx, tiles, ev)
        wv, Bw = wload(w_in_ml[:, 3072:3088], KC, 16)
        proj_tm(wv, Bw, KC, 16, xnT, allx, tiles,
                lambda pa, Bp, ti, rows: P.op("dve", lambda e: e.tensor_tensor(out=gates[0:rows, ti, :], in0=pa, in1=bg_s[0:rows, :], op=ALU.add),
                                              reads=[Bp, B_vec], writes=[B_g]))

    def mlstm_gate_rows(tiles, do_rows=True):
        nt = len(tiles)
        R = tiles[0][1]
        f = gates[0:R, 0:nt, 8:16]
        P.op("dve", lambda e: e.scalar_tensor_tensor(out=gtmp[0:R, 0:nt, :, 0], in0=f, scalar=-1.0, in1=f, op0=ALU.mult, op1=ALU.max), reads=[B_g], writes=[B_gt])
        P.op("act", lambda e: e.activation(out=gtmp[0:R, 0:nt, :, 1], in_=gtmp[0:R, 0:nt, :, 0], func=AF.Exp, scale=-1.0), reads=[B_gt], writes=[B_gt])
        P.op("act", lambda e: e.activation(out=gtmp[0:R, 0:nt, :, 2], in_=gtmp[0:R, 0:nt, :, 1], func=AF.Ln, bias=1.0, scale=1.0), reads=[B_gt], writes=[B_gt])
        P.op("dve", lambda e: e.tensor_scalar(gtmp[0:R, 0:nt, :, 3], f, 0.0, None, ALU.min), reads=[B_g, B_gt], writes=[B_gt])
        P.op("dve", lambda e: e.tensor_tensor(out=logf[0:R, 0:nt, :], in0=gtmp[0:R, 0:nt, :, 3], in1=gtmp[0:R, 0:nt, :, 2], op=ALU.subtract),
             reads=[B_gt], writes=[B_gt])
        if not do_rows:
            return
        for ti, (c0, rows) in enumerate(tiles):
            pm0, Bm0 = pM[0], B_pM[0]
            P.op("pe", lambda e, ti=ti, rows=rows: e.transpose(pm0[0:8, 0:rows], gates[0:rows, ti, 0:8], ident[0:rows, 0:rows]),
                 reads=[B_g, B_ident], writes=[Bm0])
            P.op("pe", lambda e, ti=ti, rows=rows: e.transpose(pm0[0:8, 128:128 + rows], logf[0:rows, ti, :], ident[0:rows, 0:rows]),
                 reads=[B_gt, B_ident], writes=[Bm0])
            P.op("act", lambda e, c0=c0, rows=rows: e.copy(irow[:, c0:c0 + rows], pm0[0:8, 0:rows]), reads=[Bm0], writes=[B_rows])
            P.op("act", lambda e, c0=c0, rows=rows: e.copy(frow[:, c0:c0 + rows], pm0[0:8, 128:128 + rows]), reads=[Bm0], writes=[B_rows])

    def w_out_apply(w2d, tiles, srcT, Bsrc, nkc=KC):
        for c in range(4):
            wv, Bw = wload(w2d[:, c * 256:(c + 1) * 256], nkc, 256)
            proj_tm(wv, Bw, nkc, 256, srcT, Bsrc, tiles,
                    lambda pa, Bp, ti, rows, c=c: P.op("dve", lambda e: e.tensor_tensor(
                        out=h[0:rows, ti, c * 256:(c + 1) * 256], in0=pa, in1=h[0:rows, ti, c * 256:(c + 1) * 256], op=ALU.add),
                        reads=[Bp, B_h[ti]], writes=[B_h[ti]]))

    actT = arena[:, 0:NF_ * GT].rearrange("p (f t) -> p f t", f=NF_); B_act = Buf("actT")
    sil = mscr[:, 0:512]; B_sil = Buf("sil")

    def ffn(layer, tiles, ntok):
        NF = DFF // 128
        for fc in range(NF):
            wg, Bwg = wload(w_gu[layer, :, fc * 128:(fc + 1) * 128], KC, 128)
            wu, Bwu = wload(w_gu[layer, :, DFF + fc * 128:DFF + (fc + 1) * 128], KC, 128)
            for t0 in range(0, ntok, 512):
                n = min(512, ntok - t0)
                pg, Bpg = next_pA()
                for k in range(KC):
                    P.op("pe", lambda e, k=k, pg=pg, t0=t0, n=n, wg=wg: e.matmul(pg[:, 0:n], lhsT=wg[:, k, :], rhs=xnT[:, k, t0:t0 + n],
                                                                                start=(k == 0), stop=(k == KC - 1)), reads=[Bwg] + B_xnT, writes=[Bpg])
                pu, Bpu = next_pA()
                for k in range(KC):
                    P.op("pe", lambda e, k=k, pu=pu, t0=t0, n=n, wu=wu: e.matmul(pu[:, 0:n], lhsT=wu[:, k, :], rhs=xnT[:, k, t0:t0 + n],
                                                                                start=(k == 0), stop=(k == KC - 1)), reads=[Bwu] + B_xnT, writes=[Bpu])
                P.op("act", lambda e, pg=pg, n=n: e.activation(out=sil[:, 0:n], in_=pg[:, 0:n], func=AF.Silu), reads=[Bpg], writes=[B_sil])
                P.op("dve", lambda e, pu=pu, n=n, fc=fc, t0=t0: e.tensor_tensor(out=actT[:, fc, t0:t0 + n], in0=pu[:, 0:n], in1=sil[:, 0:n], op=ALU.mult),
                     reads=[Bpu, B_sil], writes=[B_act])
        for half in range(2):
            wds = []
            for ti, (c0, rows) in enumerate(tiles):
                pass
            groups = [(k0, min(4, NF - k0)) for k0 in range(0, NF, 4)]
            accs = [(pM[i], B_pM[i]) for i in range(5)] + [(pA[0], B_pA[0]), (pA[1], B_pA[1])]
            for tb in range(0, len(tiles), len(accs)):
                tl = tiles[tb:tb + len(accs)]
                for (k0, nk) in groups:
                    wv, Bw = wload(w_dn[layer, k0 * 128:(k0 + nk) * 128, half * 512:(half + 1) * 512], nk, 512)
                    for j, (c0, rows) in enumerate(tl):
                        pa, Bp = accs[j]
                        for kk in range(nk):
                            P.op("pe", lambda e, pa=pa, kk=kk, k0=k0, c0=c0, rows=rows, wv=wv: e.matmul(
                                pa[0:rows, 0:512], lhsT=actT[:, k0 + kk, c0:c0 + rows], rhs=wv[:, kk, :],
                                start=(k0 + kk == 0), stop=(k0 + kk == NF - 1)), reads=[Bw, B_act], writes=[Bp])
                for j, (c0, rows) in enumerate(tl):
                    pa, Bp = accs[j]
                    ti = tb + j
                    P.op("dve", lambda e, pa=pa, ti=ti, rows=rows, half=half: e.tensor_tensor(
                        out=h[0:rows, ti, half * 512:(half + 1) * 512], in0=pa[0:rows, 0:512], in1=h[0:rows, ti, half * 512:(half + 1) * 512], op=ALU.add),
                        reads=[Bp, B_h[ti]], writes=[B_h[ti]])

    NKB = S // 128
    knT = sb("knT", [128, NH, S], BF16); B_kn = Buf("knT")
    krT = sb("krT", [64, S], BF16); B_kr = Buf("krT")
    Vh = sb("Vh", [128, NKB, NH, 129], BF16); B_Vh = Buf("Vh")
    P.op("pool", lambda e: e.memset(Vh[:, :, :, 128:129], 1.0), writes=[B_Vh])
    gqfm_s = sb("gqfm_s", [128, 3])
    P.dma("sp", lambda e: e.dma_start(out=gqfm_s[:], in_=gqfm[:, :]), writes=[B_gfm])
    gfin_s = ghead_s
    ckq = mscr[:, 0:NT * 704].rearrange("p (t c) -> p t c", t=NT); B_ckq = [Buf(f"ckq{t}") for t in range(NT)]
    a0 = 0
    cqnT = arena[:, a0:a0 + 3 * GT].rearrange("p (c t) -> p c t", c=3); B_cqn = [Buf(f"cqn{t}") for t in range(NT)]; a0 += 3 * GT
    ckvT = arena[:, a0:a0 + 2 * GT].rearrange("p (c t) -> p c t", c=2); B_ckvT = [Buf(f"ckvT{t}") for t in range(NT)]; a0 += 2 * GT
    qnT = arena[:, a0:a0 + NH * GT].rearrange("p (h t) -> p h t", h=NH); B_qn = Buf("qnT"); a0 += NH * GT
    qrT = arena[:, a0:a0 + NH * GT].rearrange("p (h t) -> p h t", h=NH); B_qr = Buf("qrT"); a0 += NH * GT
    PTb = arena[:, a0:a0 + 2 * NH * 128].rearrange("p (b h t) -> p b h t", b=2, h=NH); B_PT = [Buf("PT0"), Buf("PT1")]; a0 += 2 * NH * 128
    attb = arena[:, a0:a0 + D]; B_att = Buf("attb"); a0 += D
    wrot = arena[:, a0:a0 + 3 * 64].rearrange("p (c r) -> p c r", c=3); B_wrot = Buf("wrot"); a0 += 192
    assert a0 <= ASZ, (a0, ASZ)
    sarena = knT[:, :, :].rearrange("p h s -> p (h s)") if NH * S >= 9000 else sb("sarena", [128, 9000], BF16)
    arena_p, arena, a0 = arena, sarena, 0
    NKP = 4
    kp = [arena[:, a0 + i * 322:a0 + i * 322 + 321] for i in range(NKP)]; B_kp = [Buf(f"kp{i}") for i in range(NKP)]; a0 += NKP * 322
    KTp = [arena[:, a0 + i * 384:a0 + (i + 1) * 384].rearrange("p (c k) -> p c k", c=3) for i in range(2)]; B_KTp = [Buf("KTp0"), Buf("KTp1")]; a0 += 768
    PTs = [arena[:, a0 + i * 8:a0 + (i + 1) * 8] for i in range(2)]; B_PTs = [Buf("PTs0"), Buf("PTs1")]; a0 += 16
    wukT = arena[:, a0:a0 + 2048].rearrange("p (h k c) -> p h k c", h=NH, k=2); B_wukT = Buf("wukT"); a0 += 2048
    qlT = arena[:, a0:a0 + 2 * NS * NH].rearrange("p (k t h) -> p k t h", k=2, t=NS); B_qlT = Buf("qlT"); a0 += 2 * NS * NH
    cnew = arena[:, a0:a0 + 258]; B_cnew = Buf("cnew"); a0 += 258
    krTn = arena[:, a0:a0 + NS]; B_krTn = Buf("krTn"); a0 += NS
    olat = arena[:, a0:a0 + 256]; B_olat = Buf("olat"); a0 += 256
    olT = arena[:, a0:a0 + 2 * NH * NS].rearrange("p (k h t) -> p k h t", k=2, h=NH); B_olT = Buf("olT"); a0 += 2 * NH * NS
    pnew = arena[:, a0:a0 + 8]; B_pnew = Buf("pnew"); a0 += 8
    qs_b = arena[:, a0:a0 + 4 * NS].rearrange("p (c t) -> p c t", c=4); B_qs = Buf("qs"); a0 += 4 * NS
    qTm = arena[:, a0:a0 + 4 * NS * NS].rearrange("p (c a t) -> p c a t", c=4, a=NS); B_qTm = Buf("qTm"); a0 += 4 * NS * NS
    keb = arena[:, a0:a0 + 512]; B_keb = Buf("keb"); a0 += 512
    kmt = [arena[:, a0 + i * 512:a0 + (i + 1) * 512] for i in range(2)]; B_kmt = [Buf("kmt0"), Buf("kmt1")]; a0 += 1024
    c0b = [arena[:, a0 + i * 516:a0 + (i + 1) * 516].rearrange("p (a b) -> p a b", a=4) for i in range(2)]; B_c0b = [Buf("c0b0"), Buf("c0b1")]; a0 += 1032
    assert a0 <= 9000, a0
    arena = arena_p
    if USE_CACHE:
        idx_i = sb("idx_i", [128, NPG], I32); idx_f = sb("idx_f", [128, NPG]); B_idx = Buf("idx")
        iota_p = sb("iota_p", [128, 1]); B_iota = Buf("iota")
        P.op("pool", lambda e: e.iota(iota_p[:], pattern=[[0, 1]], base=0, channel_multiplier=1, allow_small_or_imprecise_dtypes=True), writes=[B_iota])
    oh16 = sb("oh16", [128, NS, NS]); B_oh = Buf("oh16")
    P.dma("sp", lambda e: e.dma_start(out=oh16[:], in_=oh16_d[:, :, :]), writes=[B_oh])
    sg = sb("sg", [NS, 12, 8]); B_sg = Buf("sg")
    d8 = sb("d8", [8, NS]); R8 = sb("R8", [8, 4, NS]); B_d8 = Buf("d8")
    decT = sb("decT", [128, 4, NS]); B_decT = Buf("decT")
    _c0 = mscr[:, 2056:2056 + 516].rearrange("p (a b) -> p a b", a=4)
    c0f = [_c0, _c0]; _B = Buf("c0f"); B_c0f = [_B, _B]

    def sample_attention(R):
        for c in range(4):
            wv, Bw = wload(w_uk[:, c * 256:(c + 1) * 256], 2, 256)
            for kc in range(2):
                for hh in range(2):
                    sl = kc * 2 + hh
                    P.op("pe", lambda e, wv=wv, kc=kc, hh=hh, sl=sl: e.transpose(pT[:, sl * 128:(sl + 1) * 128], wv[:, kc, hh * 128:(hh + 1) * 128], identb[:, :]),
                         reads=[Bw, B_identb], writes=[B_pT])
            P.op("act", lambda e, c=c: e.copy(wukT[:, 2 * c:2 * c + 2, :, :], pT[:, 0:512].rearrange("p (k h c) -> p h k c", k=2, h=2)), reads=[B_pT], writes=[B_wukT])
        pa, Bp = next_pA()
        for kc in range(2):
            for hd in range(NH):
                o = (kc * NH + hd) * R
                P.op("pe", lambda e, pa=pa, kc=kc, hd=hd, o=o: e.matmul(pa[:, o:o + R], lhsT=wukT[:, hd, kc, :], rhs=qnT[:, hd, 0:R], start=True, stop=True),
                     reads=[B_wukT, B_qn], writes=[Bp])
        P.op("act", lambda e, pa=pa: e.copy(qlT[:, :, 0:R, :].rearrange("p k t h -> p k h t"), pa[:, 0:2 * NH * R].rearrange("p (k h t) -> p k h t", k=2, h=NH)),
             reads=[Bp], writes=[B_qlT])
        P.op("pool", lambda e: e.memset(cnew[0:R, 256:257], 1.0), writes=[B_cnew])
        for i in range(NKP):
            P.op("pool", lambda e, i=i: e.memset(kp[i][:, 256:257], 1.0), writes=[B_kp[i]])
        kctr = 0
        for t in range(R):
            if USE_CACHE:
                P.dma("sp", lambda e, t=t: e.dma_start(out=idx_i[:], in_=ptb[0:1, t * NPG:(t + 1) * NPG].partition_broadcast(128)), writes=[B_idx])
                P.op("pool", lambda e: e.tensor_copy(idx_f[:], idx_i[:]), reads=[B_idx], writes=[B_idx])
                P.op("pool", lambda e: e.tensor_scalar(idx_f[:], idx_f[:], 128.0, iota_p[:, 0:1], ALU.mult, ALU.add), reads=[B_idx, B_iota], writes=[B_idx])
                P.op("pool", lambda e: e.tensor_copy(idx_i[:], idx_f[:]), reads=[B_idx], writes=[B_idx])
            acc, Bacc = pM[0], B_pM[0]
            npg = NPG if USE_CACHE else 0
            for j in range(npg):
                ki = kctr % NKP; k2 = kctr % 2; kctr += 1
                P.dma("pool", lambda e, ki=ki, j=j: e.indirect_dma_start(out=kp[ki][:, 0:256], out_offset=None, in_=lat[:, :],
                                                                    in_offset=bass.IndirectOffsetOnAxis(ap=idx_i[:, j:j + 1], axis=0)), reads=[B_idx], writes=[B_kp[ki]])
                P.dma("pool", lambda e, ki=ki, j=j: e.indirect_dma_start(out=kp[ki][:, 257:321], out_offset=None, in_=kr[:, :],
                                                                    in_offset=bass.IndirectOffsetOnAxis(ap=idx_i[:, j:j + 1], axis=0)), reads=[B_idx], writes=[B_kp[ki]])
                for k in range(2):
                    P.op("pe", lambda e, ki=ki, k=k: e.transpose(pT[:, k * 128:(k + 1) * 128], kp[ki][:, k * 128:(k + 1) * 128], identb[:, :]),
                         reads=[B_kp[ki], B_identb], writes=[B_pT])
                P.op("pe", lambda e, ki=ki: e.transpose(pT[0:64, 256:384], kp[ki][:, 257:321], identb[:, :]), reads=[B_kp[ki], B_identb], writes=[B_pT])
                P.op("act", lambda e, k2=k2: e.copy(KTp[k2][:, 0:2, :], pT[:, 0:256].rearrange("p (c k) -> p c k", c=2)), reads=[B_pT], writes=[B_KTp[k2]])
                P.op("dve", lambda e, k2=k2: e.tensor_copy(KTp[k2][0:64, 2, :], pT[0:64, 256:384]), reads=[B_pT], writes=[B_KTp[k2]])
                ps_, Bps = next_pA()
                for k in range(2):
                    P.op("pe", lambda e, ps_=ps_, k=k, k2=k2, t=t: e.matmul(ps_[:, 0:8], lhsT=KTp[k2][:, k, :], rhs=qlT[:, k, t, :], start=(k == 0), stop=False),
                         reads=[B_KTp[k2], B_qlT], writes=[Bps])
                P.op("pe", lambda e, ps_=ps_, k2=k2, t=t: e.matmul(ps_[:, 0:8], lhsT=KTp[k2][0:64, 2, :], rhs=qrT[0:64, :, t], start=False, stop=True),
                     reads=[B_KTp[k2], B_qr], writes=[Bps])
                P.op("act", lambda e, ps_=ps_, k2=k2: e.activation(out=PTs[k2][:, :], in_=ps_[:, 0:8], func=AF.Exp, scale=MLA_SCALE), reads=[Bps], writes=[B_PTs[k2]])
                P.op("pe", lambda e, k2=k2, ki=ki, j=j: e.matmul(acc[0:8, 0:257], lhsT=PTs[k2][:, :], rhs=kp[ki][:, 0:257], start=(j == 0), stop=False),
                     reads=[B_PTs[k2], B_kp[ki]], writes=[Bacc])
            pn, Bpn = pM[1], B_pM[1]
            for k in range(2):
                P.op("pe", lambda e, k=k, t=t: e.matmul(pn[0:R, 0:8], lhsT=ckvT[:, k, 0:R], rhs=qlT[:, k, t, :], start=(k == 0), stop=False),
                     reads=B_ckvT + [B_qlT], writes=[Bpn])
            P.op("pe", lambda e, t=t: e.matmul(pn[0:R, 0:8], lhsT=krTn[0:64, 0:R], rhs=qrT[0:64, :, t], start=False, stop=True), reads=[B_krTn, B_qr], writes=[Bpn])
            P.op("act", lambda e: e.activation(out=sg[0:R, 9, :], in_=pn[0:R, 0:8], func=AF.Exp, scale=MLA_SCALE), reads=[Bpn], writes=[B_sg])
            P.op("dve", lambda e, t=t: e.tensor_scalar(pnew[0:R, :], sg[0:R, 9, :], ident[0:R, t:t + 1], None, ALU.mult), reads=[B_sg, B_ident], writes=[B_pnew])
            P.op("pe", lambda e, npg=npg: e.matmul(acc[0:8, 0:257], lhsT=pnew[0:R, :], rhs=cnew[0:R, 0:257], start=(npg == 0), stop=True), reads=[B_pnew, B_cnew], writes=[Bacc])
            P.op("dve", lambda e: e.reciprocal(orc[0:8, 0:1], acc[0:8, 256:257]), reads=[Bacc], writes=[B_orc])
            P.op("dve", lambda e: e.tensor_scalar(olat[0:8, :], acc[0:8, 0:256], orc[0:8, 0:1], None, ALU.mult), reads=[Bacc, B_orc], writes=[B_olat])
            for k in range(2):
                P.op("pe", lambda e, k=k: e.transpose(pT[:, 512 + k * 8:512 + (k + 1) * 8], olat[0:8, k * 128:(k + 1) * 128], identb[0:8, 0:8]), reads=[B_olat, B_identb], writes=[B_pT])
            P.op("act", lambda e, t=t: e.copy(olT[:, :, :, t], pT[:, 512:528].rearrange("p (k h) -> p k h", k=2)), reads=[B_pT], writes=[B_olT])
        for c in range(2):
            wv, Bw = wload(w_uv[:, c * 512:(c + 1) * 512], 2, 512)
            pa, Bp = next_pA()
            for hh in range(4):
                hd = c * 4 + hh
                for k in range(2):
                    P.op("pe", lambda e, pa=pa, wv=wv, hh=hh, hd=hd, k=k: e.matmul(pa[:, hh * R:(hh + 1) * R], lhsT=wv[:, k, hh * 128:(hh + 1) * 128], rhs=olT[:, k, hd, 0:R],
                                                                            start=(k == 0), stop=(k == 1)), reads=[Bw, B_olT], writes=[Bp])
            P.op("act", lambda e, pa=pa, c=c: e.copy(hmT[:, 4 * c:4 * c + 4, 0:R], pa[:, 0:4 * R].rearrange("p (h t) -> p h t", h=4)), reads=[Bp], writes=[B_hm[0]])

    def sample_group():
        R = NS
        tiles = [(0, R)]
        P.dma("sp", lambda e: e.dma_start(out=h[0:R, 0, :], in_=xs[:, :]), writes=[B_h[0]])
        rmsnorm_T(h[0:R, 0, :], B_h[0], R, 0, xnT[:, :, 0:R], B_xnT[0])
        mlstm_project(tiles, R)
        mlstm_gate_rows(tiles, do_rows=False)
        ig = gates[0:R, 0, 0:8]; lf = logf[0:R, 0, :]
        P.dma("sp", lambda e: e.dma_start(out=sg[0:R, 0, :], in_=stm[:, :]), writes=[B_sg])
        P.op("dve", lambda e: e.tensor_tensor(out=sg[0:R, 1, :], in0=ig, in1=lf, op=ALU.subtract), reads=[B_g, B_gt, B_sg], writes=[B_sg])
        P.op("dve", lambda e: e.tensor_tensor(out=sg[0:R, 2, :], in0=sg[0:R, 0, :], in1=sg[0:R, 1, :], op=ALU.max), reads=[B_sg], writes=[B_sg])
        P.op("dve", lambda e: e.tensor_tensor(out=sg[0:R, 3, :], in0=lf, in1=sg[0:R, 2, :], op=ALU.add), reads=[B_gt, B_sg], writes=[B_sg])
        P.op("dve", lambda e: e.tensor_tensor(out=sg[0:R, 4, :], in0=sg[0:R, 0, :], in1=sg[0:R, 2, :], op=ALU.subtract), reads=[B_sg], writes=[B_sg])
        P.op("act", lambda e: e.activation(out=sg[0:R, 4, :], in_=sg[0:R, 4, :], func=AF.Exp), reads=[B_sg], writes=[B_sg])
        P.op("dve", lambda e: e.tensor_tensor(out=sg[0:R, 5, :], in0=sg[0:R, 1, :], in1=sg[0:R, 2, :], op=ALU.subtract), reads=[B_sg], writes=[B_sg])
        P.op("act", lambda e: e.activation(out=sg[0:R, 5, :], in_=sg[0:R, 5, :], func=AF.Exp), reads=[B_sg], writes=[B_sg])
        P.op("act", lambda e: e.activation(out=sg[0:R, 6, :], in_=sg[0:R, 3, :], func=AF.Exp, scale=-1.0), reads=[B_sg], writes=[B_sg])
        P.dma("pool", lambda e: e.dma_start(out=ms_[:, :], in_=sg[0:R, 3, :]), reads=[B_sg], writes=[B_out])
        for cb in range(4):
            P.op("pe", lambda e, cb=cb: e.transpose(pT[0:R, cb * 128:(cb + 1) * 128], qT[:, cb, 0:R], identb[:, :]), reads=[B_qT, B_identb], writes=[B_pT])
        P.op("act", lambda e: e.copy(xnb[0:R, 0:512], pT[0:R, 0:512]), reads=[B_pT], writes=[B_xnb])
        for cb in range(4):
            P.op("pe", lambda e, cb=cb: e.transpose(pT[0:R, cb * 128:(cb + 1) * 128], kT[:, cb, 0:R], identb[:, :]), reads=[B_kT, B_identb], writes=[B_pT])
        P.op("act", lambda e: e.copy(ktok[0:R, :], pT[0:R, 0:512]), reads=[B_pT], writes=[B_kt])
        sq64 = sqs[0:R, :, 0:64]
        P.op("dve", lambda e: e.tensor_tensor(out=sq64, in0=xnb[0:R, 0:512].rearrange("p (h d) -> p h d", h=NH), in1=ktok[0:R, :].rearrange("p (h d) -> p h d", h=NH), op=ALU.mult),
             reads=[B_xnb, B_kt], writes=[B_sq])
        P.op("dve", lambda e: e.tensor_reduce(out=sg[0:R, 7, :], in_=sq64, axis=AX.X, op=ALU.add), reads=[B_sq, B_sg], writes=[B_sg])
        P.op("dve", lambda e: e.tensor_tensor(out=sg[0:R, 8, :], in0=sg[0:R, 7, :], in1=sg[0:R, 5, :], op=ALU.mult), reads=[B_sg], writes=[B_sg])
        P.op("dve", lambda e: e.tensor_tensor(out=nd[0:R], in0=vaug[0:R, 0], in1=sg[0:R, 8, :].unsqueeze(2).to_broadcast([R, NH, 129]), op=ALU.mult),
             reads=[B_v[0], B_sg], writes=[B_nd])
        P.op("dve", lambda e: e.tensor_tensor(out=keb[0:R, :].rearrange("p (h d) -> p h d", h=NH), in0=ktok[0:R, :].rearrange("p (h d) -> p h d", h=NH),
                                              in1=sg[0:R, 5, :].unsqueeze(2).to_broadcast([R, NH, 64]), op=ALU.mult), reads=[B_kt, B_sg], writes=[B_keb])
        P.op("pe", lambda e: e.transpose(pM[4][0:8, 0:R], sg[0:R, 4, :], ident[0:R, 0:R]), reads=[B_sg, B_ident], writes=[B_pM[4]])
        P.op("act", lambda e: e.copy(d8[:, 0:R], pM[4][0:8, 0:R]), reads=[B_pM[4]], writes=[B_d8])
        P.op("dve", lambda e: e.tensor_tensor(out=R8[:, :, 0:R], in0=d8[:, 0:R].unsqueeze(1).to_broadcast([8, 4, R]), in1=pairsel[:, :].unsqueeze(2).to_broadcast([8, 4, R]), op=ALU.mult),
             reads=[B_d8, B_sel], writes=[B_d8])
        P.op("pe", lambda e: e.matmul(pM[4][:, 64:64 + 4 * R], lhsT=parsel[:, :], rhs=R8[:, :, 0:R].rearrange("p c t -> p (c t)"), start=True, stop=True), reads=[B_d8, B_sel], writes=[B_pM[4]])
        P.op("act", lambda e: e.copy(decT[:, :, 0:R], pM[4][:, 64:64 + 4 * R].rearrange("p (c t) -> p c t", c=4)), reads=[B_pM[4]], writes=[B_decT])
        P.op("dve", lambda e: e.tensor_tensor(out=qs_b[:, :, 0:R], in0=qT[:, :, 0:R], in1=decT[:, :, 0:R], op=ALU.mult), reads=[B_qT, B_decT], writes=[B_qs])
        P.op("dve", lambda e: e.tensor_tensor(out=qTm[:, :, 0:R, 0:R], in0=qs_b[:, :, 0:R].unsqueeze(2).to_broadcast([128, 4, R, R]),
                                              in1=oh16[:, 0:R, 0:R].unsqueeze(1).to_broadcast([128, 4, R, R]), op=ALU.mult), reads=[B_qs, B_oh], writes=[B_qTm])

        def ibank(par, hp):
            bi = par * 2 + (1 if hp == 3 else 0)
            return (pM[bi], B_pM[bi], 0 if hp == 3 else hp, bi)
        started = set()
        for t in range(R):
            ci = t % 2
            stC_v = stC[t].rearrange("(hp par) d e -> par d hp e", par=2)
            stn_v = stn[t].rearrange("(hp par) (d o) -> par d hp o", par=2, o=1)
            for par in range(2):
                P.dma("sp", lambda e, ci=ci, par=par, stC_v=stC_v: e.dma_start(out=c0f[ci][par * 64:(par + 1) * 64, :, 0:128], in_=stC_v[par]), writes=[B_c0f[ci]])
                P.dma("sp", lambda e, ci=ci, par=par, stn_v=stn_v: e.dma_start(out=c0f[ci][par * 64:(par + 1) * 64, :, 128:129], in_=stn_v[par], allow_slow_non_contiguous=True), writes=[B_c0f[ci]])
            P.op("pool", lambda e, ci=ci: e.tensor_copy(c0b[ci], c0f[ci][:]), reads=[B_c0f[ci]], writes=[B_c0b[ci]])
            for hd in range(NH):
                par, hp = hd % 2, hd // 2
                bank, Bb, sl, bi = ibank(par, hp)
                first = bi not in started
                started.add(bi)
                last = (t == R - 1) and (hp == 3 or hp == 2)
                P.op("pe", lambda e, bank=bank, sl=sl, par=par, hp=hp, t=t, ci=ci, first=first, last=last: e.matmul(
                    bank[0:R, sl * 129:(sl + 1) * 129], lhsT=qTm[par * 64:(par + 1) * 64, hp, t, 0:R], rhs=c0b[ci][par * 64:(par + 1) * 64, hp, :],
                    start=first, stop=last, skip_group_check=True), reads=[B_qTm, B_c0b[ci]], writes=[Bb])
            km = kmt[t % 2]; Bkm = B_kmt[t % 2]
            P.op("dve", lambda e, km=km, t=t: e.tensor_scalar(km[0:R, :], keb[0:R, :], ident[0:R, t:t + 1], None, ALU.mult), reads=[B_keb, B_ident], writes=[Bkm])
            pk0, Bk0 = pM[4], B_pM[4]
            pk1, Bk1 = next_pA()
            for hd in range(NH):
                par, hp = hd % 2, hd // 2
                pk, sl = (pk1, 0) if hp == 3 else (pk0, hp)
                Bk = Bk1 if hp == 3 else Bk0
                P.op("pe", lambda e, pk=pk, sl=sl, par=par, hd=hd, km=km: e.matmul(
                    pk[par * 64:(par + 1) * 64, sl * 129:(sl + 1) * 129], lhsT=km[0:R, hd * 64:(hd + 1) * 64], rhs=vaug[0:R, 0, hd, :], start=True, stop=True),
                    reads=[Bkm, B_v[0]], writes=[Bk])
            P.op("dve", lambda e, ci=ci, t=t: e.tensor_tensor(out=c0f[ci][:], in0=c0f[ci][:], in1=decT[:, :, t].unsqueeze(2).to_broadcast([128, 4, 129]), op=ALU.mult),
                 reads=[B_c0f[ci], B_decT, B_c0b[ci]], writes=[B_c0f[ci]])
            P.op("dve", lambda e, ci=ci, pk0=pk0: e.tensor_tensor(out=c0f[ci][:, 0:3, :], in0=c0f[ci][:, 0:3, :], in1=pk0[:, 0:387].rearrange("p (a b) -> p a b", a=3), op=ALU.add),
                 reads=[B_c0f[ci], Bk0], writes=[B_c0f[ci]])
            P.op("dve", lambda e, ci=ci, pk1=pk1: e.tensor_tensor(out=c0f[ci][:, 3, :], in0=c0f[ci][:, 3, :], in1=pk1[:, 0:129], op=ALU.add),
                 reads=[B_c0f[ci], Bk1], writes=[B_c0f[ci]])
            Cs_v = Cs[t].rearrange("(hp par) d e -> par d hp e", par=2)
            ns_v = ns_[t].rearrange("(hp par) (d o) -> par d hp o", par=2, o=1)
            for par in range(2):
                P.dma("pool", lambda e, ci=ci, par=par, Cs_v=Cs_v: e.dma_start(out=Cs_v[par], in_=c0f[ci][par * 64:(par + 1) * 64, :, 0:128]), reads=[B_c0f[ci]], writes=[B_out])
                P.dma("pool", lambda e, ci=ci, par=par, ns_v=ns_v: e.dma_start(out=ns_v[par], in_=c0f[ci][par * 64:(par + 1) * 64, :, 128:129], allow_slow_non_contiguous=True), reads=[B_c0f[ci]], writes=[B_out])
        for par in range(2):
            for (hps, bq) in [((0, 1, 2), 0), ((3,), 1)]:
                bank, Bb = pM[par * 2 + bq], B_pM[par * 2 + bq]
                for sl, hp in enumerate(hps):
                    hd = hp * 2 + par
                    P.op("dve", lambda e, bank=bank, sl=sl, hd=hd: e.tensor_tensor(out=nd[0:R, hd, :], in0=nd[0:R, hd, :], in1=bank[0:R, sl * 129:(sl + 1) * 129], op=ALU.add),
                         reads=[Bb, B_nd], writes=[B_nd])
        mlstm_post(R, so[0:R, 0, :], B_so[0], hmT[:, :, 0:R], B_hm[0], sg[0:R, 6, :], B_sg)
        w_out_apply(w_out_ml, tiles, hmT, B_hm)
        if stage >= 2:
            rmsnorm_T(h[0:R, 0, :], B_h[0], R, 1, xnT[:, :, 0:R], B_xnT[0])
            ffn(0, tiles, R)
        if stage >= 3:
            rmsnorm_T(h[0:R, 0, :], B_h[0], R, 2, xnT[:, :, 0:R], B_xnT[0])
            mla_mix(0, 0, tiles, sample=True)
        if stage >= 4:
            rmsnorm_T(h[0:R, 0, :], B_h[0], R, 3, xnT[:, :, 0:R], B_xnT[0])
            ffn(1, tiles, R)
            load_gfin()
            final_norm(h[0:R, 0, :], B_h[0], R)
        P.dma("pool", lambda e: e.dma_start(out=ys[:, :], in_=h[0:R, 0, :]), reads=[B_h[0]], writes=[B_out])

    assert 2 * GT <= 1024
    csf = mscr[0:64, 1024:1024 + 2 * GT].rearrange("p (a b) -> p a b", a=2); B_csf = Buf("csf")
    cst = sb("cst", [128, NT, 64]); B_cst = Buf("cst")
    lko = sb("lko", [128, 320]); B_lko = Buf("lko")
    rp4 = sb("rp4", [128, 4, 32]); B_rp4 = Buf("rp4")
    t64 = mscr[0:64, 0:1024].rearrange("p (a b) -> p a b", a=2); B_t64 = Buf("t64")
    orc = sb("orc", [128, 8]); B_orc = Buf("orc")

    def mla_mix(seq, g, tiles, sample=False):
        pos0 = 0 if sample else g * GT
        tok0 = 0 if sample else seq * S + pos0
        ntok = sum(r for _, r in tiles)
        lat_o, kr_o = (lats, krs) if sample else (latp, krp)
        if sample:
            P.dma("sp", lambda e: e.dma_start(out=cst[0:ntok, 0, :], in_=cs_tm[S:S + 1, :].partition_broadcast(ntok)), writes=[B_cst])
        else:
            P.dma("sp", lambda e: e.dma_start(out=cst[:], in_=cs_tm[pos0:pos0 + GT, :].rearrange("(t p) c -> p t c", p=128)), writes=[B_cst])
        for (c0, nc_) in [(0, 256), (256, 128), (384, 256), (640, 64)]:
            wv, Bw = wload(w_in_mla[:, c0:c0 + nc_], KC, nc_)
            proj_tm(wv, Bw, KC, nc_, xnT, B_xnT, tiles,
                    lambda pa, Bp, ti, rows, c0=c0, nc_=nc_: P.op("act", lambda e: e.copy(ckq[0:rows, ti, c0:c0 + nc_], pa), reads=[Bp], writes=[B_ckq[ti]] + ([B_c0f[0]] if sample else [])))
        for ti, (tc, rows) in enumerate(tiles):
            kb = (pos0 + tc) // 128
            rmsnorm_T(ckq[0:rows, ti, 0:QL], B_ckq[ti], rows, None, cqnT[:, :, tc:tc + rows], B_cqn[ti], width=QL, gap=gqfm_s[:, 0:3])
            ckv = ckq[0:rows, ti, QL:QL + KVL]
            P.op("dve", lambda e, ckv=ckv, rows=rows: e.scalar_tensor_tensor(out=junk[0:rows, 0:KVL], in0=ckv, scalar=1.0, in1=ckv, op0=ALU.mult, op1=ALU.mult,
                                                                     accum_out=st4[0:rows, 4:5]), reads=[B_ckq[ti]], writes=[B_junk, B_st4])
            P.op("dve", lambda e, rows=rows: e.tensor_scalar(st4[0:rows, 5:6], st4[0:rows, 4:5], 1.0 / KVL, EPS, ALU.mult, ALU.add), reads=[B_st4], writes=[B_st4])
            P.op("act", lambda e, rows=rows: e.activation(out=st4[0:rows, 6:7], in_=st4[0:rows, 5:6], func=AF.Ln), reads=[B_st4], writes=[B_st4])
            P.op("act", lambda e, rows=rows: e.activation(out=st4[0:rows, 7:8], in_=st4[0:rows, 6:7], func=AF.Exp, scale=-0.5), reads=[B_st4], writes=[B_st4])
            P.op("dve", lambda e, ckv=ckv, rows=rows: e.scalar_tensor_tensor(out=lko[0:rows, 0:KVL], in0=ckv, scalar=st4[0:rows, 7:8], in1=gkv_s[0:rows, :],
                                                                     op0=ALU.mult, op1=ALU.mult), reads=[B_ckq[ti], B_st4, B_vec], writes=[B_lko])
            x1 = ckq[0:rows, ti, 640:672]; x2 = ckq[0:rows, ti, 672:704]
            cos = cst[0:rows, ti, 0:32]; sin = cst[0:rows, ti, 32:64]
            P.op("dve", lambda e, x1=x1, cos=cos, rows=rows: e.tensor_tensor(out=rp4[0:rows, 0, :], in0=x1, in1=cos, op=ALU.mult), reads=[B_ckq[ti], B_cst], writes=[B_rp4])
            P.op("dve", lambda e, x2=x2, sin=sin, rows=rows: e.tensor_tensor(out=rp4[0:rows, 1, :], in0=x2, in1=sin, op=ALU.mult), reads=[B_ckq[ti], B_cst], writes=[B_rp4])
            P.op("dve", lambda e, x1=x1, sin=sin, rows=rows: e.tensor_tensor(out=rp4[0:rows, 2, :], in0=x1, in1=sin, op=ALU.mult), reads=[B_ckq[ti], B_cst], writes=[B_rp4])
            P.op("dve", lambda e, x2=x2, cos=cos, rows=rows: e.tensor_tensor(out=rp4[0:rows, 3, :], in0=x2, in1=cos, op=ALU.mult), reads=[B_ckq[ti], B_cst], writes=[B_rp4])
            P.op("dve", lambda e, rows=rows: e.tensor_tensor(out=lko[0:rows, 256:288], in0=rp4[0:rows, 0, :], in1=rp4[0:rows, 1, :], op=ALU.subtract), reads=[B_rp4, B_lko], writes=[B_lko])
            P.op("dve", lambda e, rows=rows: e.tensor_tensor(out=lko[0:rows, 288:320], in0=rp4[0:rows, 2, :], in1=rp4[0:rows, 3, :], op=ALU.add), reads=[B_rp4, B_lko], writes=[B_lko])
            P.dma("pool", lambda e, tc=tc, rows=rows: e.dma_start(out=lat_o[tok0 + tc:tok0 + tc + rows, :], in_=lko[0:rows, 0:KVL]), reads=[B_lko], writes=[B_out])
            P.dma("pool", lambda e, tc=tc, rows=rows: e.dma_start(out=kr_o[tok0 + tc:tok0 + tc + rows, :], in_=lko[0:rows, 256:320]), reads=[B_lko], writes=[B_out])
            P.op("act", lambda e, rows=rows: e.copy(xnb[0:rows, 0:KVL], lko[0:rows, 0:KVL]), reads=[B_lko], writes=[B_xnb])
            for k in range(2):
                P.op("pe", lambda e, k=k, rows=rows: e.transpose(pT[:, k * 128:k * 128 + rows], xnb[0:rows, k * 128:(k + 1) * 128], identb[0:rows, 0:rows]),
                     reads=[B_xnb, B_identb], writes=[B_pT])
            P.op("act", lambda e, tc=tc, rows=rows: e.copy(ckvT[:, :, tc:tc + rows], pT[:, 0:256].rearrange("p (k t) -> p k t", k=2)[:, :, 0:rows]), reads=[B_pT], writes=[B_ckvT[ti]])
            P.op("pe", lambda e, rows=rows: e.transpose(pM[4][0:64, 0:rows], lko[0:rows, 256:320], ident[0:rows, 0:rows]), reads=[B_lko, B_ident], writes=[B_pM[4]])
            if sample:
                P.op("act", lambda e, rows=rows: e.copy(krTn[0:64, 0:rows], pM[4][0:64, 0:rows]), reads=[B_pM[4]], writes=[B_krTn])
                P.op("act", lambda e, rows=rows: e.copy(cnew[0:rows, 0:KVL], lko[0:rows, 0:KVL]), reads=[B_lko], writes=[B_cnew])
            else:
                P.op("act", lambda e, tc=tc, rows=rows: e.copy(krT[:, pos0 + tc:pos0 + tc + rows], pM[4][0:64, 0:rows]), reads=[B_pM[4]], writes=[B_kr])
        if sample:
            for t_ in range(ntok):
                P.dma("sp", lambda e, t_=t_: e.dma_start(out=csf[:, :, t_:t_ + 1], in_=cs_fm[0:64, :, S:S + 1], allow_slow_non_contiguous=True), writes=[B_csf] + B_ckq)
        else:
            P.dma("sp", lambda e: e.dma_start(out=csf[:], in_=cs_fm[0:64, :, pos0:pos0 + GT]), writes=[B_csf] + B_ckq)
        for c in range(0 if sample else 4):
            wv, Bw = wload(w_uk[:, c * 256:(c + 1) * 256], 2, 256)
            proj_fm(wv, Bw, 2, 256, ckvT, B_ckvT, ntok,
                    lambda pa, Bp, cb, t0, n, c=c: P.op("act", lambda e: e.copy(knT[:, 2 * c + cb, pos0 + t0:pos0 + t0 + n], pa), reads=[Bp], writes=[B_kn]))
        for c in range(0 if sample else 2):
            wv, Bw = wload(w_uv[:, c * 512:(c + 1) * 512], 2, 512)
            proj_tm(wv, Bw, 2, 512, ckvT, B_ckvT, tiles,
                    lambda pa, Bp, ti, rows, c=c: P.op("act", lambda e: e.copy(
                        Vh[0:rows, (pos0 + tiles[ti][0]) // 128, 4 * c:4 * c + 4, 0:128], pa.rearrange("p (a b) -> p a b", a=4)), reads=[Bp], writes=[B_Vh]))
        for hd in range(NH):
            wv, Bw = wload(w_uq[:, hd * 192:hd * 192 + 128], 3, 128)
            proj_fm(wv, Bw, 3, 128, cqnT, B_cqn, ntok,
                    lambda pa, Bp, cb, t0, n, hd=hd: P.op("act", lambda e: e.copy(qnT[:, hd, t0:t0 + n], pa), reads=[Bp], writes=[B_qn]))
        for hd in range(NH):
            wv, Bw = wload(w_uq[:, hd * 192 + 128:hd * 192 + 192], 3, 64)
            P.op("pool", lambda e, wv=wv: e.tensor_scalar(wrot[:, :, 0:32], wv[:, :, 32:64], -1.0, None, ALU.mult), reads=[Bw], writes=[B_wrot])
            P.op("pool", lambda e, wv=wv: e.tensor_copy(wrot[:, :, 32:64], wv[:, :, 0:32]), reads=[Bw], writes=[B_wrot])
            for t0 in range(0, ntok, 512):
                n = min(512, ntok - t0)
                px, Bpx = next_pA()
                for k in range(3):
                    P.op("pe", lambda e, k=k, px=px, wv=wv, t0=t0, n=n: e.matmul(px[0:64, 0:n], lhsT=wv[:, k, :], rhs=cqnT[:, k, t0:t0 + n], start=(k == 0), stop=(k == 2)),
                         reads=[Bw] + B_cqn, writes=[Bpx])
                pr, Bpr = next_pA()
                for k in range(3):
                    P.op("pe", lambda e, k=k, pr=pr, t0=t0, n=n: e.matmul(pr[0:64, 0:n], lhsT=wrot[:, k, :], rhs=cqnT[:, k, t0:t0 + n], start=(k == 0), stop=(k == 2)),
                         reads=[B_wrot] + B_cqn, writes=[Bpr])
                P.op("dve", lambda e, px=px, t0=t0, n=n: e.tensor_tensor(out=t64[:, 0, 0:n], in0=px[0:64, 0:n], in1=csf[:, 0, t0:t0 + n], op=ALU.mult), reads=[Bpx, B_csf], writes=[B_t64] + B_ckq)
                P.op("dve", lambda e, pr=pr, t0=t0, n=n: e.tensor_tensor(out=t64[:, 1, 0:n], in0=pr[0:64, 0:n], in1=csf[:, 1, t0:t0 + n], op=ALU.mult), reads=[Bpr, B_csf, B_t64], writes=[B_t64] + B_ckq)
                P.op("dve", lambda e, hd=hd, t0=t0, n=n: e.tensor_tensor(out=qrT[0:64, hd, t0:t0 + n], in0=t64[:, 0, 0:n], in1=t64[:, 1, 0:n], op=ALU.add), reads=[B_t64, B_csf] + B_ckq, writes=[B_qr])
        if sample:
            sample_attention(ntok)
            w_out_apply(w_o, tiles, hmT, B_hm)
            return
        accb = [(pM[0], B_pM[0], (0, 1, 2)), (pM[1], B_pM[1], (3, 4, 5)), (pM[2], B_pM[2], (6, 7))]
        pctr = 0
        for ti, (tc, rows) in enumerate(tiles):
            qi = (pos0 + tc) // 128
            for j in range(qi + 1):
                pb = pctr % 2; pctr += 1
                for half in range(2):
                    pa, Bp = next_pA()
                    for hh in range(4):
                        hd = half * 4 + hh
                        P.op("pe", lambda e, pa=pa, hh=hh, hd=hd, j=j, tc=tc: e.matmul(
                            pa[:, hh * 128:(hh + 1) * 128], lhsT=knT[:, hd, j * 128:(j + 1) * 128], rhs=qnT[:, hd, tc:tc + 128], start=True, stop=False),
                            reads=[B_kn, B_qn], writes=[Bp])
                        P.op("pe", lambda e, pa=pa, hh=hh, hd=hd, j=j, tc=tc: e.matmul(
                            pa[:, hh * 128:(hh + 1) * 128], lhsT=krT[0:64, j * 128:(j + 1) * 128], rhs=qrT[0:64, hd, tc:tc + 128], start=False, stop=True),
                            reads=[B_kr, B_qr], writes=[Bp])
                    P.op("act", lambda e, pa=pa, pb=pb, half=half: e.activation(
                        out=PTb[:, pb, half * 4:(half + 1) * 4, :], in_=pa[:, :].rearrange("p (a b) -> p a b", a=4), func=AF.Exp, scale=MLA_SCALE),
                        reads=[Bp], writes=[B_PT[pb]])
                if j == qi:
                    P.op("pool", lambda e, pb=pb: e.tensor_tensor(out=PTb[:, pb], in0=PTb[:, pb], in1=maskTb[:, :].unsqueeze(1).to_broadcast([128, NH, 128]), op=ALU.mult),
                         reads=[B_PT[pb], B_maskT], writes=[B_PT[pb]])
                for (pm, Bm, hds) in accb:
                    for si, hd in enumerate(hds):
                        P.op("pe", lambda e, pm=pm, si=si, hd=hd, pb=pb, j=j, first=(j == 0 and si == 0), last=(j == qi and si == len(hds) - 1): e.matmul(
                            pm[:, si * 129:(si + 1) * 129], lhsT=PTb[:, pb, hd, :], rhs=Vh[:, j, hd, :], start=first, stop=last, skip_group_check=True),
                            reads=[B_PT[pb], B_Vh], writes=[Bm])
            for (pm, Bm, hds) in accb:
                nh_ = len(hds); h0 = hds[0]
                pv = pm[:, 0:nh_ * 129].rearrange("p (a b) -> p a b", a=nh_)
                P.op("dve", lambda e, pv=pv, h0=h0, nh_=nh_: e.reciprocal(orc[:, h0:h0 + nh_], pv[:, :, 128]), reads=[Bm], writes=[B_orc])
                P.op("dve", lambda e, pv=pv, h0=h0, nh_=nh_: e.tensor_tensor(
                    out=attb[:, h0 * 128:(h0 + nh_) * 128].rearrange("p (a b) -> p a b", a=nh_), in0=pv[:, :, 0:128],
                    in1=orc[:, h0:h0 + nh_].unsqueeze(2).to_broadcast([128, nh_, 128]), op=ALU.mult), reads=[Bm, B_orc], writes=[B_att])
            for k in range(KC):
                P.op("pe", lambda e, k=k: e.transpose(pT[:, k * 128:(k + 1) * 128], attb[:, k * 128:(k + 1) * 128], identb[:, :]), reads=[B_att, B_identb], writes=[B_pT])
            P.op("act", lambda e, tc=tc: e.copy(hmT[:, :, tc:tc + 128], pT[:, :].rearrange("p (k t) -> p k t", k=KC)), reads=[B_pT], writes=[B_hm[ti]])
        w_out_apply(w_o, tiles, hmT, B_hm)

    B_out = Buf("out")
    if cfg.get("do_sample", True):
        sample_group()
    for seq in range(NSP if cfg.get("do_prompt", True) else 0):
        for g in range(NG):
            tok0 = seq * S + g * GT
            tiles = [(t * 128, 128) for t in range(NT)]
            for t in range(NT):
                P.dma("sp", lambda e, t=t, tok0=tok0: e.dma_start(out=h[:, t, :], in_=xp[tok0 + t * 128: tok0 + (t + 1) * 128, :]), writes=[B_h[t]])
            upto = cfg.get("upto", 99)
            for t in range(NT):
                if upto >= 1:
                    rmsnorm_T(h[:, t, :], B_h[t], 128, 0, xnT[:, :, t * 128:(t + 1) * 128], B_xnT[t])
            if upto >= 2:
                mlstm_project(tiles, GT)
            if upto >= 3:
                mlstm_gate_rows(tiles)
            if upto < 3:
                pass
            elif g == 0:
                P.op("pool", lambda e: e.memset(Bext[:, 0:1], 0.0), writes=[B_scan])
                P.op("pool", lambda e: e.memset(mext[:, 0:1], 0.0), writes=[B_scan])
                P.op("pool", lambda e: e.memset(mu[:, :], 0.0), writes=[B_mu])
                P.op("pool", lambda e: e.memset(Chat[:], 0.0), writes=[B_Ch])
            else:
                P.op("dve", lambda e: e.tensor_copy(Bext[:, 0:1], Bext[:, GT:GT + 1]), reads=[B_scan], writes=[B_scan])
                P.op("dve", lambda e: e.tensor_copy(mext[:, 0:1], mext[:, GT:GT + 1]), reads=[B_scan], writes=[B_scan])
            if upto >= 3:
              P.op("dve", lambda e: e.tensor_tensor_scan(out=Bext[:, 1:GT + 1], data0=frow[:, 0:GT], data1=zeros8[:, 0:GT], initial=Bext[:, 0:1],
                                                       op0=ALU.add, op1=ALU.add), reads=[B_rows, B_z8, B_scan], writes=[B_scan])
            if upto >= 3:
              P.op("dve", lambda e: e.tensor_tensor_scan(out=mext[:, 1:GT + 1], data0=frow[:, 0:GT], data1=irow[:, 0:GT], initial=mext[:, 0:1],
                                                       op0=ALU.add, op1=ALU.max), reads=[B_rows, B_scan], writes=[B_scan])
            for t in range(NT if upto >= 4 else 0):
                mlstm_chunk(128, t * 128, vaug[:, t], B_v[t], so[:, t, :], B_so[t], hmT[:, :, t * 128:(t + 1) * 128], B_hm[t],
                            Bext[:, t * 128:t * 128 + 1], Bext[:, t * 128 + 1:(t + 1) * 128 + 1], irow[:, t * 128:(t + 1) * 128], maskT[:, :], g == 0 and t == 0)
            if upto >= 5:
                w_out_apply(w_out_ml, tiles, hmT, B_hm)
            if g == NG - 1 and upto >= 4:
                P.op("dve", lambda e: e.tensor_tensor(out=rs8[:, 8:9], in0=mu[:, 0:1], in1=mext[:, GT:GT + 1], op=ALU.subtract), reads=[B_mu, B_scan, B_rs8], writes=[B_rs8])
                P.op("act", lambda e: e.activation(out=rs8[:, 9:10], in_=rs8[:, 8:9], func=AF.Exp), reads=[B_rs8], writes=[B_rs8])
                P.op("dve", lambda e: e.tensor_scalar(dg8[:, :], pairsel[:, :], rs8[:, 9:10], None, ALU.mult), reads=[B_rs8, B_sel, B_dg8], writes=[B_dg8])
                P.op("pe", lambda e: e.matmul(pM[0][:, 16:20], lhsT=parsel[:, :], rhs=dg8[:, :], start=True, stop=True), reads=[B_dg8, B_sel], writes=[B_pM[0]])
                P.op("act", lambda e: e.copy(decb[:, :], pM[0][:, 16:20]), reads=[B_pM[0]], writes=[B_decb])
                P.op("dve", lambda e: e.tensor_tensor(out=nd[:, 0:4, :], in0=Chat[:], in1=decb[:, :].unsqueeze(2).to_broadcast([128, 4, 129]), op=ALU.mult),
                     reads=[B_Ch, B_decb, B_nd], writes=[B_nd])
                for hp in range(4):
                    P.dma("pool", lambda e, hp=hp, seq=seq: e.dma_start(out=Cp[seq, 2 * hp:2 * hp + 2, :, :].rearrange("a d e -> (a d) e"), in_=nd[:, hp, 0:128]),
                          reads=[B_nd], writes=[B_out])
                    P.dma("pool", lambda e, hp=hp, seq=seq: e.dma_start(out=np_[seq, 2 * hp:2 * hp + 2, :].rearrange("a (d o) -> (a d) o", o=1), in_=nd[:, hp, 128:129]),
                          reads=[B_nd], writes=[B_out])
                P.dma("pool", lambda e, seq=seq: e.dma_start(out=mp[seq:seq + 1, :].rearrange("o h -> h o"), in_=mext[:, GT:GT + 1]), reads=[B_scan], writes=[B_out])
            if stage >= 2:
                for t in range(NT):
                    rmsnorm_T(h[:, t, :], B_h[t], 128, 1, xnT[:, :, t * 128:(t + 1) * 128], B_xnT[t])
                ffn(0, tiles, GT)
            if stage >= 3:
                for t in range(NT):
                    rmsnorm_T(h[:, t, :], B_h[t], 128, 2, xnT[:, :, t * 128:(t + 1) * 128], B_xnT[t])
                mla_mix(seq, g, tiles)
            if stage >= 4:
                for t in range(NT):
                    rmsnorm_T(h[:, t, :], B_h[t], 128, 3, xnT[:, :, t * 128:(t + 1) * 128], B_xnT[t])
                ffn(1, tiles, GT)
                load_gfin()
                for t in range(NT):
                    final_norm(h[:, t, :], B_h[t], 128)
            for t in range(NT):
                P.dma("pool", lambda e, t=t, tok0=tok0: e.dma_start(out=yp[tok0 + t * 128: tok0 + (t + 1) * 128, :], in_=h[:, t, :]), reads=[B_h[t]], writes=[B_out])

    P.emit()
    return nc, es


def host_consts(S, past_len):
    ident = np.eye(128, dtype=np.float32)
    maskT = np.triu(np.ones((128, 128), np.float32))
    parsel = np.zeros((8, 128), np.float32)
    for k in range(8):
        parsel[k, (k % 2) * 64:(k % 2) * 64 + 64] = 1.0
    pairsel = np.zeros((8, 4), np.float32)
    for k in range(8):
        pairsel[k, k // 2] = 1.0
    inv = (10000.0 ** (-np.arange(0, 64, 2, dtype=np.float32) / np.float32(64))).astype(np.float32)
    pos = np.concatenate([np.arange(S, dtype=np.float32), np.array([past_len], np.float32)])
    ang = (pos[:, None] * inv[None, :]).astype(np.float32)
    cs_tm = np.concatenate([np.cos(ang), np.sin(ang)], axis=1).astype(np.float32)
    cs_fm = np.zeros((128, 2, S + 1), np.float32)
    for p in range(128):
        cs_fm[p, 0] = np.cos(ang[:, p % 32])
        cs_fm[p, 1] = np.sin(ang[:, p % 32])
    oh16 = np.ascontiguousarray(np.broadcast_to(np.eye(16, dtype=np.float32)[None], (128, 16, 16)))
    return dict(oh16=oh16, ident=ident, maskT=maskT, parsel=parsel, pairsel=pairsel, cs_tm=cs_tm, cs_fm=cs_fm)


def make_in_maps(inputs, cfg, ncores):
    S, NSP, NS, NPG, NPHYS = cfg["S"], cfg["NSP"], cfg["NS"], cfg["NPG"], cfg["NPHYS"]
    f = lambda a: np.ascontiguousarray(np.asarray(a))
    consts = host_consts(S, NPG * 128)
    gfm = np.stack([f(inputs["norm_mix"])[0], f(inputs["norm_ffn"])[0], f(inputs["norm_mix"])[1], f(inputs["norm_ffn"])[1],
                    f(inputs["norm_final"])], axis=0)
    gfm = np.ascontiguousarray(gfm.reshape(5, KC, 128).transpose(2, 0, 1))
    shared = dict(
        gfm=gfm, gqfm=np.ascontiguousarray(f(inputs["mla_g_q"])[0].reshape(3, 128).T), g_fin=f(inputs["norm_final"]).reshape(1, D), w_in_ml=f(inputs["mlstm_w_in"])[0], b_gates=f(inputs["mlstm_b_gates"])[0].reshape(1, 16),
        g_head=f(inputs["mlstm_g_head"])[0].reshape(1, D), w_out_ml=f(inputs["mlstm_w_out"])[0],
        w_in_mla=f(inputs["mla_w_in"])[0], g_q=f(inputs["mla_g_q"])[0].reshape(1, QL), g_kv=f(inputs["mla_g_kv"])[0].reshape(1, KVL),
        w_uq=f(inputs["mla_w_uq"])[0], w_uk=f(inputs["mla_w_uk"])[0], w_uv=f(inputs["mla_w_uv"])[0], w_o=f(inputs["mla_w_o"])[0],
        w_gu=f(inputs["ffn_w_gate_up"]), w_dn=f(inputs["ffn_w_down"]), **consts)
    xp_all = f(inputs["x_prompt"]); xs_all = f(inputs["x_sample"])
    maps = []
    for c in range(ncores):
        m = dict(shared)
        m["xp"] = xp_all[c * NSP:(c + 1) * NSP].reshape(NSP * S, D)
        m["xs"] = xs_all[c * NS:(c + 1) * NS].reshape(NS, D)
        m["stC"] = f(inputs["state_mlstm_C"])[0, c * NS:(c + 1) * NS]
        m["stn"] = f(inputs["state_mlstm_n"])[0, c * NS:(c + 1) * NS]
        m["stm"] = f(inputs["state_mlstm_m"])[0, c * NS:(c + 1) * NS]
        if cfg.get("use_cache", False):
            m["lat"] = f(inputs["cache_latent"])[0].reshape(NPHYS * 128, KVL)
            m["kr"] = f(inputs["cache_k_rope"])[0].reshape(NPHYS * 128, RP)
            m["ptb"] = f(inputs["page_table"])[c * NS:(c + 1) * NS].reshape(1, NS * NPG).astype(np.int32)
        maps.append(m)
    return maps


def gather_outputs(res, cfg, ncores):
    S, NSP, NS = cfg["S"], cfg["NSP"], cfg["NS"]
    R = res.results
    cat = lambda k: np.concatenate([R[c][k] for c in range(ncores)], axis=0)
    y_p = cat("yp").reshape(ncores * NSP, S, D)
    y_s = cat("ys").reshape(ncores * NS, 1, D)
    return (y_p, y_s, cat("Cp")[None], cat("np")[None], cat("mp")[None], cat("Cs")[None], cat("ns")[None], cat("ms")[None],
            cat("latp").reshape(1, ncores * NSP, S, KVL), cat("krp").reshape(1, ncores * NSP, S, RP),
            cat("lats").reshape(1, ncores * NS, 1, KVL), cat("krs").reshape(1, ncores * NS, 1, RP))


FULL_CFG = dict(S=2048, NSP=2, GT=512, NS=16, NPG=128, NPHYS=20480, use_cache=True)


def kernel(**inputs):
    cfg = dict(FULL_CFG)
    ncores = 8
    nc, es = build(cfg)
    maps = make_in_maps(inputs, cfg, ncores)
    res = run_bass_kernel_spmd(nc, maps, core_ids=list(range(ncores)))
    return gather_outputs(res, cfg, ncores)
```

```python
import numpy as np
from contextlib import ExitStack
import concourse.bass as bass
import concourse.mybir as mybir
from concourse.bass_utils import run_bass_kernel_spmd

F32, BF16, I32 = mybir.dt.float32, mybir.dt.bfloat16, mybir.dt.int32
ALU = mybir.AluOpType
AF = mybir.ActivationFunctionType
AX = mybir.AxisListType

D = 1024
KC = 8
NH = 8
DQK = 64
DV = 128
ML_IN = 3088
DFF = 2816
EPS = 1e-6
QL, KVL, RP = 384, 256, 64
MLA_SCALE = float((128 + 64) ** -0.5)


class Buf:
    __slots__ = ("name", "writer", "readers")

    def __init__(self, name):
        self.name = name
        self.writer = None
        self.readers = []


class Prog:
    ENG = ["pe", "act", "dve", "pool", "sp"]
    EMAP = {"pe": "tensor", "act": "scalar", "dve": "vector", "pool": "gpsimd", "sp": "sync"}

    def __init__(self, nc, es, ndma=14):
        self.nc = nc
        self.ops = {e: [] for e in self.ENG}
        self.cnt = {e: 0 for e in ["pe", "act", "dve", "pool"]}
        self.semh = {}
        for e in ["pe", "act", "dve", "pool"]:
            self.semh["c_" + e] = es.enter_context(nc.semaphore("c_" + e))
        self.ndma = ndma
        self.dma_cnt = {}
        self.dma_rr = {"sp": 0, "pool": 0}
        for q in ["sp", "pool"]:
            for i in range(ndma):
                k = f"d_{q}{i}"
                self.semh[k] = es.enter_context(nc.semaphore(k))
                self.dma_cnt[k] = 0
        self.seen = {e: {} for e in self.ENG}

    def _deps(self, eng, reads, writes):
        need = {}

        def add(tok):
            if tok is None:
                return
            k, v = tok
            if need.get(k, 0) < v:
                need[k] = v

        for b in reads:
            add(b.writer)
        for b in writes:
            add(b.writer)
            for t in b.readers:
                add(t)
        waits = []
        for k, v in need.items():
            if k == "c_pe" and eng == "pe":
                continue
            if self.seen[eng].get(k, 0) < v:
                self.seen[eng][k] = v
                waits.append((k, v))
        return waits

    def _upd(self, tok, reads, writes):
        for b in reads:
            b.readers.append(tok)
        for b in writes:
            b.writer = tok
            b.readers = []

    def op(self, eng, fn, reads=(), writes=()):
        waits = self._deps(eng, reads, writes)
        self.cnt[eng] += 1
        tok = ("c_" + eng, self.cnt[eng])
        self.ops[eng].append((waits, fn, ("c_" + eng, 1)))
        self._upd(tok, reads, writes)

    def dma(self, q, fn, reads=(), writes=()):
        i = self.dma_rr[q]
        self.dma_rr[q] = (i + 1) % self.ndma
        key = f"d_{q}{i}"
        waits = self._deps(q, reads, writes)
        prev = self.dma_cnt[key]
        if prev > 0 and self.seen[q].get(key, 0) < prev:
            self.seen[q][key] = prev
            waits.append((key, prev))
        self.dma_cnt[key] += 16
        tok = (key, self.dma_cnt[key])
        self.ops[q].append((waits, fn, (key, 16)))
        self._upd(tok, reads, writes)

    def emit(self):
        nc = self.nc
        with nc.Block() as block:
            for e in self.ENG:
                def body(eng, e=e):
                    for waits, fn, inc in self.ops[e]:
                        for k, v in waits:
                            eng.wait_ge(self.semh[k], v)
                        ins = fn(eng)
                        ins.then_inc(self.semh[inc[0]], inc[1])
                    if e in ("sp", "pool"):
                        for k, c in self.dma_cnt.items():
                            if k.startswith(f"d_{e}") and c > 0:
                                eng.wait_ge(self.semh[k], c)
                getattr(block, self.EMAP[e])(body)


def build(cfg):
    S = cfg["S"]
    NSP = cfg["NSP"]
    GT = cfg["GT"]
    NS = cfg["NS"]
    NPG = cfg["NPG"]
    NPHYS = cfg["NPHYS"]
    stage = cfg.get("stage", 99)
    NT = GT // 128
    NG = S // GT
    SLAB = min(512, GT)

    nc = bass.Bass("TRN2", target_bir_lowering=False)
    es = ExitStack()
    P = Prog(nc, es)

    def din(name, shape, dt=F32):
        return nc.dram_tensor(name, list(shape), dt, kind="ExternalInput").ap()

    def dout(name, shape, dt=F32):
        return nc.dram_tensor(name, list(shape), dt, kind="ExternalOutput").ap()

    xp = din("xp", [NSP * S, D])
    xs = din("xs", [NS, D])
    stC = din("stC", [NS, NH, DQK, DV])
    stn = din("stn", [NS, NH, DQK])
    stm = din("stm", [NS, NH])
    USE_CACHE = cfg.get("use_cache", False)
    if USE_CACHE:
        latkr = din("latkr", [NPHYS * 128, KVL + RP])
        ptb = din("ptb", [1, NS * NPG], I32)
    gfm = din("gfm", [128, 5, KC])
    w_in_ml = din("w_in_ml", [D, ML_IN])
    b_gates = din("b_gates", [1, 16])
    g_head = din("g_head", [1, D])
    w_out_ml = din("w_out_ml", [D, D])
    w_in_mla = din("w_in_mla", [D, QL + KVL + RP])
    g_q = din("g_q", [1, QL])
    g_kv = din("g_kv", [1, KVL])
    w_uq = din("w_uq", [QL, 8 * 192])
    w_uk = din("w_uk", [KVL, 1024])
    w_uv = din("w_uv", [KVL, 1024])
    w_o = din("w_o", [D, D])
    w_gu = din("w_gu", [2, D, 2 * DFF])
    w_dn = din("w_dn", [2, DFF, D])
    ident_d = din("ident", [128, 128])
    maskT_d = din("maskT", [128, 128])
    parsel_d = din("parsel", [8, 128])
    pairsel_d = din("pairsel", [8, 4])
    gqfm = din("gqfm", [128, 3])
    g_fin = din("g_fin", [1, D])
    oh16_d = din("oh16", [128, NS, NS])
    cs_tm = din("cs_tm", [S + 1, 64])
    cs_fm = din("cs_fm", [128, 2, S + 1])

    yp = dout("yp", [NSP * S, D])
    ys = dout("ys", [NS, D])
    Cp = dout("Cp", [NSP, NH, DQK, DV])
    np_ = dout("np", [NSP, NH, DQK])
    mp = dout("mp", [NSP, NH])
    Cs = dout("Cs", [NS, NH, DQK, DV])
    ns_ = dout("ns", [NS, NH, DQK])
    ms_ = dout("ms", [NS, NH])
    latp = dout("latp", [NSP * S, KVL])
    krp = dout("krp", [NSP * S, RP])
    lats = dout("lats", [NS, KVL])
    krs = dout("krs", [NS, RP])

    def sb(name, shape, dt=F32):
        return es.enter_context(nc.sbuf_tensor("s_" + name, list(shape), dt))

    def ps(name, shape, dt=F32):
        return es.enter_context(nc.psum_tensor("p_" + name, list(shape), dt))

    ident = sb("ident", [128, 128]); B_ident = Buf("ident")
    identb = sb("identb", [128, 128], BF16); B_identb = Buf("identb")
    maskT = sb("maskT", [128, 128]); B_maskT = Buf("maskT")
    maskTb = sb("maskTb", [128, 128], BF16)
    parsel = sb("parsel", [8, 128]); pairsel = sb("pairsel", [8, 4]); B_sel = Buf("sel")
    gfm_s = sb("gfm_s", [128, 5, KC]); B_gfm = Buf("gfm")
    bg_s = sb("bg_s", [128, 16]); ghead_s = sb("ghead_s", [128, D]); B_vec = Buf("vec")
    gkv_s = sb("gkv_s", [128, KVL])
    zeros8 = sb("zeros8", [8, GT if GT > 128 else 128]); B_z8 = Buf("z8")
    ones1 = sb("ones1", [128, 1], BF16)

    P.dma("sp", lambda e: e.dma_start(out=ident[:], in_=ident_d[:, :]), writes=[B_ident])
    P.dma("sp", lambda e: e.dma_start(out=maskT[:], in_=maskT_d[:, :]), writes=[B_maskT])
    P.dma("sp", lambda e: e.dma_start(out=parsel[:], in_=parsel_d[:, :]), writes=[B_sel])
    P.dma("sp", lambda e: e.dma_start(out=pairsel[:], in_=pairsel_d[:, :]), writes=[B_sel])
    P.dma("sp", lambda e: e.dma_start(out=gfm_s[:], in_=gfm[:, :, :]), writes=[B_gfm])
    P.dma("sp", lambda e: e.dma_start(out=bg_s[:], in_=b_gates[0:1, :].partition_broadcast(128)), writes=[B_vec])
    B_gh = Buf("gh")
    P.dma("sp", lambda e: e.dma_start(out=gkv_s[:], in_=g_kv[0:1, :].partition_broadcast(128)), writes=[B_vec])
    P.op("pool", lambda e: e.tensor_copy(identb[:], ident[:]), reads=[B_ident], writes=[B_identb])
    P.op("pool", lambda e: e.tensor_copy(maskTb[:], maskT[:]), reads=[B_maskT], writes=[B_maskT])
    P.op("pool", lambda e: e.memset(zeros8[:], 0.0), writes=[B_z8])
    P.op("pool", lambda e: e.memset(ones1[:], 1.0), writes=[B_z8])

    h = sb("h", [128, NT, D])
    B_h = [Buf(f"h{t}") for t in range(NT)]
    xnT = sb("xnT", [128, KC, GT], BF16)
    B_xnT = [Buf(f"xnT{t}") for t in range(NT)]

    WCH = 2048
    NSTG, NWB = 2, 2
    stg = [sb(f"stg{i}", [128, WCH]) for i in range(NSTG)]
    B_stg = [Buf(f"stg{i}") for i in range(NSTG)]
    wbuf = [sb(f"wb{i}", [128, WCH], BF16) for i in range(NWB)]
    B_wb = [Buf(f"wb{i}") for i in range(NWB)]
    wctr = [0, 0]

    def wload(dram2d, kc, ncols):
        assert kc * ncols <= WCH
        si = wctr[0] % NSTG; wctr[0] += 1
        wi = wctr[1] % NWB; wctr[1] += 1
        sv = stg[si][:, 0:kc * ncols].rearrange("p (k n) -> p k n", k=kc)
        wv = wbuf[wi][:, 0:kc * ncols].rearrange("p (k n) -> p k n", k=kc)
        src = dram2d.rearrange("(k p) n -> p k n", p=128)
        P.dma("sp", lambda e: e.dma_start(out=sv, in_=src), writes=[B_stg[si]])
        P.op("pool", lambda e: e.tensor_copy(wv, sv), reads=[B_stg[si]], writes=[B_wb[wi]])
        return wv, B_wb[wi]

    pA = [ps(f"pA{i}", [128, 512]) for i in range(2)]; B_pA = [Buf(f"pA{i}") for i in range(2)]
    pT = ps("pT", [128, 1024], BF16); B_pT = Buf("pT")
    pM = [ps(f"pM{i}", [128, 512]) for i in range(5)]; B_pM = [Buf(f"pM{i}") for i in range(5)]
    pactr = [0]

    def next_pA():
        i = pactr[0] % 2; pactr[0] += 1
        return pA[i], B_pA[i]

    xnb = sb("xnb", [128, D], BF16); B_xnb = Buf("xnb")
    junk = xnb; B_junk = B_xnb
    st4 = sb("st4", [128, 8]); B_st4 = Buf("st4")

    def rmsnorm_T(hap, Bh, rows, gi, dstT, Bdst, width=D, gap=None):
        nk = width // 128
        P.op("dve", lambda e: e.scalar_tensor_tensor(out=junk[0:rows, 0:width], in0=hap, scalar=1.0, in1=hap,
                                                     op0=ALU.mult, op1=ALU.mult, accum_out=st4[0:rows, 0:1]),
             reads=[Bh], writes=[B_junk, B_st4])
        P.op("dve", lambda e: e.tensor_scalar(st4[0:rows, 1:2], st4[0:rows, 0:1], 1.0 / width, EPS, ALU.mult, ALU.add),
             reads=[B_st4], writes=[B_st4])
        P.op("act", lambda e: e.activation(out=st4[0:rows, 2:3], in_=st4[0:rows, 1:2], func=AF.Ln), reads=[B_st4], writes=[B_st4])
        P.op("act", lambda e: e.activation(out=st4[0:rows, 3:4], in_=st4[0:rows, 2:3], func=AF.Exp, scale=-0.5), reads=[B_st4], writes=[B_st4])
        P.op("dve", lambda e: e.tensor_scalar(xnb[0:rows, 0:width], hap, st4[0:rows, 3:4], None, ALU.mult),
             reads=[Bh, B_st4], writes=[B_xnb])
        for k in range(nk):
            P.op("pe", lambda e, k=k: e.transpose(pT[:, k * 128:k * 128 + rows], xnb[0:rows, k * 128:(k + 1) * 128], identb[0:rows, 0:rows]),
                 reads=[B_xnb, B_identb], writes=[B_pT])
        src = pT[:, 0:nk * 128].rearrange("p (k t) -> p k t", k=nk)[:, :, 0:rows]
        if gap is None:
            gap_ = gfm_s[:, gi, 0:nk]
        else:
            gap_ = gap
        P.op("dve", lambda e: e.tensor_tensor(out=dstT, in0=src, in1=gap_.unsqueeze(2).to_broadcast([128, nk, rows]), op=ALU.mult),
             reads=[B_pT, B_gfm], writes=[Bdst])

    def final_norm(hap, Bh, rows):
        P.op("dve", lambda e: e.scalar_tensor_tensor(out=junk[0:rows, :], in0=hap, scalar=1.0, in1=hap, op0=ALU.mult, op1=ALU.mult,
                                                     accum_out=st4[0:rows, 0:1]), reads=[Bh], writes=[B_junk, B_st4])
        P.op("dve", lambda e: e.tensor_scalar(st4[0:rows, 1:2], st4[0:rows, 0:1], 1.0 / D, EPS, ALU.mult, ALU.add), reads=[B_st4], writes=[B_st4])
        P.op("act", lambda e: e.activation(out=st4[0:rows, 2:3], in_=st4[0:rows, 1:2], func=AF.Ln), reads=[B_st4], writes=[B_st4])
        P.op("act", lambda e: e.activation(out=st4[0:rows, 3:4], in_=st4[0:rows, 2:3], func=AF.Exp, scale=-0.5), reads=[B_st4], writes=[B_st4])
        P.op("dve", lambda e: e.scalar_tensor_tensor(out=hap, in0=hap, scalar=st4[0:rows, 3:4], in1=gfin_s[0:rows, :], op0=ALU.mult, op1=ALU.mult),
             reads=[Bh, B_st4, B_gh], writes=[Bh])

    def load_gfin():
        P.dma("sp", lambda e: e.dma_start(out=gfin_s[:], in_=g_fin[0:1, :].partition_broadcast(128)), writes=[B_gh])

    def proj_fm(wv, Bw, kc, ncols, xT, Bx_list, ntok, evac):
        for cb in range((ncols + 127) // 128):
            m = min(128, ncols - cb * 128)
            for t0 in range(0, ntok, 512):
                n = min(512, ntok - t0)
                pa, Bp = next_pA()
                for k in range(kc):
                    P.op("pe", lambda e, k=k, pa=pa, cb=cb, m=m, t0=t0, n=n: e.matmul(
                        pa[0:m, 0:n], lhsT=wv[:, k, cb * 128:cb * 128 + m], rhs=xT[:, k, t0:t0 + n],
                        start=(k == 0), stop=(k == kc - 1)), reads=[Bw] + Bx_list, writes=[Bp])
                evac(pa[0:m, 0:n], Bp, cb, t0, n)

    def proj_tm(wv, Bw, kc, ncols, xT, Bx_list, tiles, evac):
        for ti, (c0, rows) in enumerate(tiles):
            pa, Bp = next_pA()
            for k in range(kc):
                P.op("pe", lambda e, k=k, pa=pa, c0=c0, rows=rows: e.matmul(
                    pa[0:rows, 0:ncols], lhsT=xT[:, k, c0:c0 + rows], rhs=wv[:, k, 0:ncols],
                    start=(k == 0), stop=(k == kc - 1)), reads=[Bw] + Bx_list, writes=[Bp])
            evac(pa[0:rows, 0:ncols], Bp, ti, rows)

    NF_ = DFF // 128
    ASZ = max(NF_ * GT, 8 * GT + NT * NH * 129 + NT * D, 21 * GT + 2 * NH * 128 + D + 192)
    arena = sb("arena", [128, ASZ], BF16)
    qT = arena[:, 0:4 * GT].rearrange("p (c t) -> p c t", c=4); B_qT = Buf("qT")
    kT = arena[:, 4 * GT:8 * GT].rearrange("p (c t) -> p c t", c=4); B_kT = Buf("kT")
    o0 = 8 * GT
    vaug = arena[:, o0:o0 + NT * NH * 129].rearrange("p (t h e) -> p t h e", t=NT, h=NH); B_v = [Buf(f"v{t}") for t in range(NT)]
    o1 = o0 + NT * NH * 129
    so = arena[:, o1:o1 + NT * D].rearrange("p (t d) -> p t d", t=NT); B_so = [Buf(f"so{t}") for t in range(NT)]
    gates = sb("gates", [128, NT, 16]); B_g = Buf("gates")
    gtmp = sb("gtmp", [128, NT, 8, 4]); B_gt = Buf("gtmp")
    logf = sb("logf", [128, NT, 8])
    irow = sb("irow", [8, GT]); frow = sb("frow", [8, GT]); B_rows = Buf("rows")
    Bext = sb("Bext", [8, GT + 1]); mext = sb("mext", [8, GT + 1]); B_scan = Buf("scan")
    mu = sb("mu", [8, 1]); B_mu = Buf("mu")
    rs8 = sb("rs8", [8, 16]); B_rs8 = Buf("rs8")
    zrow = sb("zrow", [8, 128]); erow = sb("erow", [8, 128]); trow = sb("trow", [8, 128]); B_zr = Buf("zr")
    dg8 = sb("dg8", [8, 4]); B_dg8 = Buf("dg8")
    et = sb("et", [128, 16]); B_et = Buf("et")
    decb = sb("decb", [128, 4]); B_decb = Buf("decb")
    Chat = sb("Chat", [128, 4, 129]); B_Ch = Buf("Chat")
    Csb = sb("Csb", [128, 4, 129], BF16); B_Cs = Buf("Csb")
    vp = sb("vp", [128, NH, 129], BF16); B_vp = Buf("vp")
    ktok = sb("ktok", [128, 512], BF16); B_kt = Buf("ktok")
    STb = sb("STb", [128, 8, 128], BF16); B_ST = [Buf("ST0"), Buf("ST1")]
    mscr = sb("mscr", [128, max(NT * 704, 2816)])
    nd = mscr[:, 0:NH * 129].rearrange("p (a b) -> p a b", a=NH); B_nd = Buf("nd")
    sqs = mscr[:, NH * 129:NH * 129 + NH * 128].rearrange("p (a b) -> p a b", a=NH); B_sq = Buf("sqs")
    p8 = sb("p8", [128, 8, 8]); B_p8 = Buf("p8")
    ogb = junk; B_og = B_junk
    hmT = sb("hmT", [128, KC, GT], BF16); B_hm = [Buf(f"hm{t}") for t in range(NT)]

    def mlstm_post(R, soap, Bso, hmT_dst, Bhm, thr_ap, Bthr):
        den = nd[0:R, :, 128]
        P.op("dve", lambda e: e.scalar_tensor_tensor(out=p8[0:R, 0, :], in0=den, scalar=-1.0, in1=den, op0=ALU.mult, op1=ALU.max), reads=[B_nd], writes=[B_p8])
        P.op("dve", lambda e: e.tensor_tensor(out=p8[0:R, 1, :], in0=p8[0:R, 0, :], in1=thr_ap, op=ALU.max), reads=[B_p8, Bthr], writes=[B_p8])
        P.op("dve", lambda e: e.reciprocal(p8[0:R, 2, :], p8[0:R, 1, :]), reads=[B_p8], writes=[B_p8])
        P.op("pool", lambda e: e.tensor_tensor(out=sqs[0:R], in0=nd[0:R, :, 0:128], in1=nd[0:R, :, 0:128], op=ALU.mult), reads=[B_nd], writes=[B_sq])
        P.op("dve", lambda e: e.tensor_reduce(out=p8[0:R, 3, :], in_=sqs[0:R], axis=AX.X, op=ALU.add), reads=[B_sq], writes=[B_p8])
        P.op("dve", lambda e: e.tensor_tensor(out=p8[0:R, 4, :], in0=p8[0:R, 2, :], in1=p8[0:R, 2, :], op=ALU.mult), reads=[B_p8], writes=[B_p8])
        P.op("dve", lambda e: e.tensor_tensor(out=p8[0:R, 4, :], in0=p8[0:R, 4, :], in1=p8[0:R, 3, :], op=ALU.mult), reads=[B_p8], writes=[B_p8])
        P.op("dve", lambda e: e.tensor_scalar(p8[0:R, 4, :], p8[0:R, 4, :], 1.0 / 128, EPS, ALU.mult, ALU.add), reads=[B_p8], writes=[B_p8])
        P.op("act", lambda e: e.activation(out=p8[0:R, 5, :], in_=p8[0:R, 4, :], func=AF.Ln), reads=[B_p8], writes=[B_p8])
        P.op("act", lambda e: e.activation(out=p8[0:R, 6, :], in_=p8[0:R, 5, :], func=AF.Exp, scale=-0.5), reads=[B_p8], writes=[B_p8])
        P.op("dve", lambda e: e.tensor_tensor(out=p8[0:R, 7, :], in0=p8[0:R, 6, :], in1=p8[0:R, 2, :], op=ALU.mult), reads=[B_p8], writes=[B_p8])
        P.op("dve", lambda e: e.tensor_tensor(out=sqs[0:R], in0=nd[0:R, :, 0:128], in1=p8[0:R, 7, :].unsqueeze(2).to_broadcast([R, NH, 128]), op=ALU.mult),
             reads=[B_nd, B_p8, B_sq], writes=[B_sq])
        P.op("dve", lambda e: e.tensor_tensor(out=ogb[0:R, :], in0=sqs[0:R].rearrange("p a b -> p (a b)"), in1=soap, op=ALU.mult),
             reads=[B_sq, Bso], writes=[B_og])
        for k in range(KC):
            P.op("pe", lambda e, k=k: e.transpose(pT[:, k * 128:k * 128 + R], ogb[0:R, k * 128:(k + 1) * 128], identb[0:R, 0:R]),
                 reads=[B_og, B_identb], writes=[B_pT])
        P.op("act", lambda e: e.copy(hmT_dst, pT[:, :].rearrange("p (k t) -> p k t", k=KC)[:, :, 0:R]), reads=[B_pT], writes=[Bhm])


    def mlstm_chunk(rows, tcol, vap, Bv, soap, Bso, hmT_dst, Bhm, Bprev_col, Bslice, irow_sl, mask_ap, seq_first):
        R = rows
        P.op("dve", lambda e: e.scalar_tensor_tensor(out=zrow[:, 0:R], in0=irow_sl, scalar=Bprev_col, in1=Bslice,
                                                     op0=ALU.add, op1=ALU.subtract), reads=[B_rows, B_scan], writes=[B_zr])
        P.op("dve", lambda e: e.reduce_max(out=rs8[:, 0:1], in_=zrow[:, 0:R], axis=AX.X), reads=[B_zr], writes=[B_rs8])
        P.op("dve", lambda e: e.tensor_tensor(out=rs8[:, 1:2], in0=rs8[:, 0:1], in1=mu[:, 0:1], op=ALU.max), reads=[B_rs8, B_mu], writes=[B_rs8])
        P.op("dve", lambda e: e.tensor_scalar(rs8[:, 2:3], rs8[:, 1:2], -1.0, None, ALU.mult), reads=[B_rs8], writes=[B_rs8])
        P.op("dve", lambda e: e.tensor_tensor(out=rs8[:, 3:4], in0=Bprev_col, in1=rs8[:, 1:2], op=ALU.subtract), reads=[B_rs8, B_scan], writes=[B_rs8])
        P.op("act", lambda e: e.activation(out=erow[:, 0:R], in_=zrow[:, 0:R], func=AF.Exp, bias=rs8[:, 2:3], scale=1.0), reads=[B_zr, B_rs8], writes=[B_zr])
        P.op("act", lambda e: e.activation(out=trow[:, 0:R], in_=Bslice, func=AF.Exp, bias=rs8[:, 3:4], scale=-1.0), reads=[B_scan, B_rs8], writes=[B_zr])
        P.op("act", lambda e: e.activation(out=rs8[:, 4:5], in_=mu[:, 0:1], func=AF.Exp, bias=rs8[:, 2:3], scale=1.0), reads=[B_mu, B_rs8], writes=[B_rs8])
        P.op("dve", lambda e: e.scalar_tensor_tensor(out=mu[:, 0:1], in0=Bslice[:, R - 1:R], scalar=Bprev_col, in1=rs8[:, 1:2],
                                                     op0=ALU.subtract, op1=ALU.add), reads=[B_scan, B_rs8, B_mu], writes=[B_mu])
        pm0, Bm0 = pM[0], B_pM[0]
        P.op("pe", lambda e: e.transpose(pm0[0:R, 0:8], erow[:, 0:R], ident[0:8, 0:8]), reads=[B_zr, B_ident], writes=[Bm0])
        P.op("pe", lambda e: e.transpose(pm0[0:R, 8:16], trow[:, 0:R], ident[0:8, 0:8]), reads=[B_zr, B_ident], writes=[Bm0])
        P.op("act", lambda e: e.copy(et[0:R, :], pm0[0:R, 0:16]), reads=[Bm0], writes=[B_et])
        P.op("dve", lambda e: e.tensor_scalar(dg8[:, :], pairsel[:, :], rs8[:, 4:5], None, ALU.mult), reads=[B_rs8, B_sel], writes=[B_dg8])
        P.op("pe", lambda e: e.matmul(pm0[:, 16:20], lhsT=parsel[:, :], rhs=dg8[:, :], start=True, stop=True), reads=[B_dg8, B_sel], writes=[Bm0])
        P.op("act", lambda e: e.copy(decb[:, :], pm0[:, 16:20]), reads=[Bm0], writes=[B_decb])
        P.op("dve", lambda e: e.tensor_tensor(out=Chat[:], in0=Chat[:], in1=decb[:, :].unsqueeze(2).to_broadcast([128, 4, 129]), op=ALU.mult),
             reads=[B_Ch, B_decb], writes=[B_Ch])
        P.op("act", lambda e: e.copy(Csb[:], Chat[:]), reads=[B_Ch], writes=[B_Cs])
        P.op("dve", lambda e: e.tensor_tensor(out=vp[0:R], in0=vap, in1=et[0:R, 0:8].unsqueeze(2).to_broadcast([R, NH, 129]), op=ALU.mult),
             reads=[Bv, B_et], writes=[B_vp])
        for cb in range(4):
            P.op("pe", lambda e, cb=cb: e.transpose(pT[0:R, cb * 128:(cb + 1) * 128], kT[:, cb, tcol:tcol + R], identb[:, :]),
                 reads=[B_kT, B_identb], writes=[B_pT])
        P.op("act", lambda e: e.copy(ktok[0:R, :], pT[0:R, 0:512]), reads=[B_pT], writes=[B_kt])
        for par in range(2):
            pm, Bm = pM[1 + par], B_pM[1 + par]
            for hh in range(4):
                hd = hh * 2 + par
                cb, base = hd // 2, par * 64
                P.op("pe", lambda e, pm=pm, hh=hh, cb=cb, base=base: e.matmul(
                    pm[0:R, hh * 128:hh * 128 + R], lhsT=kT[base:base + 64, cb, tcol:tcol + R], rhs=qT[base:base + 64, cb, tcol:tcol + R],
                    start=True, stop=True), reads=[B_kT, B_qT], writes=[Bm])
            P.op("dve", lambda e, pm=pm, par=par: e.tensor_tensor(
                out=STb[0:R, par * 4:(par + 1) * 4, 0:R], in0=pm[0:R, :].rearrange("p (a b) -> p a b", a=4)[:, :, 0:R],
                in1=mask_ap.unsqueeze(1).to_broadcast([R, 4, R]), op=ALU.mult), reads=[Bm, B_maskT], writes=[B_ST[par]])
        for ph, hhs in enumerate([(0, 1, 2), (3,)]):
            for par in range(2):
                pm, Bm = pM[3 + par], B_pM[3 + par]
                for sl, hh in enumerate(hhs):
                    hd = hh * 2 + par
                    cb, base = hd // 2, par * 64
                    P.op("pe", lambda e, pm=pm, sl=sl, hd=hd, par=par, hh=hh: e.matmul(
                        pm[0:R, sl * 129:(sl + 1) * 129], lhsT=STb[0:R, par * 4 + hh, 0:R], rhs=vp[0:R, hd, :], start=True, stop=False),
                        reads=[B_ST[par], B_vp], writes=[Bm])
                    P.op("pe", lambda e, pm=pm, sl=sl, cb=cb, base=base, hd=hd: e.matmul(
                        pm[0:R, sl * 129:(sl + 1) * 129], lhsT=qT[base:base + 64, cb, tcol:tcol + R], rhs=Csb[base:base + 64, hd // 2, :],
                        start=False, stop=True), reads=[B_qT, B_Cs], writes=[Bm])
                for sl, hh in enumerate(hhs):
                    hd = hh * 2 + par
                    P.op("act", lambda e, pm=pm, sl=sl, hd=hd: e.copy(nd[0:R, hd, :], pm[0:R, sl * 129:(sl + 1) * 129]),
                         reads=[Bm], writes=[B_nd])
        for hp2 in range(2):
            pm, Bm = pM[1 + hp2], B_pM[1 + hp2]
            for hq in range(2):
                for par in range(2):
                    hd = (hp2 * 2 + hq) * 2 + par
                    P.op("pe", lambda e, pm=pm, hq=hq, par=par, hd=hd: e.matmul(
                        pm[par * 64:(par + 1) * 64, hq * 129:(hq + 1) * 129], lhsT=ktok[0:R, hd * 64:(hd + 1) * 64], rhs=vp[0:R, hd, :],
                        start=True, stop=True), reads=[B_kt, B_vp], writes=[Bm])
            P.op("dve", lambda e, pm=pm, hp2=hp2: e.tensor_tensor(
                out=Chat[:, hp2 * 2:hp2 * 2 + 2, :], in0=Chat[:, hp2 * 2:hp2 * 2 + 2, :],
                in1=pm[:, 0:258].rearrange("p (a b) -> p a b", a=2), op=ALU.add), reads=[Bm, B_Ch], writes=[B_Ch])
        mlstm_post(R, soap, Bso, hmT_dst, Bhm, et[0:R, 8:16], B_et)

    def mlstm_project(tiles, ntok):
        allx = B_xnT
        P.dma("sp", lambda e: e.dma_start(out=ghead_s[:], in_=g_head[0:1, :].partition_broadcast(128)), writes=[B_gh])
        P.op("pool", lambda e: e.memset(vaug[:, :, :, 128:129], 1.0), writes=B_v)
        for c in range(2):
            wv, Bw = wload(w_in_ml[:, c * 256:(c + 1) * 256], KC, 256)
            proj_fm(wv, Bw, KC, 256, xnT, allx, ntok,
                    lambda pa, Bp, cb, t0, n, c=c: P.op("act", lambda e: e.copy(qT[:, c * 2 + cb, t0:t0 + n], pa), reads=[Bp], writes=[B_qT]))
        for c in range(2):
            wv, Bw = wload(w_in_ml[:, 512 + c * 256:512 + (c + 1) * 256], KC, 256)
            proj_fm(wv, Bw, KC, 256, xnT, allx, ntok,
                    lambda pa, Bp, cb, t0, n, c=c: P.op("act", lambda e: e.mul(kT[:, c * 2 + cb, t0:t0 + n], pa, DQK ** -0.5), reads=[Bp], writes=[B_kT]))
        for c in range(4):
            wv, Bw = wload(w_in_ml[:, 1024 + c * 256:1024 + (c + 1) * 256], KC, 256)
            proj_tm(wv, Bw, KC, 256, xnT, allx, tiles,
                    lambda pa, Bp, ti, rows, c=c: P.op("act", lambda e: e.copy(
                        vaug[0:rows, ti, 2 * c:2 * c + 2, 0:128], pa.rearrange("p (a b) -> p a b", a=2)), reads=[Bp], writes=[B_v[ti]]))
        for c in range(4):
            wv, Bw = wload(w_in_ml[:, 2048 + c * 256:2048 + (c + 1) * 256], KC, 256)

            def ev(pa, Bp, ti, rows, c=c):
                P.op("act", lambda e: e.activation(out=junk[0:rows, 0:256], in_=pa, func=AF.Sigmoid), reads=[Bp], writes=[B_junk])
                P.op("dve", lambda e: e.tensor_tensor(out=so[0:rows, ti, c * 256:(c + 1) * 256], in0=junk[0:rows, 0:256],
                                                      in1=ghead_s[0:rows, c * 256:(c + 1) * 256], op=ALU.mult), reads=[B_junk, B_gh], writes=[B_so[ti]])
            proj_tm(wv, Bw, KC, 256, xnT, allx, tiles, ev)
        wv, Bw = wload(w_in_ml[:, 3072:3088], KC, 16)
        proj_tm(wv, Bw, KC, 16, xnT, allx, tiles,
                lambda pa, Bp, ti, rows: P.op("dve", lambda e: e.tensor_tensor(out=gates[0:rows, ti, :], in0=pa, in1=bg_s[0:rows, :], op=ALU.add),
                                              reads=[Bp, B_vec], writes=[B_g]))

    def mlstm_gate_rows(tiles, do_rows=True):
        nt = len(tiles)
        R = tiles[0][1]
        f = gates[0:R, 0:nt, 8:16]
        P.op("dve", lambda e: e.scalar_tensor_tensor(out=gtmp[0:R, 0:nt, :, 0], in0=f, scalar=-1.0, in1=f, op0=ALU.mult, op1=ALU.max), reads=[B_g], writes=[B_gt])
        P.op("act", lambda e: e.activation(out=gtmp[0:R, 0:nt, :, 1], in_=gtmp[0:R, 0:nt, :, 0], func=AF.Exp, scale=-1.0), reads=[B_gt], writes=[B_gt])
        P.op("act", lambda e: e.activation(out=gtmp[0:R, 0:nt, :, 2], in_=gtmp[0:R, 0:nt, :, 1], func=AF.Ln, bias=1.0, scale=1.0), reads=[B_gt], writes=[B_gt])
        P.op("dve", lambda e: e.tensor_scalar(gtmp[0:R, 0:nt, :, 3], f, 0.0, None, ALU.min), reads=[B_g, B_gt], writes=[B_gt])
        P.op("dve", lambda e: e.tensor_tensor(out=logf[0:R, 0:nt, :], in0=gtmp[0:R, 0:nt, :, 3], in1=gtmp[0:R, 0:nt, :, 2], op=ALU.subtract),
             reads=[B_gt], writes=[B_gt])
        if not do_rows:
            return
        for ti, (c0, rows) in enumerate(tiles):
            pm0, Bm0 = pM[0], B_pM[0]
            P.op("pe", lambda e, ti=ti, rows=rows: e.transpose(pm0[0:8, 0:rows], gates[0:rows, ti, 0:8], ident[0:rows, 0:rows]),
                 reads=[B_g, B_ident], writes=[Bm0])
            P.op("pe", lambda e, ti=ti, rows=rows: e.transpose(pm0[0:8, 128:128 + rows], logf[0:rows, ti, :], ident[0:rows, 0:rows]),
                 reads=[B_gt, B_ident], writes=[Bm0])
            P.op("act", lambda e, c0=c0, rows=rows: e.copy(irow[:, c0:c0 + rows], pm0[0:8, 0:rows]), reads=[Bm0], writes=[B_rows])
            P.op("act", lambda e, c0=c0, rows=rows: e.copy(frow[:, c0:c0 + rows], pm0[0:8, 128:128 + rows]), reads=[Bm0], writes=[B_rows])

    def w_out_apply(w2d, tiles, srcT, Bsrc, nkc=KC):
        for c in range(4):
            wv, Bw = wload(w2d[:, c * 256:(c + 1) * 256], nkc, 256)
            proj_tm(wv, Bw, nkc, 256, srcT, Bsrc, tiles,
                    lambda pa, Bp, ti, rows, c=c: P.op("dve", lambda e: e.tensor_tensor(
                        out=h[0:rows, ti, c * 256:(c + 1) * 256], in0=pa, in1=h[0:rows, ti, c * 256:(c + 1) * 256], op=ALU.add),
                        reads=[Bp, B_h[ti]], writes=[B_h[ti]]))

    actT = arena[:, 0:NF_ * GT].rearrange("p (f t) -> p f t", f=NF_); B_act = Buf("actT")
    sil = mscr[:, 0:512]; B_sil = Buf("sil")

    def ffn(layer, tiles, ntok):
        NF = DFF // 128
        for fc in range(NF):
            wg, Bwg = wload(w_gu[layer, :, fc * 128:(fc + 1) * 128], KC, 128)
            wu, Bwu = wload(w_gu[layer, :, DFF + fc * 128:DFF + (fc + 1) * 128], KC, 128)
            for t0 in range(0, ntok, 512):
                n = min(512, ntok - t0)
                pg, Bpg = next_pA()
                for k in range(KC):
                    P.op("pe", lambda e, k=k, pg=pg, t0=t0, n=n, wg=wg: e.matmul(pg[:, 0:n], lhsT=wg[:, k, :], rhs=xnT[:, k, t0:t0 + n],
                                                                                start=(k == 0), stop=(k == KC - 1)), reads=[Bwg] + B_xnT, writes=[Bpg])
                pu, Bpu = next_pA()
                for k in range(KC):
                    P.op("pe", lambda e, k=k, pu=pu, t0=t0, n=n, wu=wu: e.matmul(pu[:, 0:n], lhsT=wu[:, k, :], rhs=xnT[:, k, t0:t0 + n],
                                                                                start=(k == 0), stop=(k == KC - 1)), reads=[Bwu] + B_xnT, writes=[Bpu])
                P.op("act", lambda e, pg=pg, n=n: e.activation(out=sil[:, 0:n], in_=pg[:, 0:n], func=AF.Silu), reads=[Bpg], writes=[B_sil])
                P.op("dve", lambda e, pu=pu, n=n, fc=fc, t0=t0: e.tensor_tensor(out=actT[:, fc, t0:t0 + n], in0=pu[:, 0:n], in1=sil[:, 0:n], op=ALU.mult),
                     reads=[Bpu, B_sil], writes=[B_act])
        for half in range(2):
            wds = []
            for ti, (c0, rows) in enumerate(tiles):
                pass
            groups = [(k0, min(4, NF - k0)) for k0 in range(0, NF, 4)]
            accs = [(pM[i], B_pM[i]) for i in range(5)] + [(pA[0], B_pA[0]), (pA[1], B_pA[1])]
            for tb in range(0, len(tiles), len(accs)):
                tl = tiles[tb:tb + len(accs)]
                for (k0, nk) in groups:
                    wv, Bw = wload(w_dn[layer, k0 * 128:(k0 + nk) * 128, half * 512:(half + 1) * 512], nk, 512)
                    for j, (c0, rows) in enumerate(tl):
                        pa, Bp = accs[j]
                        for kk in range(nk):
                            P.op("pe", lambda e, pa=pa, kk=kk, k0=k0, c0=c0, rows=rows, wv=wv: e.matmul(
                                pa[0:rows, 0:512], lhsT=actT[:, k0 + kk, c0:c0 + rows], rhs=wv[:, kk, :],
                                start=(k0 + kk == 0), stop=(k0 + kk == NF - 1)), reads=[Bw, B_act], writes=[Bp])
                for j, (c0, rows) in enumerate(tl):
                    pa, Bp = accs[j]
                    ti = tb + j
                    P.op("dve", lambda e, pa=pa, ti=ti, rows=rows, half=half: e.tensor_tensor(
                        out=h[0:rows, ti, half * 512:(half + 1) * 512], in0=pa[0:rows, 0:512], in1=h[0:rows, ti, half * 512:(half + 1) * 512], op=ALU.add),
                        reads=[Bp, B_h[ti]], writes=[B_h[ti]])

    NKB = S // 128
    knT = sb("knT", [128, NH, S], BF16); B_kn = Buf("knT")
    krT = sb("krT", [64, S], BF16); B_kr = Buf("krT")
    Vh = sb("Vh", [128, NKB, NH, 129], BF16); B_Vh = Buf("Vh")
    P.op("pool", lambda e: e.memset(Vh[:, :, :, 128:129], 1.0), writes=[B_Vh])
    gqfm_s = sb("gqfm_s", [128, 3])
    P.dma("sp", lambda e: e.dma_start(out=gqfm_s[:], in_=gqfm[:, :]), writes=[B_gfm])
    gfin_s = ghead_s
    ckq = mscr[:, 0:NT * 704].rearrange("p (t c) -> p t c", t=NT); B_ckq = [Buf(f"ckq{t}") for t in range(NT)]
    a0 = 0
    cqnT = arena[:, a0:a0 + 3 * GT].rearrange("p (c t) -> p c t", c=3); B_cqn = [Buf(f"cqn{t}") for t in range(NT)]; a0 += 3 * GT
    ckvT = arena[:, a0:a0 + 2 * GT].rearrange("p (c t) -> p c t", c=2); B_ckvT = [Buf(f"ckvT{t}") for t in range(NT)]; a0 += 2 * GT
    qnT = arena[:, a0:a0 + NH * GT].rearrange("p (h t) -> p h t", h=NH); B_qn = Buf("qnT"); a0 += NH * GT
    qrT = arena[:, a0:a0 + NH * GT].rearrange("p (h t) -> p h t", h=NH); B_qr = Buf("qrT"); a0 += NH * GT
    PTb = arena[:, a0:a0 + 2 * NH * 128].rearrange("p (b h t) -> p b h t", b=2, h=NH); B_PT = [Buf("PT0"), Buf("PT1")]; a0 += 2 * NH * 128
    attb = arena[:, a0:a0 + D]; B_att = Buf("attb"); a0 += D
    wrot = arena[:, a0:a0 + 3 * 64].rearrange("p (c r) -> p c r", c=3); B_wrot = Buf("wrot"); a0 += 192
    assert a0 <= ASZ, (a0, ASZ)
    sarena = knT[:, :, :].rearrange("p h s -> p (h s)") if NH * S >= 13000 else sb("sarena", [128, 13000], BF16)
    arena_p, arena, a0 = arena, sarena, 0
    NKP = 16
    kp = [arena[:, a0 + i * 322:a0 + (i + 1) * 322] for i in range(NKP)]; B_kp = [Buf(f"kp{i}") for i in range(NKP)]; a0 += NKP * 322
    KTp = [arena[:, a0 + i * 384:a0 + (i + 1) * 384].rearrange("p (c k) -> p c k", c=3) for i in range(2)]; B_KTp = [Buf("KTp0"), Buf("KTp1")]; a0 += 768
    PTs = [arena[:, a0 + i * 8:a0 + (i + 1) * 8] for i in range(2)]; B_PTs = [Buf("PTs0"), Buf("PTs1")]; a0 += 16
    wukT = arena[:, a0:a0 + 2048].rearrange("p (h k c) -> p h k c", h=NH, k=2); B_wukT = Buf("wukT"); a0 += 2048
    qlT = arena[:, a0:a0 + 2 * NS * NH].rearrange("p (k t h) -> p k t h", k=2, t=NS); B_qlT = Buf("qlT"); a0 += 2 * NS * NH
    cnew = arena[:, a0:a0 + 258]; B_cnew = Buf("cnew"); a0 += 258
    krTn = arena[:, a0:a0 + NS]; B_krTn = Buf("krTn"); a0 += NS
    olat = arena[:, a0:a0 + 256]; B_olat = Buf("olat"); a0 += 256
    olT = arena[:, a0:a0 + 2 * NH * NS].rearrange("p (k h t) -> p k h t", k=2, h=NH); B_olT = Buf("olT"); a0 += 2 * NH * NS
    pnew = arena[:, a0:a0 + 8]; B_pnew = Buf("pnew"); a0 += 8
    qs_b = arena[:, a0:a0 + 4 * NS].rearrange("p (c t) -> p c t", c=4); B_qs = Buf("qs"); a0 += 4 * NS
    qTm = arena[:, a0:a0 + 4 * NS * NS].rearrange("p (c a t) -> p c a t", c=4, a=NS); B_qTm = Buf("qTm"); a0 += 4 * NS * NS
    keb = arena[:, a0:a0 + 512]; B_keb = Buf("keb"); a0 += 512
    kmt = [arena[:, a0 + i * 512:a0 + (i + 1) * 512] for i in range(2)]; B_kmt = [Buf("kmt0"), Buf("kmt1")]; a0 += 1024
    c0b = [arena[:, a0 + i * 516:a0 + (i + 1) * 516].rearrange("p (a b) -> p a b", a=4) for i in range(2)]; B_c0b = [Buf("c0b0"), Buf("c0b1")]; a0 += 1032
    assert a0 <= 13000, a0
    arena = arena_p
    if USE_CACHE:
        idx_i = sb("idx_i", [128, NPG], I32); idx_f = sb("idx_f", [128, NPG]); B_idx = Buf("idx")
        iota_p = sb("iota_p", [128, 1]); B_iota = Buf("iota")
        P.op("pool", lambda e: e.iota(iota_p[:], pattern=[[0, 1]], base=0, channel_multiplier=1, allow_small_or_imprecise_dtypes=True), writes=[B_iota])
    oh16 = sb("oh16", [128, NS, NS]); B_oh = Buf("oh16")
    P.dma("sp", lambda e: e.dma_start(out=oh16[:], in_=oh16_d[:, :, :]), writes=[B_oh])
    sg = sb("sg", [NS, 12, 8]); B_sg = Buf("sg")
    d8 = sb("d8", [8, NS]); R8 = sb("R8", [8, 4, NS]); B_d8 = Buf("d8")
    decT = sb("decT", [128, 4, NS]); B_decT = Buf("decT")
    _c0 = mscr[:, 2056:2056 + 516].rearrange("p (a b) -> p a b", a=4)
    c0f = [_c0, _c0]; _B = Buf("c0f"); B_c0f = [_B, _B]

    def sample_attention(R):
        for c in range(4):
            wv, Bw = wload(w_uk[:, c * 256:(c + 1) * 256], 2, 256)
            for kc in range(2):
                for hh in range(2):
                    sl = kc * 2 + hh
                    P.op("pe", lambda e, wv=wv, kc=kc, hh=hh, sl=sl: e.transpose(pT[:, sl * 128:(sl + 1) * 128], wv[:, kc, hh * 128:(hh + 1) * 128], identb[:, :]),
                         reads=[Bw, B_identb], writes=[B_pT])
            P.op("act", lambda e, c=c: e.copy(wukT[:, 2 * c:2 * c + 2, :, :], pT[:, 0:512].rearrange("p (k h c) -> p h k c", k=2, h=2)), reads=[B_pT], writes=[B_wukT])
        pa, Bp = next_pA()
        for kc in range(2):
            for hd in range(NH):
                o = (kc * NH + hd) * R
                P.op("pe", lambda e, pa=pa, kc=kc, hd=hd, o=o: e.matmul(pa[:, o:o + R], lhsT=wukT[:, hd, kc, :], rhs=qnT[:, hd, 0:R], start=True, stop=True),
                     reads=[B_wukT, B_qn], writes=[Bp])
        P.op("act", lambda e, pa=pa: e.copy(qlT[:, :, 0:R, :].rearrange("p k t h -> p k h t"), pa[:, 0:2 * NH * R].rearrange("p (k h t) -> p k h t", k=2, h=NH)),
             reads=[Bp], writes=[B_qlT])
        P.op("pool", lambda e: e.memset(cnew[0:R, 0:1], 0.0), writes=[B_cnew])
        P.op("pool", lambda e: e.memset(cnew[0:R, 1:2], 1.0), writes=[B_cnew])
        for i in range(NKP):
            P.op("pool", lambda e, i=i: e.memset(kp[i][:, 0:1], 0.0), writes=[B_kp[i]])
            P.op("pool", lambda e, i=i: e.memset(kp[i][:, 1:2], 1.0), writes=[B_kp[i]])
        kctr = 0
        for t in range(R):
            if USE_CACHE:
                P.dma("sp", lambda e, t=t: e.dma_start(out=idx_i[:], in_=ptb[0:1, t * NPG:(t + 1) * NPG].partition_broadcast(128)), writes=[B_idx])
                P.op("pool", lambda e: e.tensor_copy(idx_f[:], idx_i[:]), reads=[B_idx], writes=[B_idx])
                P.op("pool", lambda e: e.tensor_scalar(idx_f[:], idx_f[:], 128.0, iota_p[:, 0:1], ALU.mult, ALU.add), reads=[B_idx, B_iota], writes=[B_idx])
                P.op("pool", lambda e: e.tensor_copy(idx_i[:], idx_f[:]), reads=[B_idx], writes=[B_idx])
            acc, Bacc = pM[0], B_pM[0]
            npg = NPG if USE_CACHE else 0
            for j in range(npg):
                ki = kctr % NKP; k2 = kctr % 2; kctr += 1
                P.dma("pool", lambda e, ki=ki, j=j: e.indirect_dma_start(out=kp[ki][:, 2:322], out_offset=None, in_=latkr[:, :],
                                                                    in_offset=bass.IndirectOffsetOnAxis(ap=idx_i[:, j:j + 1], axis=0)), reads=[B_idx], writes=[B_kp[ki]])
                for k in range(2):
                    P.op("pe", lambda e, ki=ki, k=k: e.transpose(pT[:, k * 128:(k + 1) * 128], kp[ki][:, 2 + k * 128:2 + (k + 1) * 128], identb[:, :]),
                         reads=[B_kp[ki], B_identb], writes=[B_pT])
                P.op("pe", lambda e, ki=ki: e.transpose(pT[0:64, 256:384], kp[ki][:, 258:322], identb[:, :]), reads=[B_kp[ki], B_identb], writes=[B_pT])
                P.op("act", lambda e, k2=k2: e.copy(KTp[k2][:, 0:2, :], pT[:, 0:256].rearrange("p (c k) -> p c k", c=2)), reads=[B_pT], writes=[B_KTp[k2]])
                P.op("dve", lambda e, k2=k2: e.tensor_copy(KTp[k2][0:64, 2, :], pT[0:64, 256:384]), reads=[B_pT], writes=[B_KTp[k2]])
                ps_, Bps = next_pA()
                for k in range(2):
                    P.op("pe", lambda e, ps_=ps_, k=k, k2=k2, t=t: e.matmul(ps_[:, 0:8], lhsT=KTp[k2][:, k, :], rhs=qlT[:, k, t, :], start=(k == 0), stop=False),
                         reads=[B_KTp[k2], B_qlT], writes=[Bps])
                P.op("pe", lambda e, ps_=ps_, k2=k2, t=t: e.matmul(ps_[:, 0:8], lhsT=KTp[k2][0:64, 2, :], rhs=qrT[0:64, :, t], start=False, stop=True),
                     reads=[B_KTp[k2], B_qr], writes=[Bps])
                P.op("act", lambda e, ps_=ps_, k2=k2: e.activation(out=PTs[k2][:, :], in_=ps_[:, 0:8], func=AF.Exp, scale=MLA_SCALE), reads=[Bps], writes=[B_PTs[k2]])
                P.op("pe", lambda e, k2=k2, ki=ki, j=j: e.matmul(acc[0:8, 0:258], lhsT=PTs[k2][:, :], rhs=kp[ki][:, 0:258], start=(j == 0), stop=False),
                     reads=[B_PTs[k2], B_kp[ki]], writes=[Bacc])
            pn, Bpn = pM[1], B_pM[1]
            for k in range(2):
                P.op("pe", lambda e, k=k, t=t: e.matmul(pn[0:R, 0:8], lhsT=ckvT[:, k, 0:R], rhs=qlT[:, k, t, :], start=(k == 0), stop=False),
                     reads=B_ckvT + [B_qlT], writes=[Bpn])
            P.op("pe", lambda e, t=t: e.matmul(pn[0:R, 0:8], lhsT=krTn[0:64, 0:R], rhs=qrT[0:64, :, t], start=False, stop=True), reads=[B_krTn, B_qr], writes=[Bpn])
            P.op("act", lambda e: e.activation(out=sg[0:R, 9, :], in_=pn[0:R, 0:8], func=AF.Exp, scale=MLA_SCALE), reads=[Bpn], writes=[B_sg])
            P.op("dve", lambda e, t=t: e.tensor_scalar(pnew[0:R, :], sg[0:R, 9, :], ident[0:R, t:t + 1], None, ALU.mult), reads=[B_sg, B_ident], writes=[B_pnew])
            P.op("pe", lambda e, npg=npg: e.matmul(acc[0:8, 0:258], lhsT=pnew[0:R, :], rhs=cnew[0:R, 0:258], start=(npg == 0), stop=True), reads=[B_pnew, B_cnew], writes=[Bacc])
            P.op("dve", lambda e: e.reciprocal(orc[0:8, 0:1], acc[0:8, 1:2]), reads=[Bacc], writes=[B_orc])
            P.op("dve", lambda e: e.tensor_scalar(olat[0:8, :], acc[0:8, 2:258], orc[0:8, 0:1], None, ALU.mult), reads=[Bacc, B_orc], writes=[B_olat])
            for k in range(2):
                P.op("pe", lambda e, k=k: e.transpose(pT[:, 512 + k * 8:512 + (k + 1) * 8], olat[0:8, k * 128:(k + 1) * 128], identb[0:8, 0:8]), reads=[B_olat, B_identb], writes=[B_pT])
            P.op("act", lambda e, t=t: e.copy(olT[:, :, :, t], pT[:, 512:528].rearrange("p (k h) -> p k h", k=2)), reads=[B_pT], writes=[B_olT])
        for c in range(2):
            wv, Bw = wload(w_uv[:, c * 512:(c + 1) * 512], 2, 512)
            pa, Bp = next_pA()
            for hh in range(4):
                hd = c * 4 + hh
                for k in range(2):
                    P.op("pe", lambda e, pa=pa, wv=wv, hh=hh, hd=hd, k=k: e.matmul(pa[:, hh * R:(hh + 1) * R], lhsT=wv[:, k, hh * 128:(hh + 1) * 128], rhs=olT[:, k, hd, 0:R],
                                                                            start=(k == 0), stop=(k == 1)), reads=[Bw, B_olT], writes=[Bp])
            P.op("act", lambda e, pa=pa, c=c: e.copy(hmT[:, 4 * c:4 * c + 4, 0:R], pa[:, 0:4 * R].rearrange("p (h t) -> p h t", h=4)), reads=[Bp], writes=[B_hm[0]])

    def sample_group():
        R = NS
        tiles = [(0, R)]
        P.dma("sp", lambda e: e.dma_start(out=h[0:R, 0, :], in_=xs[:, :]), writes=[B_h[0]])
        rmsnorm_T(h[0:R, 0, :], B_h[0], R, 0, xnT[:, :, 0:R], B_xnT[0])
        mlstm_project(tiles, R)
        mlstm_gate_rows(tiles, do_rows=False)
        ig = gates[0:R, 0, 0:8]; lf = logf[0:R, 0, :]
        P.dma("sp", lambda e: e.dma_start(out=sg[0:R, 0, :], in_=stm[:, :]), writes=[B_sg])
        P.op("dve", lambda e: e.tensor_tensor(out=sg[0:R, 1, :], in0=ig, in1=lf, op=ALU.subtract), reads=[B_g, B_gt, B_sg], writes=[B_sg])
        P.op("dve", lambda e: e.tensor_tensor(out=sg[0:R, 2, :], in0=sg[0:R, 0, :], in1=sg[0:R, 1, :], op=ALU.max), reads=[B_sg], writes=[B_sg])
        P.op("dve", lambda e: e.tensor_tensor(out=sg[0:R, 3, :], in0=lf, in1=sg[0:R, 2, :], op=ALU.add), reads=[B_gt, B_sg], writes=[B_sg])
        P.op("dve", lambda e: e.tensor_tensor(out=sg[0:R, 4, :], in0=sg[0:R, 0, :], in1=sg[0:R, 2, :], op=ALU.subtract), reads=[B_sg], writes=[B_sg])
        P.op("act", lambda e: e.activation(out=sg[0:R, 4, :], in_=sg[0:R, 4, :], func=AF.Exp), reads=[B_sg], writes=[B_sg])
        P.op("dve", lambda e: e.tensor_tensor(out=sg[0:R, 5, :], in0=sg[0:R, 1, :], in1=sg[0:R, 2, :], op=ALU.subtract), reads=[B_sg], writes=[B_sg])
        P.op("act", lambda e: e.activation(out=sg[0:R, 5, :], in_=sg[0:R, 5, :], func=AF.Exp), reads=[B_sg], writes=[B_sg])
        P.op("act", lambda e: e.activation(out=sg[0:R, 6, :], in_=sg[0:R, 3, :], func=AF.Exp, scale=-1.0), reads=[B_sg], writes=[B_sg])
        P.dma("pool", lambda e: e.dma_start(out=ms_[:, :], in_=sg[0:R, 3, :]), reads=[B_sg], writes=[B_out])
        for cb in range(4):
            P.op("pe", lambda e, cb=cb: e.transpose(pT[0:R, cb * 128:(cb + 1) * 128], qT[:, cb, 0:R], identb[:, :]), reads=[B_qT, B_identb], writes=[B_pT])
        P.op("act", lambda e: e.copy(xnb[0:R, 0:512], pT[0:R, 0:512]), reads=[B_pT], writes=[B_xnb])
        for cb in range(4):
            P.op("pe", lambda e, cb=cb: e.transpose(pT[0:R, cb * 128:(cb + 1) * 128], kT[:, cb, 0:R], identb[:, :]), reads=[B_kT, B_identb], writes=[B_pT])
        P.op("act", lambda e: e.copy(ktok[0:R, :], pT[0:R, 0:512]), reads=[B_pT], writes=[B_kt])
        sq64 = sqs[0:R, :, 0:64]
        P.op("dve", lambda e: e.tensor_tensor(out=sq64, in0=xnb[0:R, 0:512].rearrange("p (h d) -> p h d", h=NH), in1=ktok[0:R, :].rearrange("p (h d) -> p h d", h=NH), op=ALU.mult),
             reads=[B_xnb, B_kt], writes=[B_sq])
        P.op("dve", lambda e: e.tensor_reduce(out=sg[0:R, 7, :], in_=sq64, axis=AX.X, op=ALU.add), reads=[B_sq, B_sg], writes=[B_sg])
        P.op("dve", lambda e: e.tensor_tensor(out=sg[0:R, 8, :], in0=sg[0:R, 7, :], in1=sg[0:R, 5, :], op=ALU.mult), reads=[B_sg], writes=[B_sg])
        P.op("dve", lambda e: e.tensor_tensor(out=nd[0:R], in0=vaug[0:R, 0], in1=sg[0:R, 8, :].unsqueeze(2).to_broadcast([R, NH, 129]), op=ALU.mult),
             reads=[B_v[0], B_sg], writes=[B_nd])
        P.op("dve", lambda e: e.tensor_tensor(out=keb[0:R, :].rearrange("p (h d) -> p h d", h=NH), in0=ktok[0:R, :].rearrange("p (h d) -> p h d", h=NH),
                                              in1=sg[0:R, 5, :].unsqueeze(2).to_broadcast([R, NH, 64]), op=ALU.mult), reads=[B_kt, B_sg], writes=[B_keb])
        P.op("pe", lambda e: e.transpose(pM[4][0:8, 0:R], sg[0:R, 4, :], ident[0:R, 0:R]), reads=[B_sg, B_ident], writes=[B_pM[4]])
        P.op("act", lambda e: e.copy(d8[:, 0:R], pM[4][0:8, 0:R]), reads=[B_pM[4]], writes=[B_d8])
        P.op("dve", lambda e: e.tensor_tensor(out=R8[:, :, 0:R], in0=d8[:, 0:R].unsqueeze(1).to_broadcast([8, 4, R]), in1=pairsel[:, :].unsqueeze(2).to_broadcast([8, 4, R]), op=ALU.mult),
             reads=[B_d8, B_sel], writes=[B_d8])
        P.op("pe", lambda e: e.matmul(pM[4][:, 64:64 + 4 * R], lhsT=parsel[:, :], rhs=R8[:, :, 0:R].rearrange("p c t -> p (c t)"), start=True, stop=True), reads=[B_d8, B_sel], writes=[B_pM[4]])
        P.op("act", lambda e: e.copy(decT[:, :, 0:R], pM[4][:, 64:64 + 4 * R].rearrange("p (c t) -> p c t", c=4)), reads=[B_pM[4]], writes=[B_decT])
        P.op("dve", lambda e: e.tensor_tensor(out=qs_b[:, :, 0:R], in0=qT[:, :, 0:R], in1=decT[:, :, 0:R], op=ALU.mult), reads=[B_qT, B_decT], writes=[B_qs])
        P.op("dve", lambda e: e.tensor_tensor(out=qTm[:, :, 0:R, 0:R], in0=qs_b[:, :, 0:R].unsqueeze(2).to_broadcast([128, 4, R, R]),
                                              in1=oh16[:, 0:R, 0:R].unsqueeze(1).to_broadcast([128, 4, R, R]), op=ALU.mult), reads=[B_qs, B_oh], writes=[B_qTm])

        def ibank(par, hp):
            bi = par * 2 + (1 if hp == 3 else 0)
            return (pM[bi], B_pM[bi], 0 if hp == 3 else hp, bi)
        started = set()
        for t in range(R):
            ci = t % 2
            stC_v = stC[t].rearrange("(hp par) d e -> par d hp e", par=2)
            stn_v = stn[t].rearrange("(hp par) (d o) -> par d hp o", par=2, o=1)
            for par in range(2):
                P.dma("sp", lambda e, ci=ci, par=par, stC_v=stC_v: e.dma_start(out=c0f[ci][par * 64:(par + 1) * 64, :, 0:128], in_=stC_v[par]), writes=[B_c0f[ci]])
                P.dma("sp", lambda e, ci=ci, par=par, stn_v=stn_v: e.dma_start(out=c0f[ci][par * 64:(par + 1) * 64, :, 128:129], in_=stn_v[par], allow_slow_non_contiguous=True), writes=[B_c0f[ci]])
            P.op("pool", lambda e, ci=ci: e.tensor_copy(c0b[ci], c0f[ci][:]), reads=[B_c0f[ci]], writes=[B_c0b[ci]])
            for hd in range(NH):
                par, hp = hd % 2, hd // 2
                bank, Bb, sl, bi = ibank(par, hp)
                first = bi not in started
                started.add(bi)
                last = (t == R - 1) and (hp == 3 or hp == 2)
                P.op("pe", lambda e, bank=bank, sl=sl, par=par, hp=hp, t=t, ci=ci, first=first, last=last: e.matmul(
                    bank[0:R, sl * 129:(sl + 1) * 129], lhsT=qTm[par * 64:(par + 1) * 64, hp, t, 0:R], rhs=c0b[ci][par * 64:(par + 1) * 64, hp, :],
                    start=first, stop=last, skip_group_check=True), reads=[B_qTm, B_c0b[ci]], writes=[Bb])
            km = kmt[t % 2]; Bkm = B_kmt[t % 2]
            P.op("dve", lambda e, km=km, t=t: e.tensor_scalar(km[0:R, :], keb[0:R, :], ident[0:R, t:t + 1], None, ALU.mult), reads=[B_keb, B_ident], writes=[Bkm])
            pk0, Bk0 = pM[4], B_pM[4]
            pk1, Bk1 = next_pA()
            for hd in range(NH):
                par, hp = hd % 2, hd // 2
                pk, sl = (pk1, 0) if hp == 3 else (pk0, hp)
                Bk = Bk1 if hp == 3 else Bk0
                P.op("pe", lambda e, pk=pk, sl=sl, par=par, hd=hd, km=km: e.matmul(
                    pk[par * 64:(par + 1) * 64, sl * 129:(sl + 1) * 129], lhsT=km[0:R, hd * 64:(hd + 1) * 64], rhs=vaug[0:R, 0, hd, :], start=True, stop=True),
                    reads=[Bkm, B_v[0]], writes=[Bk])
            P.op("dve", lambda e, ci=ci, t=t: e.tensor_tensor(out=c0f[ci][:], in0=c0f[ci][:], in1=decT[:, :, t].unsqueeze(2).to_broadcast([128, 4, 129]), op=ALU.mult),
                 reads=[B_c0f[ci], B_decT, B_c0b[ci]], writes=[B_c0f[ci]])
            P.op("dve", lambda e, ci=ci, pk0=pk0: e.tensor_tensor(out=c0f[ci][:, 0:3, :], in0=c0f[ci][:, 0:3, :], in1=pk0[:, 0:387].rearrange("p (a b) -> p a b", a=3), op=ALU.add),
                 reads=[B_c0f[ci], Bk0], writes=[B_c0f[ci]])
            P.op("dve", lambda e, ci=ci, pk1=pk1: e.tensor_tensor(out=c0f[ci][:, 3, :], in0=c0f[ci][:, 3, :], in1=pk1[:, 0:129], op=ALU.add),
                 reads=[B_c0f[ci], Bk1], writes=[B_c0f[ci]])
            Cs_v = Cs[t].rearrange("(hp par) d e -> par d hp e", par=2)
            ns_v = ns_[t].rearrange("(hp par) (d o) -> par d hp o", par=2, o=1)
            for par in range(2):
                P.dma("pool", lambda e, ci=ci, par=par, Cs_v=Cs_v: e.dma_start(out=Cs_v[par], in_=c0f[ci][par * 64:(par + 1) * 64, :, 0:128]), reads=[B_c0f[ci]], writes=[B_out])
                P.dma("pool", lambda e, ci=ci, par=par, ns_v=ns_v: e.dma_start(out=ns_v[par], in_=c0f[ci][par * 64:(par + 1) * 64, :, 128:129], allow_slow_non_contiguous=True), reads=[B_c0f[ci]], writes=[B_out])
        for par in range(2):
            for (hps, bq) in [((0, 1, 2), 0), ((3,), 1)]:
                bank, Bb = pM[par * 2 + bq], B_pM[par * 2 + bq]
                for sl, hp in enumerate(hps):
                    hd = hp * 2 + par
                    P.op("dve", lambda e, bank=bank, sl=sl, hd=hd: e.tensor_tensor(out=nd[0:R, hd, :], in0=nd[0:R, hd, :], in1=bank[0:R, sl * 129:(sl + 1) * 129], op=ALU.add),
                         reads=[Bb, B_nd], writes=[B_nd])
        mlstm_post(R, so[0:R, 0, :], B_so[0], hmT[:, :, 0:R], B_hm[0], sg[0:R, 6, :], B_sg)
        w_out_apply(w_out_ml, tiles, hmT, B_hm)
        if stage >= 2:
            rmsnorm_T(h[0:R, 0, :], B_h[0], R, 1, xnT[:, :, 0:R], B_xnT[0])
            ffn(0, tiles, R)
        if stage >= 3:
            rmsnorm_T(h[0:R, 0, :], B_h[0], R, 2, xnT[:, :, 0:R], B_xnT[0])
            mla_mix(0, 0, tiles, sample=True)
        if stage >= 4:
            rmsnorm_T(h[0:R, 0, :], B_h[0], R, 3, xnT[:, :, 0:R], B_xnT[0])
            ffn(1, tiles, R)
            load_gfin()
            final_norm(h[0:R, 0, :], B_h[0], R)
        P.dma("pool", lambda e: e.dma_start(out=ys[:, :], in_=h[0:R, 0, :]), reads=[B_h[0]], writes=[B_out])

    assert 2 * GT <= 1024
    csf = mscr[0:64, 1024:1024 + 2 * GT].rearrange("p (a b) -> p a b", a=2); B_csf = Buf("csf")
    cst = sb("cst", [128, NT, 64]); B_cst = Buf("cst")
    lko = sb("lko", [128, 320]); B_lko = Buf("lko")
    rp4 = sb("rp4", [128, 4, 32]); B_rp4 = Buf("rp4")
    t64 = mscr[0:64, 0:1024].rearrange("p (a b) -> p a b", a=2); B_t64 = Buf("t64")
    orc = sb("orc", [128, 8]); B_orc = Buf("orc")

    def mla_mix(seq, g, tiles, sample=False):
        pos0 = 0 if sample else g * GT
        tok0 = 0 if sample else seq * S + pos0
        ntok = sum(r for _, r in tiles)
        lat_o, kr_o = (lats, krs) if sample else (latp, krp)
        if sample:
            P.dma("sp", lambda e: e.dma_start(out=cst[0:ntok, 0, :], in_=cs_tm[S:S + 1, :].partition_broadcast(ntok)), writes=[B_cst])
        else:
            P.dma("sp", lambda e: e.dma_start(out=cst[:], in_=cs_tm[pos0:pos0 + GT, :].rearrange("(t p) c -> p t c", p=128)), writes=[B_cst])
        for (c0, nc_) in [(0, 256), (256, 128), (384, 256), (640, 64)]:
            wv, Bw = wload(w_in_mla[:, c0:c0 + nc_], KC, nc_)
            proj_tm(wv, Bw, KC, nc_, xnT, B_xnT, tiles,
                    lambda pa, Bp, ti, rows, c0=c0, nc_=nc_: P.op("act", lambda e: e.copy(ckq[0:rows, ti, c0:c0 + nc_], pa), reads=[Bp], writes=[B_ckq[ti]] + ([B_c0f[0]] if sample else [])))
        for ti, (tc, rows) in enumerate(tiles):
            kb = (pos0 + tc) // 128
            rmsnorm_T(ckq[0:rows, ti, 0:QL], B_ckq[ti], rows, None, cqnT[:, :, tc:tc + rows], B_cqn[ti], width=QL, gap=gqfm_s[:, 0:3])
            ckv = ckq[0:rows, ti, QL:QL + KVL]
            P.op("dve", lambda e, ckv=ckv, rows=rows: e.scalar_tensor_tensor(out=junk[0:rows, 0:KVL], in0=ckv, scalar=1.0, in1=ckv, op0=ALU.mult, op1=ALU.mult,
                                                                     accum_out=st4[0:rows, 4:5]), reads=[B_ckq[ti]], writes=[B_junk, B_st4])
            P.op("dve", lambda e, rows=rows: e.tensor_scalar(st4[0:rows, 5:6], st4[0:rows, 4:5], 1.0 / KVL, EPS, ALU.mult, ALU.add), reads=[B_st4], writes=[B_st4])
            P.op("act", lambda e, rows=rows: e.activation(out=st4[0:rows, 6:7], in_=st4[0:rows, 5:6], func=AF.Ln), reads=[B_st4], writes=[B_st4])
            P.op("act", lambda e, rows=rows: e.activation(out=st4[0:rows, 7:8], in_=st4[0:rows, 6:7], func=AF.Exp, scale=-0.5), reads=[B_st4], writes=[B_st4])
            P.op("dve", lambda e, ckv=ckv, rows=rows: e.scalar_tensor_tensor(out=lko[0:rows, 0:KVL], in0=ckv, scalar=st4[0:rows, 7:8], in1=gkv_s[0:rows, :],
                                                                     op0=ALU.mult, op1=ALU.mult), reads=[B_ckq[ti], B_st4, B_vec], writes=[B_lko])
            x1 = ckq[0:rows, ti, 640:672]; x2 = ckq[0:rows, ti, 672:704]
            cos = cst[0:rows, ti, 0:32]; sin = cst[0:rows, ti, 32:64]
            P.op("dve", lambda e, x1=x1, cos=cos, rows=rows: e.tensor_tensor(out=rp4[0:rows, 0, :], in0=x1, in1=cos, op=ALU.mult), reads=[B_ckq[ti], B_cst], writes=[B_rp4])
            P.op("dve", lambda e, x2=x2, sin=sin, rows=rows: e.tensor_tensor(out=rp4[0:rows, 1, :], in0=x2, in1=sin, op=ALU.mult), reads=[B_ckq[ti], B_cst], writes=[B_rp4])
            P.op("dve", lambda e, x1=x1, sin=sin, rows=rows: e.tensor_tensor(out=rp4[0:rows, 2, :], in0=x1, in1=sin, op=ALU.mult), reads=[B_ckq[ti], B_cst], writes=[B_rp4])
            P.op("dve", lambda e, x2=x2, cos=cos, rows=rows: e.tensor_tensor(out=rp4[0:rows, 3, :], in0=x2, in1=cos, op=ALU.mult), reads=[B_ckq[ti], B_cst], writes=[B_rp4])
            P.op("dve", lambda e, rows=rows: e.tensor_tensor(out=lko[0:rows, 256:288], in0=rp4[0:rows, 0, :], in1=rp4[0:rows, 1, :], op=ALU.subtract), reads=[B_rp4, B_lko], writes=[B_lko])
            P.op("dve", lambda e, rows=rows: e.tensor_tensor(out=lko[0:rows, 288:320], in0=rp4[0:rows, 2, :], in1=rp4[0:rows, 3, :], op=ALU.add), reads=[B_rp4, B_lko], writes=[B_lko])
            P.dma("pool", lambda e, tc=tc, rows=rows: e.dma_start(out=lat_o[tok0 + tc:tok0 + tc + rows, :], in_=lko[0:rows, 0:KVL]), reads=[B_lko], writes=[B_out])
            P.dma("pool", lambda e, tc=tc, rows=rows: e.dma_start(out=kr_o[tok0 + tc:tok0 + tc + rows, :], in_=lko[0:rows, 256:320]), reads=[B_lko], writes=[B_out])
            P.op("act", lambda e, rows=rows: e.copy(xnb[0:rows, 0:KVL], lko[0:rows, 0:KVL]), reads=[B_lko], writes=[B_xnb])
            for k in range(2):
                P.op("pe", lambda e, k=k, rows=rows: e.transpose(pT[:, k * 128:k * 128 + rows], xnb[0:rows, k * 128:(k + 1) * 128], identb[0:rows, 0:rows]),
                     reads=[B_xnb, B_identb], writes=[B_pT])
            P.op("act", lambda e, tc=tc, rows=rows: e.copy(ckvT[:, :, tc:tc + rows], pT[:, 0:256].rearrange("p (k t) -> p k t", k=2)[:, :, 0:rows]), reads=[B_pT], writes=[B_ckvT[ti]])
            P.op("pe", lambda e, rows=rows: e.transpose(pM[4][0:64, 0:rows], lko[0:rows, 256:320], ident[0:rows, 0:rows]), reads=[B_lko, B_ident], writes=[B_pM[4]])
            if sample:
                P.op("act", lambda e, rows=rows: e.copy(krTn[0:64, 0:rows], pM[4][0:64, 0:rows]), reads=[B_pM[4]], writes=[B_krTn])
                P.op("act", lambda e, rows=rows: e.copy(cnew[0:rows, 2:2 + KVL], lko[0:rows, 0:KVL]), reads=[B_lko], writes=[B_cnew])
            else:
                P.op("act", lambda e, tc=tc, rows=rows: e.copy(krT[:, pos0 + tc:pos0 + tc + rows], pM[4][0:64, 0:rows]), reads=[B_pM[4]], writes=[B_kr])
        if sample:
            for t_ in range(ntok):
                P.dma("sp", lambda e, t_=t_: e.dma_start(out=csf[:, :, t_:t_ + 1], in_=cs_fm[0:64, :, S:S + 1], allow_slow_non_contiguous=True), writes=[B_csf] + B_ckq)
        else:
            P.dma("sp", lambda e: e.dma_start(out=csf[:], in_=cs_fm[0:64, :, pos0:pos0 + GT]), writes=[B_csf] + B_ckq)
        for c in range(0 if sample else 4):
            wv, Bw = wload(w_uk[:, c * 256:(c + 1) * 256], 2, 256)
            proj_fm(wv, Bw, 2, 256, ckvT, B_ckvT, ntok,
                    lambda pa, Bp, cb, t0, n, c=c: P.op("act", lambda e: e.copy(knT[:, 2 * c + cb, pos0 + t0:pos0 + t0 + n], pa), reads=[Bp], writes=[B_kn]))
        for c in range(0 if sample else 2):
            wv, Bw = wload(w_uv[:, c * 512:(c + 1) * 512], 2, 512)
            proj_tm(wv, Bw, 2, 512, ckvT, B_ckvT, tiles,
                    lambda pa, Bp, ti, rows, c=c: P.op("act", lambda e: e.copy(
                        Vh[0:rows, (pos0 + tiles[ti][0]) // 128, 4 * c:4 * c + 4, 0:128], pa.rearrange("p (a b) -> p a b", a=4)), reads=[Bp], writes=[B_Vh]))
        for hd in range(NH):
            wv, Bw = wload(w_uq[:, hd * 192:hd * 192 + 128], 3, 128)
            proj_fm(wv, Bw, 3, 128, cqnT, B_cqn, ntok,
                    lambda pa, Bp, cb, t0, n, hd=hd: P.op("act", lambda e: e.copy(qnT[:, hd, t0:t0 + n], pa), reads=[Bp], writes=[B_qn]))
        for hd in range(NH):
            wv, Bw = wload(w_uq[:, hd * 192 + 128:hd * 192 + 192], 3, 64)
            P.op("pool", lambda e, wv=wv: e.tensor_scalar(wrot[:, :, 0:32], wv[:, :, 32:64], -1.0, None, ALU.mult), reads=[Bw], writes=[B_wrot])
            P.op("pool", lambda e, wv=wv: e.tensor_copy(wrot[:, :, 32:64], wv[:, :, 0:32]), reads=[Bw], writes=[B_wrot])
            for t0 in range(0, ntok, 512):
                n = min(512, ntok - t0)
                px, Bpx = next_pA()
                for k in range(3):
                    P.op("pe", lambda e, k=k, px=px, wv=wv, t0=t0, n=n: e.matmul(px[0:64, 0:n], lhsT=wv[:, k, :], rhs=cqnT[:, k, t0:t0 + n], start=(k == 0), stop=(k == 2)),
                         reads=[Bw] + B_cqn, writes=[Bpx])
                pr, Bpr = next_pA()
                for k in range(3):
                    P.op("pe", lambda e, k=k, pr=pr, t0=t0, n=n: e.matmul(pr[0:64, 0:n], lhsT=wrot[:, k, :], rhs=cqnT[:, k, t0:t0 + n], start=(k == 0), stop=(k == 2)),
                         reads=[B_wrot] + B_cqn, writes=[Bpr])
                P.op("dve", lambda e, px=px, t0=t0, n=n: e.tensor_tensor(out=t64[:, 0, 0:n], in0=px[0:64, 0:n], in1=csf[:, 0, t0:t0 + n], op=ALU.mult), reads=[Bpx, B_csf], writes=[B_t64] + B_ckq)
                P.op("dve", lambda e, pr=pr, t0=t0, n=n: e.tensor_tensor(out=t64[:, 1, 0:n], in0=pr[0:64, 0:n], in1=csf[:, 1, t0:t0 + n], op=ALU.mult), reads=[Bpr, B_csf, B_t64], writes=[B_t64] + B_ckq)
                P.op("dve", lambda e, hd=hd, t0=t0, n=n: e.tensor_tensor(out=qrT[0:64, hd, t0:t0 + n], in0=t64[:, 0, 0:n], in1=t64[:, 1, 0:n], op=ALU.add), reads=[B_t64, B_csf] + B_ckq, writes=[B_qr])
        if sample:
            sample_attention(ntok)
            w_out_apply(w_o, tiles, hmT, B_hm)
            return
        accb = [(pM[0], B_pM[0], (0, 1, 2)), (pM[1], B_pM[1], (3, 4, 5)), (pM[2], B_pM[2], (6, 7))]
        pctr = 0
        for ti, (tc, rows) in enumerate(tiles):
            qi = (pos0 + tc) // 128
            for j in range(qi + 1):
                pb = pctr % 2; pctr += 1
                for half in range(2):
                    pa, Bp = next_pA()
                    for hh in range(4):
                        hd = half * 4 + hh
                        P.op("pe", lambda e, pa=pa, hh=hh, hd=hd, j=j, tc=tc: e.matmul(
                            pa[:, hh * 128:(hh + 1) * 128], lhsT=knT[:, hd, j * 128:(j + 1) * 128], rhs=qnT[:, hd, tc:tc + 128], start=True, stop=False),
                            reads=[B_kn, B_qn], writes=[Bp])
                        P.op("pe", lambda e, pa=pa, hh=hh, hd=hd, j=j, tc=tc: e.matmul(
                            pa[:, hh * 128:(hh + 1) * 128], lhsT=krT[0:64, j * 128:(j + 1) * 128], rhs=qrT[0:64, hd, tc:tc + 128], start=False, stop=True),
                            reads=[B_kr, B_qr], writes=[Bp])
                    P.op("act", lambda e, pa=pa, pb=pb, half=half: e.activation(
                        out=PTb[:, pb, half * 4:(half + 1) * 4, :], in_=pa[:, :].rearrange("p (a b) -> p a b", a=4), func=AF.Exp, scale=MLA_SCALE),
                        reads=[Bp], writes=[B_PT[pb]])
                if j == qi:
                    P.op("pool", lambda e, pb=pb: e.tensor_tensor(out=PTb[:, pb], in0=PTb[:, pb], in1=maskTb[:, :].unsqueeze(1).to_broadcast([128, NH, 128]), op=ALU.mult),
                         reads=[B_PT[pb], B_maskT], writes=[B_PT[pb]])
                for (pm, Bm, hds) in accb:
                    for si, hd in enumerate(hds):
                        P.op("pe", lambda e, pm=pm, si=si, hd=hd, pb=pb, j=j, first=(j == 0 and si == 0), last=(j == qi and si == len(hds) - 1): e.matmul(
                            pm[:, si * 129:(si + 1) * 129], lhsT=PTb[:, pb, hd, :], rhs=Vh[:, j, hd, :], start=first, stop=last, skip_group_check=True),
                            reads=[B_PT[pb], B_Vh], writes=[Bm])
            for (pm, Bm, hds) in accb:
                nh_ = len(hds); h0 = hds[0]
                pv = pm[:, 0:nh_ * 129].rearrange("p (a b) -> p a b", a=nh_)
                P.op("dve", lambda e, pv=pv, h0=h0, nh_=nh_: e.reciprocal(orc[:, h0:h0 + nh_], pv[:, :, 128]), reads=[Bm], writes=[B_orc])
                P.op("dve", lambda e, pv=pv, h0=h0, nh_=nh_: e.tensor_tensor(
                    out=attb[:, h0 * 128:(h0 + nh_) * 128].rearrange("p (a b) -> p a b", a=nh_), in0=pv[:, :, 0:128],
                    in1=orc[:, h0:h0 + nh_].unsqueeze(2).to_broadcast([128, nh_, 128]), op=ALU.mult), reads=[Bm, B_orc], writes=[B_att])
            for k in range(KC):
                P.op("pe", lambda e, k=k: e.transpose(pT[:, k * 128:(k + 1) * 128], attb[:, k * 128:(k + 1) * 128], identb[:, :]), reads=[B_att, B_identb], writes=[B_pT])
            P.op("act", lambda e, tc=tc: e.copy(hmT[:, :, tc:tc + 128], pT[:, :].rearrange("p (k t) -> p k t", k=KC)), reads=[B_pT], writes=[B_hm[ti]])
        w_out_apply(w_o, tiles, hmT, B_hm)

    B_out = Buf("out")
    if cfg.get("do_sample", True):
        sample_group()
    for seq in range(NSP if cfg.get("do_prompt", True) else 0):
        for g in range(NG):
            tok0 = seq * S + g * GT
            tiles = [(t * 128, 128) for t in range(NT)]
            for t in range(NT):
                P.dma("sp", lambda e, t=t, tok0=tok0: e.dma_start(out=h[:, t, :], in_=xp[tok0 + t * 128: tok0 + (t + 1) * 128, :]), writes=[B_h[t]])
            upto = cfg.get("upto", 99)
            for t in range(NT):
                if upto >= 1:
                    rmsnorm_T(h[:, t, :], B_h[t], 128, 0, xnT[:, :, t * 128:(t + 1) * 128], B_xnT[t])
            if upto >= 2:
                mlstm_project(tiles, GT)
            if upto >= 3:
                mlstm_gate_rows(tiles)
            if upto < 3:
                pass
            elif g == 0:
                P.op("pool", lambda e: e.memset(Bext[:, 0:1], 0.0), writes=[B_scan])
                P.op("pool", lambda e: e.memset(mext[:, 0:1], 0.0), writes=[B_scan])
                P.op("pool", lambda e: e.memset(mu[:, :], 0.0), writes=[B_mu])
                P.op("pool", lambda e: e.memset(Chat[:], 0.0), writes=[B_Ch])
            else:
                P.op("dve", lambda e: e.tensor_copy(Bext[:, 0:1], Bext[:, GT:GT + 1]), reads=[B_scan], writes=[B_scan])
                P.op("dve", lambda e: e.tensor_copy(mext[:, 0:1], mext[:, GT:GT + 1]), reads=[B_scan], writes=[B_scan])
            if upto >= 3:
              P.op("dve", lambda e: e.tensor_tensor_scan(out=Bext[:, 1:GT + 1], data0=frow[:, 0:GT], data1=zeros8[:, 0:GT], initial=Bext[:, 0:1],
                                                       op0=ALU.add, op1=ALU.add), reads=[B_rows, B_z8, B_scan], writes=[B_scan])
            if upto >= 3:
              P.op("dve", lambda e: e.tensor_tensor_scan(out=mext[:, 1:GT + 1], data0=frow[:, 0:GT], data1=irow[:, 0:GT], initial=mext[:, 0:1],
                                                       op0=ALU.add, op1=ALU.max), reads=[B_rows, B_scan], writes=[B_scan])
            for t in range(NT if upto >= 4 else 0):
                mlstm_chunk(128, t * 128, vaug[:, t], B_v[t], so[:, t, :], B_so[t], hmT[:, :, t * 128:(t + 1) * 128], B_hm[t],
                            Bext[:, t * 128:t * 128 + 1], Bext[:, t * 128 + 1:(t + 1) * 128 + 1], irow[:, t * 128:(t + 1) * 128], maskT[:, :], g == 0 and t == 0)
            if upto >= 5:
                w_out_apply(w_out_ml, tiles, hmT, B_hm)
            if g == NG - 1 and upto >= 4:
                P.op("dve", lambda e: e.tensor_tensor(out=rs8[:, 8:9], in0=mu[:, 0:1], in1=mext[:, GT:GT + 1], op=ALU.subtract), reads=[B_mu, B_scan, B_rs8], writes=[B_rs8])
                P.op("act", lambda e: e.activation(out=rs8[:, 9:10], in_=rs8[:, 8:9], func=AF.Exp), reads=[B_rs8], writes=[B_rs8])
                P.op("dve", lambda e: e.tensor_scalar(dg8[:, :], pairsel[:, :], rs8[:, 9:10], None, ALU.mult), reads=[B_rs8, B_sel, B_dg8], writes=[B_dg8])
                P.op("pe", lambda e: e.matmul(pM[0][:, 16:20], lhsT=parsel[:, :], rhs=dg8[:, :], start=True, stop=True), reads=[B_dg8, B_sel], writes=[B_pM[0]])
                P.op("act", lambda e: e.copy(decb[:, :], pM[0][:, 16:20]), reads=[B_pM[0]], writes=[B_decb])
                P.op("dve", lambda e: e.tensor_tensor(out=nd[:, 0:4, :], in0=Chat[:], in1=decb[:, :].unsqueeze(2).to_broadcast([128, 4, 129]), op=ALU.mult),
                     reads=[B_Ch, B_decb, B_nd], writes=[B_nd])
                for hp in range(4):
                    P.dma("pool", lambda e, hp=hp, seq=seq: e.dma_start(out=Cp[seq, 2 * hp:2 * hp + 2, :, :].rearrange("a d e -> (a d) e"), in_=nd[:, hp, 0:128]),
                          reads=[B_nd], writes=[B_out])
                    P.dma("pool", lambda e, hp=hp, seq=seq: e.dma_start(out=np_[seq, 2 * hp:2 * hp + 2, :].rearrange("a (d o) -> (a d) o", o=1), in_=nd[:, hp, 128:129]),
                          reads=[B_nd], writes=[B_out])
                P.dma("pool", lambda e, seq=seq: e.dma_start(out=mp[seq:seq + 1, :].rearrange("o h -> h o"), in_=mext[:, GT:GT + 1]), reads=[B_scan], writes=[B_out])
            if stage >= 2:
                for t in range(NT):
                    rmsnorm_T(h[:, t, :], B_h[t], 128, 1, xnT[:, :, t * 128:(t + 1) * 128], B_xnT[t])
                ffn(0, tiles, GT)
            if stage >= 3:
                for t in range(NT):
                    rmsnorm_T(h[:, t, :], B_h[t], 128, 2, xnT[:, :, t * 128:(t + 1) * 128], B_xnT[t])
                mla_mix(seq, g, tiles)
            if stage >= 4:
                for t in range(NT):
                    rmsnorm_T(h[:, t, :], B_h[t], 128, 3, xnT[:, :, t * 128:(t + 1) * 128], B_xnT[t])
                ffn(1, tiles, GT)
                load_gfin()
                for t in range(NT):
                    final_norm(h[:, t, :], B_h[t], 128)
            for t in range(NT):
                P.dma("pool", lambda e, t=t, tok0=tok0: e.dma_start(out=yp[tok0 + t * 128: tok0 + (t + 1) * 128, :], in_=h[:, t, :]), reads=[B_h[t]], writes=[B_out])

    P.emit()
    return nc, es


def host_consts(S, past_len):
    ident = np.eye(128, dtype=np.float32)
    maskT = np.triu(np.ones((128, 128), np.float32))
    parsel = np.zeros((8, 128), np.float32)
    for k in range(8):
        parsel[k, (k % 2) * 64:(k % 2) * 64 + 64] = 1.0
    pairsel = np.zeros((8, 4), np.float32)
    for k in range(8):
        pairsel[k, k // 2] = 1.0
    inv = (10000.0 ** (-np.arange(0, 64, 2, dtype=np.float32) / np.float32(64))).astype(np.float32)
    pos = np.concatenate([np.arange(S, dtype=np.float32), np.array([past_len], np.float32)])
    ang = (pos[:, None] * inv[None, :]).astype(np.float32)
    cs_tm = np.concatenate([np.cos(ang), np.sin(ang)], axis=1).astype(np.float32)
    cs_fm = np.zeros((128, 2, S + 1), np.float32)
    for p in range(128):
        cs_fm[p, 0] = np.cos(ang[:, p % 32])
        cs_fm[p, 1] = np.sin(ang[:, p % 32])
    oh16 = np.ascontiguousarray(np.broadcast_to(np.eye(16, dtype=np.float32)[None], (128, 16, 16)))
    return dict(oh16=oh16, ident=ident, maskT=maskT, parsel=parsel, pairsel=pairsel, cs_tm=cs_tm, cs_fm=cs_fm)


def make_in_maps(inputs, cfg, ncores):
    S, NSP, NS, NPG, NPHYS = cfg["S"], cfg["NSP"], cfg["NS"], cfg["NPG"], cfg["NPHYS"]
    f = lambda a: np.ascontiguousarray(np.asarray(a))
    consts = host_consts(S, NPG * 128)
    gfm = np.stack([f(inputs["norm_mix"])[0], f(inputs["norm_ffn"])[0], f(inputs["norm_mix"])[1], f(inputs["norm_ffn"])[1],
                    f(inputs["norm_final"])], axis=0)
    gfm = np.ascontiguousarray(gfm.reshape(5, KC, 128).transpose(2, 0, 1))
    shared = dict(
        gfm=gfm, gqfm=np.ascontiguousarray(f(inputs["mla_g_q"])[0].reshape(3, 128).T), g_fin=f(inputs["norm_final"]).reshape(1, D), w_in_ml=f(inputs["mlstm_w_in"])[0], b_gates=f(inputs["mlstm_b_gates"])[0].reshape(1, 16),
        g_head=f(inputs["mlstm_g_head"])[0].reshape(1, D), w_out_ml=f(inputs["mlstm_w_out"])[0],
        w_in_mla=f(inputs["mla_w_in"])[0], g_q=f(inputs["mla_g_q"])[0].reshape(1, QL), g_kv=f(inputs["mla_g_kv"])[0].reshape(1, KVL),
        w_uq=f(inputs["mla_w_uq"])[0], w_uk=f(inputs["mla_w_uk"])[0], w_uv=f(inputs["mla_w_uv"])[0], w_o=f(inputs["mla_w_o"])[0],
        w_gu=f(inputs["ffn_w_gate_up"]), w_dn=f(inputs["ffn_w_down"]), **consts)
    xp_all = f(inputs["x_prompt"]); xs_all = f(inputs["x_sample"])
    if cfg.get("use_cache", False):
        latkr_all = np.concatenate([f(inputs["cache_latent"])[0].reshape(NPHYS * 128, KVL), f(inputs["cache_k_rope"])[0].reshape(NPHYS * 128, RP)], axis=1)
    maps = []
    for c in range(ncores):
        m = dict(shared)
        m["xp"] = xp_all[c * NSP:(c + 1) * NSP].reshape(NSP * S, D)
        m["xs"] = xs_all[c * NS:(c + 1) * NS].reshape(NS, D)
        m["stC"] = f(inputs["state_mlstm_C"])[0, c * NS:(c + 1) * NS]
        m["stn"] = f(inputs["state_mlstm_n"])[0, c * NS:(c + 1) * NS]
        m["stm"] = f(inputs["state_mlstm_m"])[0, c * NS:(c + 1) * NS]
        if cfg.get("use_cache", False):
            m["latkr"] = latkr_all
            m["ptb"] = f(inputs["page_table"])[c * NS:(c + 1) * NS].reshape(1, NS * NPG).astype(np.int32)
        maps.append(m)
    return maps


def gather_outputs(res, cfg, ncores):
    S, NSP, NS = cfg["S"], cfg["NSP"], cfg["NS"]
    R = res.results
    cat = lambda k: np.concatenate([R[c][k] for c in range(ncores)], axis=0)
    y_p = cat("yp").reshape(ncores * NSP, S, D)
    y_s = cat("ys").reshape(ncores * NS, 1, D)
    return (y_p, y_s, cat("Cp")[None], cat("np")[None], cat("mp")[None], cat("Cs")[None], cat("ns")[None], cat("ms")[None],
            cat("latp").reshape(1, ncores * NSP, S, KVL), cat("krp").reshape(1, ncores * NSP, S, RP),
            cat("lats").reshape(1, ncores * NS, 1, KVL), cat("krs").reshape(1, ncores * NS, 1, RP))


FULL_CFG = dict(S=2048, NSP=2, GT=512, NS=16, NPG=128, NPHYS=20480, use_cache=True)


def kernel(**inputs):
    cfg = dict(FULL_CFG)
    ncores = 8
    nc, es = build(cfg)
    maps = make_in_maps(inputs, cfg, ncores)
    res = run_bass_kernel_spmd(nc, maps, core_ids=list(range(ncores)))
    return gather_outputs(res, cfg, ncores)
```

```python
import numpy as np
from contextlib import ExitStack
import concourse.bass as bass
import concourse.mybir as mybir
from concourse.bass_utils import run_bass_kernel_spmd

F32, BF16, I32 = mybir.dt.float32, mybir.dt.bfloat16, mybir.dt.int32
ALU = mybir.AluOpType
AF = mybir.ActivationFunctionType
AX = mybir.AxisListType

D = 1024
KC = 8
NH = 8
DQK = 64
DV = 128
ML_IN = 3088
DFF = 2816
EPS = 1e-6
QL, KVL, RP = 384, 256, 64
MLA_SCALE = float((128 + 64) ** -0.5)


class Buf:
    __slots__ = ("name", "writer", "readers")

    def __init__(self, name):
        self.name = name
        self.writer = None
        self.readers = []


class Prog:
    ENG = ["pe", "act", "dve", "pool", "sp"]
    EMAP = {"pe": "tensor", "act": "scalar", "dve": "vector", "pool": "gpsimd", "sp": "sync"}

    def __init__(self, nc, es, ndma=14):
        self.nc = nc
        self.ops = {e: [] for e in self.ENG}
        self.cnt = {e: 0 for e in ["pe", "act", "dve", "pool"]}
        self.semh = {}
        for e in ["pe", "act", "dve", "pool"]:
            self.semh["c_" + e] = es.enter_context(nc.semaphore("c_" + e))
        self.ndma = ndma
        self.dma_cnt = {}
        self.dma_rr = {"sp": 0, "pool": 0}
        for q in ["sp", "pool"]:
            for i in range(ndma):
                k = f"d_{q}{i}"
                self.semh[k] = es.enter_context(nc.semaphore(k))
                self.dma_cnt[k] = 0
        self.seen = {e: {} for e in self.ENG}

    def _deps(self, eng, reads, writes):
        need = {}

        def add(tok):
            if tok is None:
                return
            k, v = tok
            if need.get(k, 0) < v:
                need[k] = v

        for b in reads:
            add(b.writer)
        for b in writes:
            add(b.writer)
            for t in b.readers:
                add(t)
        waits = []
        for k, v in need.items():
            if k == "c_pe" and eng == "pe":
                continue
            if self.seen[eng].get(k, 0) < v:
                self.seen[eng][k] = v
                waits.append((k, v))
        return waits

    def _upd(self, tok, reads, writes):
        for b in reads:
            b.readers.append(tok)
        for b in writes:
            b.writer = tok
            b.readers = []

    def op(self, eng, fn, reads=(), writes=()):
        waits = self._deps(eng, reads, writes)
        self.cnt[eng] += 1
        tok = ("c_" + eng, self.cnt[eng])
        self.ops[eng].append((waits, fn, ("c_" + eng, 1)))
        self._upd(tok, reads, writes)

    def dma(self, q, fn, reads=(), writes=()):
        i = self.dma_rr[q]
        self.dma_rr[q] = (i + 1) % self.ndma
        key = f"d_{q}{i}"
        waits = self._deps(q, reads, writes)
        prev = self.dma_cnt[key]
        if prev > 0 and self.seen[q].get(key, 0) < prev:
            self.seen[q][key] = prev
            waits.append((key, prev))
        self.dma_cnt[key] += 16
        tok = (key, self.dma_cnt[key])
        self.ops[q].append((waits, fn, (key, 16)))
        self._upd(tok, reads, writes)

    def emit(self):
        nc = self.nc
        with nc.Block() as block:
            for e in self.ENG:
                def body(eng, e=e):
                    for waits, fn, inc in self.ops[e]:
                        for k, v in waits:
                            eng.wait_ge(self.semh[k], v)
                        ins = fn(eng)
                        ins.then_inc(self.semh[inc[0]], inc[1])
                    if e in ("sp", "pool"):
                        for k, c in self.dma_cnt.items():
                            if k.startswith(f"d_{e}") and c > 0:
                                eng.wait_ge(self.semh[k], c)
                getattr(block, self.EMAP[e])(body)


def build(cfg):
    S = cfg["S"]
    NSP = cfg["NSP"]
    GT = cfg["GT"]
    NS = cfg["NS"]
    NPG = cfg["NPG"]
    NPHYS = cfg["NPHYS"]
    stage = cfg.get("stage", 99)
    NT = GT // 128
    NG = S // GT
    SLAB = min(512, GT)

    nc = bass.Bass("TRN2", target_bir_lowering=False)
    es = ExitStack()
    P = Prog(nc, es)

    def din(name, shape, dt=F32):
        return nc.dram_tensor(name, list(shape), dt, kind="ExternalInput").ap()

    def dout(name, shape, dt=F32):
        return nc.dram_tensor(name, list(shape), dt, kind="ExternalOutput").ap()

    xp = din("xp", [NSP * S, D])
    xs = din("xs", [NS, D])
    stC = din("stC", [NS, NH, DQK, DV])
    stn = din("stn", [NS, NH, DQK])
    stm = din("stm", [NS, NH])
    USE_CACHE = cfg.get("use_cache", False)
    if USE_CACHE:
        latkr = din("latkr", [NPHYS * 128, KVL + RP])
        ptb = din("ptb", [1, NS * NPG], I32)
    gfm = din("gfm", [128, 5, KC])
    w_in_ml = din("w_in_ml", [D, ML_IN])
    b_gates = din("b_gates", [1, 16])
    g_head = din("g_head", [1, D])
    w_out_ml = din("w_out_ml", [D, D])
    w_in_mla = din("w_in_mla", [D, QL + KVL + RP])
    g_q = din("g_q", [1, QL])
    g_kv = din("g_kv", [1, KVL])
    w_uq = din("w_uq", [QL, 8 * 192])
    w_uk = din("w_uk", [KVL, 1024])
    w_uv = din("w_uv", [KVL, 1024])
    w_o = din("w_o", [D, D])
    w_gu = din("w_gu", [2, D, 2 * DFF])
    w_dn = din("w_dn", [2, DFF, D])
    ident_d = din("ident", [128, 128])
    maskT_d = din("maskT", [128, 128])
    parsel_d = din("parsel", [8, 128])
    pairsel_d = din("pairsel", [8, 4])
    gqfm = din("gqfm", [128, 3])
    g_fin = din("g_fin", [1, D])
    oh16_d = din("oh16", [128, NS, NS])
    cs_tm = din("cs_tm", [S + 1, 64])
    cs_fm = din("cs_fm", [128, 2, S + 1])

    yp = dout("yp", [NSP * S, D])
    ys = dout("ys", [NS, D])
    Cp = dout("Cp", [NSP, NH, DQK, DV])
    np_ = dout("np", [NSP, NH, DQK])
    mp = dout("mp", [NSP, NH])
    Cs = dout("Cs", [NS, NH, DQK, DV])
    ns_ = dout("ns", [NS, NH, DQK])
    ms_ = dout("ms", [NS, NH])
    latp = dout("latp", [NSP * S, KVL])
    krp = dout("krp", [NSP * S, RP])
    lats = dout("lats", [NS, KVL])
    krs = dout("krs", [NS, RP])

    def sb(name, shape, dt=F32):
        return es.enter_context(nc.sbuf_tensor("s_" + name, list(shape), dt))

    def ps(name, shape, dt=F32):
        return es.enter_context(nc.psum_tensor("p_" + name, list(shape), dt))

    ident = sb("ident", [128, 128]); B_ident = Buf("ident")
    identb = sb("identb", [128, 128], BF16); B_identb = Buf("identb")
    maskT = sb("maskT", [128, 128]); B_maskT = Buf("maskT")
    maskTb = sb("maskTb", [128, 128], BF16)
    parsel = sb("parsel", [8, 128]); pairsel = sb("pairsel", [8, 4]); B_sel = Buf("sel")
    gfm_s = sb("gfm_s", [128, 5, KC]); B_gfm = Buf("gfm")
    bg_s = sb("bg_s", [128, 16]); ghead_s = sb("ghead_s", [128, D]); B_vec = Buf("vec")
    gkv_s = sb("gkv_s", [128, KVL])
    zeros8 = sb("zeros8", [8, GT if GT > 128 else 128]); B_z8 = Buf("z8")
    ones1 = sb("ones1", [128, 1], BF16)

    P.dma("sp", lambda e: e.dma_start(out=ident[:], in_=ident_d[:, :]), writes=[B_ident])
    P.dma("sp", lambda e: e.dma_start(out=maskT[:], in_=maskT_d[:, :]), writes=[B_maskT])
    P.dma("sp", lambda e: e.dma_start(out=parsel[:], in_=parsel_d[:, :]), writes=[B_sel])
    P.dma("sp", lambda e: e.dma_start(out=pairsel[:], in_=pairsel_d[:, :]), writes=[B_sel])
    P.dma("sp", lambda e: e.dma_start(out=gfm_s[:], in_=gfm[:, :, :]), writes=[B_gfm])
    P.dma("sp", lambda e: e.dma_start(out=bg_s[:], in_=b_gates[0:1, :].partition_broadcast(128)), writes=[B_vec])
    B_gh = Buf("gh")
    P.dma("sp", lambda e: e.dma_start(out=gkv_s[:], in_=g_kv[0:1, :].partition_broadcast(128)), writes=[B_vec])
    P.op("pool", lambda e: e.tensor_copy(identb[:], ident[:]), reads=[B_ident], writes=[B_identb])
    P.op("pool", lambda e: e.tensor_copy(maskTb[:], maskT[:]), reads=[B_maskT], writes=[B_maskT])
    P.op("pool", lambda e: e.memset(zeros8[:], 0.0), writes=[B_z8])
    P.op("pool", lambda e: e.memset(ones1[:], 1.0), writes=[B_z8])

    h = sb("h", [128, NT, D])
    B_h = [Buf(f"h{t}") for t in range(NT)]
    xnT = sb("xnT", [128, KC, GT], BF16)
    B_xnT = [Buf(f"xnT{t}") for t in range(NT)]

    WCH = 2048
    NSTG, NWB = 2, 2
    stg = [sb(f"stg{i}", [128, WCH]) for i in range(NSTG)]
    B_stg = [Buf(f"stg{i}") for i in range(NSTG)]
    wbuf = [sb(f"wb{i}", [128, WCH], BF16) for i in range(NWB)]
    B_wb = [Buf(f"wb{i}") for i in range(NWB)]
    wctr = [0, 0]

    def wload(dram2d, kc, ncols):
        assert kc * ncols <= WCH
        si = wctr[0] % NSTG; wctr[0] += 1
        wi = wctr[1] % NWB; wctr[1] += 1
        sv = stg[si][:, 0:kc * ncols].rearrange("p (k n) -> p k n", k=kc)
        wv = wbuf[wi][:, 0:kc * ncols].rearrange("p (k n) -> p k n", k=kc)
        src = dram2d.rearrange("(k p) n -> p k n", p=128)
        P.dma("sp", lambda e: e.dma_start(out=sv, in_=src), writes=[B_stg[si]])
        P.op("pool", lambda e: e.tensor_copy(wv, sv), reads=[B_stg[si]], writes=[B_wb[wi]])
        return wv, B_wb[wi]

    pA = [ps(f"pA{i}", [128, 512]) for i in range(2)]; B_pA = [Buf(f"pA{i}") for i in range(2)]
    pT = ps("pT", [128, 1024], BF16); B_pT = Buf("pT")
    pM = [ps(f"pM{i}", [128, 512]) for i in range(5)]; B_pM = [Buf(f"pM{i}") for i in range(5)]
    pT2 = pM[4][:, :].bitcast(BF16)
    pactr = [0]

    def next_pA():
        i = pactr[0] % 2; pactr[0] += 1
        return pA[i], B_pA[i]

    xnb = sb("xnb", [128, D], BF16); B_xnb = Buf("xnb")
    junk = xnb; B_junk = B_xnb
    st4 = sb("st4", [128, 8]); B_st4 = Buf("st4")

    def rmsnorm_T(hap, Bh, rows, gi, dstT, Bdst, width=D, gap=None):
        nk = width // 128
        P.op("dve", lambda e: e.scalar_tensor_tensor(out=junk[0:rows, 0:width], in0=hap, scalar=1.0, in1=hap,
                                                     op0=ALU.mult, op1=ALU.mult, accum_out=st4[0:rows, 0:1]),
             reads=[Bh], writes=[B_junk, B_st4])
        P.op("dve", lambda e: e.tensor_scalar(st4[0:rows, 1:2], st4[0:rows, 0:1], 1.0 / width, EPS, ALU.mult, ALU.add),
             reads=[B_st4], writes=[B_st4])
        P.op("act", lambda e: e.activation(out=st4[0:rows, 2:3], in_=st4[0:rows, 1:2], func=AF.Ln), reads=[B_st4], writes=[B_st4])
        P.op("act", lambda e: e.activation(out=st4[0:rows, 3:4], in_=st4[0:rows, 2:3], func=AF.Exp, scale=-0.5), reads=[B_st4], writes=[B_st4])
        P.op("dve", lambda e: e.tensor_scalar(xnb[0:rows, 0:width], hap, st4[0:rows, 3:4], None, ALU.mult),
             reads=[Bh, B_st4], writes=[B_xnb])
        for k in range(nk):
            P.op("pe", lambda e, k=k: e.transpose(pT[:, k * 128:k * 128 + rows], xnb[0:rows, k * 128:(k + 1) * 128], identb[0:rows, 0:rows]),
                 reads=[B_xnb, B_identb], writes=[B_pT])
        src = pT[:, 0:nk * 128].rearrange("p (k t) -> p k t", k=nk)[:, :, 0:rows]
        if gap is None:
            gap_ = gfm_s[:, gi, 0:nk]
        else:
            gap_ = gap
        P.op("dve", lambda e: e.tensor_tensor(out=dstT, in0=src, in1=gap_.unsqueeze(2).to_broadcast([128, nk, rows]), op=ALU.mult),
             reads=[B_pT, B_gfm], writes=[Bdst])

    def final_norm(hap, Bh, rows):
        P.op("dve", lambda e: e.scalar_tensor_tensor(out=junk[0:rows, :], in0=hap, scalar=1.0, in1=hap, op0=ALU.mult, op1=ALU.mult,
                                                     accum_out=st4[0:rows, 0:1]), reads=[Bh], writes=[B_junk, B_st4])
        P.op("dve", lambda e: e.tensor_scalar(st4[0:rows, 1:2], st4[0:rows, 0:1], 1.0 / D, EPS, ALU.mult, ALU.add), reads=[B_st4], writes=[B_st4])
        P.op("act", lambda e: e.activation(out=st4[0:rows, 2:3], in_=st4[0:rows, 1:2], func=AF.Ln), reads=[B_st4], writes=[B_st4])
        P.op("act", lambda e: e.activation(out=st4[0:rows, 3:4], in_=st4[0:rows, 2:3], func=AF.Exp, scale=-0.5), reads=[B_st4], writes=[B_st4])
        P.op("dve", lambda e: e.scalar_tensor_tensor(out=hap, in0=hap, scalar=st4[0:rows, 3:4], in1=gfin_s[0:rows, :], op0=ALU.mult, op1=ALU.mult),
             reads=[Bh, B_st4, B_gh], writes=[Bh])

    def load_gfin():
        P.dma("sp", lambda e: e.dma_start(out=gfin_s[:], in_=g_fin[0:1, :].partition_broadcast(128)), writes=[B_gh])

    def proj_fm(wv, Bw, kc, ncols, xT, Bx_list, ntok, evac):
        for cb in range((ncols + 127) // 128):
            m = min(128, ncols - cb * 128)
            for t0 in range(0, ntok, 512):
                n = min(512, ntok - t0)
                pa, Bp = next_pA()
                for k in range(kc):
                    P.op("pe", lambda e, k=k, pa=pa, cb=cb, m=m, t0=t0, n=n: e.matmul(
                        pa[0:m, 0:n], lhsT=wv[:, k, cb * 128:cb * 128 + m], rhs=xT[:, k, t0:t0 + n],
                        start=(k == 0), stop=(k == kc - 1)), reads=[Bw] + Bx_list, writes=[Bp])
                evac(pa[0:m, 0:n], Bp, cb, t0, n)

    def proj_tm(wv, Bw, kc, ncols, xT, Bx_list, tiles, evac):
        for ti, (c0, rows) in enumerate(tiles):
            pa, Bp = next_pA()
            for k in range(kc):
                P.op("pe", lambda e, k=k, pa=pa, c0=c0, rows=rows: e.matmul(
                    pa[0:rows, 0:ncols], lhsT=xT[:, k, c0:c0 + rows], rhs=wv[:, k, 0:ncols],
                    start=(k == 0), stop=(k == kc - 1)), reads=[Bw] + Bx_list, writes=[Bp])
            evac(pa[0:rows, 0:ncols], Bp, ti, rows)

    NF_ = DFF // 128
    ASZ = max(NF_ * GT, 8 * GT + NT * NH * 129 + NT * D, 21 * GT + 2 * NH * 128 + D + 192)
    arena = sb("arena", [128, ASZ], BF16)
    qT = arena[:, 0:4 * GT].rearrange("p (c t) -> p c t", c=4); B_qT = Buf("qT")
    kT = arena[:, 4 * GT:8 * GT].rearrange("p (c t) -> p c t", c=4); B_kT = Buf("kT")
    o0 = 8 * GT
    vaug = arena[:, o0:o0 + NT * NH * 129].rearrange("p (t h e) -> p t h e", t=NT, h=NH); B_v = [Buf(f"v{t}") for t in range(NT)]
    o1 = o0 + NT * NH * 129
    so = arena[:, o1:o1 + NT * D].rearrange("p (t d) -> p t d", t=NT); B_so = [Buf(f"so{t}") for t in range(NT)]
    gates = sb("gates", [128, NT, 16]); B_g = Buf("gates")
    gtmp = sb("gtmp", [128, NT, 8, 4]); B_gt = Buf("gtmp")
    logf = sb("logf", [128, NT, 8])
    irow = sb("irow", [8, GT]); frow = sb("frow", [8, GT]); B_rows = Buf("rows")
    Bext = sb("Bext", [8, GT + 1]); mext = sb("mext", [8, GT + 1]); B_scan = Buf("scan")
    mu = sb("mu", [8, 1]); B_mu = Buf("mu")
    rs8 = sb("rs8", [8, 16]); B_rs8 = Buf("rs8")
    zrow = sb("zrow", [8, 128]); erow = sb("erow", [8, 128]); trow = sb("trow", [8, 128]); B_zr = Buf("zr")
    dg8 = sb("dg8", [8, 4]); B_dg8 = Buf("dg8")
    et = sb("et", [128, 16]); B_et = Buf("et")
    decb = sb("decb", [128, 4]); B_decb = Buf("decb")
    Chat = sb("Chat", [128, 4, 129]); B_Ch = Buf("Chat")
    Csb = sb("Csb", [128, 4, 129], BF16); B_Cs = Buf("Csb")
    vp = sb("vp", [128, NH, 129], BF16); B_vp = Buf("vp")
    ktok = sb("ktok", [128, 512], BF16); B_kt = Buf("ktok")
    STb = sb("STb", [128, 8, 128], BF16); B_ST = [Buf("ST0"), Buf("ST1")]
    mscr = sb("mscr", [128, max(NT * 704, 2816)])
    nd = mscr[:, 0:NH * 129].rearrange("p (a b) -> p a b", a=NH); B_nd = Buf("nd")
    sqs = mscr[:, NH * 129:NH * 129 + NH * 128].rearrange("p (a b) -> p a b", a=NH); B_sq = Buf("sqs")
    p8 = sb("p8", [128, 8, 8]); B_p8 = Buf("p8")
    ogb = junk; B_og = B_junk
    hmT = sb("hmT", [128, KC, GT], BF16); B_hm = [Buf(f"hm{t}") for t in range(NT)]

    def mlstm_post(R, soap, Bso, hmT_dst, Bhm, thr_ap, Bthr):
        den = nd[0:R, :, 128]
        P.op("dve", lambda e: e.scalar_tensor_tensor(out=p8[0:R, 0, :], in0=den, scalar=-1.0, in1=den, op0=ALU.mult, op1=ALU.max), reads=[B_nd], writes=[B_p8])
        P.op("dve", lambda e: e.tensor_tensor(out=p8[0:R, 1, :], in0=p8[0:R, 0, :], in1=thr_ap, op=ALU.max), reads=[B_p8, Bthr], writes=[B_p8])
        P.op("dve", lambda e: e.reciprocal(p8[0:R, 2, :], p8[0:R, 1, :]), reads=[B_p8], writes=[B_p8])
        P.op("pool", lambda e: e.tensor_tensor(out=sqs[0:R], in0=nd[0:R, :, 0:128], in1=nd[0:R, :, 0:128], op=ALU.mult), reads=[B_nd], writes=[B_sq])
        P.op("dve", lambda e: e.tensor_reduce(out=p8[0:R, 3, :], in_=sqs[0:R], axis=AX.X, op=ALU.add), reads=[B_sq], writes=[B_p8])
        P.op("dve", lambda e: e.tensor_tensor(out=p8[0:R, 4, :], in0=p8[0:R, 2, :], in1=p8[0:R, 2, :], op=ALU.mult), reads=[B_p8], writes=[B_p8])
        P.op("dve", lambda e: e.tensor_tensor(out=p8[0:R, 4, :], in0=p8[0:R, 4, :], in1=p8[0:R, 3, :], op=ALU.mult), reads=[B_p8], writes=[B_p8])
        P.op("dve", lambda e: e.tensor_scalar(p8[0:R, 4, :], p8[0:R, 4, :], 1.0 / 128, EPS, ALU.mult, ALU.add), reads=[B_p8], writes=[B_p8])
        P.op("act", lambda e: e.activation(out=p8[0:R, 5, :], in_=p8[0:R, 4, :], func=AF.Ln), reads=[B_p8], writes=[B_p8])
        P.op("act", lambda e: e.activation(out=p8[0:R, 6, :], in_=p8[0:R, 5, :], func=AF.Exp, scale=-0.5), reads=[B_p8], writes=[B_p8])
        P.op("dve", lambda e: e.tensor_tensor(out=p8[0:R, 7, :], in0=p8[0:R, 6, :], in1=p8[0:R, 2, :], op=ALU.mult), reads=[B_p8], writes=[B_p8])
        P.op("dve", lambda e: e.tensor_tensor(out=sqs[0:R], in0=nd[0:R, :, 0:128], in1=p8[0:R, 7, :].unsqueeze(2).to_broadcast([R, NH, 128]), op=ALU.mult),
             reads=[B_nd, B_p8, B_sq], writes=[B_sq])
        P.op("dve", lambda e: e.tensor_tensor(out=ogb[0:R, :], in0=sqs[0:R].rearrange("p a b -> p (a b)"), in1=soap, op=ALU.mult),
             reads=[B_sq, Bso], writes=[B_og])
        for k in range(KC):
            P.op("pe", lambda e, k=k: e.transpose(pT[:, k * 128:k * 128 + R], ogb[0:R, k * 128:(k + 1) * 128], identb[0:R, 0:R]),
                 reads=[B_og, B_identb], writes=[B_pT])
        P.op("act", lambda e: e.copy(hmT_dst, pT[:, :].rearrange("p (k t) -> p k t", k=KC)[:, :, 0:R]), reads=[B_pT], writes=[Bhm])


    def mlstm_chunk(rows, tcol, vap, Bv, soap, Bso, hmT_dst, Bhm, Bprev_col, Bslice, irow_sl, mask_ap, seq_first):
        R = rows
        P.op("dve", lambda e: e.scalar_tensor_tensor(out=zrow[:, 0:R], in0=irow_sl, scalar=Bprev_col, in1=Bslice,
                                                     op0=ALU.add, op1=ALU.subtract), reads=[B_rows, B_scan], writes=[B_zr])
        P.op("dve", lambda e: e.reduce_max(out=rs8[:, 0:1], in_=zrow[:, 0:R], axis=AX.X), reads=[B_zr], writes=[B_rs8])
        P.op("dve", lambda e: e.tensor_tensor(out=rs8[:, 1:2], in0=rs8[:, 0:1], in1=mu[:, 0:1], op=ALU.max), reads=[B_rs8, B_mu], writes=[B_rs8])
        P.op("dve", lambda e: e.tensor_scalar(rs8[:, 2:3], rs8[:, 1:2], -1.0, None, ALU.mult), reads=[B_rs8], writes=[B_rs8])
        P.op("dve", lambda e: e.tensor_tensor(out=rs8[:, 3:4], in0=Bprev_col, in1=rs8[:, 1:2], op=ALU.subtract), reads=[B_rs8, B_scan], writes=[B_rs8])
        P.op("act", lambda e: e.activation(out=erow[:, 0:R], in_=zrow[:, 0:R], func=AF.Exp, bias=rs8[:, 2:3], scale=1.0), reads=[B_zr, B_rs8], writes=[B_zr])
        P.op("act", lambda e: e.activation(out=trow[:, 0:R], in_=Bslice, func=AF.Exp, bias=rs8[:, 3:4], scale=-1.0), reads=[B_scan, B_rs8], writes=[B_zr])
        P.op("act", lambda e: e.activation(out=rs8[:, 4:5], in_=mu[:, 0:1], func=AF.Exp, bias=rs8[:, 2:3], scale=1.0), reads=[B_mu, B_rs8], writes=[B_rs8])
        P.op("dve", lambda e: e.scalar_tensor_tensor(out=mu[:, 0:1], in0=Bslice[:, R - 1:R], scalar=Bprev_col, in1=rs8[:, 1:2],
                                                     op0=ALU.subtract, op1=ALU.add), reads=[B_scan, B_rs8, B_mu], writes=[B_mu])
        pm0, Bm0 = pM[0], B_pM[0]
        P.op("pe", lambda e: e.transpose(pm0[0:R, 0:8], erow[:, 0:R], ident[0:8, 0:8]), reads=[B_zr, B_ident], writes=[Bm0])
        P.op("pe", lambda e: e.transpose(pm0[0:R, 8:16], trow[:, 0:R], ident[0:8, 0:8]), reads=[B_zr, B_ident], writes=[Bm0])
        P.op("act", lambda e: e.copy(et[0:R, :], pm0[0:R, 0:16]), reads=[Bm0], writes=[B_et])
        P.op("dve", lambda e: e.tensor_scalar(dg8[:, :], pairsel[:, :], rs8[:, 4:5], None, ALU.mult), reads=[B_rs8, B_sel], writes=[B_dg8])
        P.op("pe", lambda e: e.matmul(pm0[:, 16:20], lhsT=parsel[:, :], rhs=dg8[:, :], start=True, stop=True), reads=[B_dg8, B_sel], writes=[Bm0])
        P.op("act", lambda e: e.copy(decb[:, :], pm0[:, 16:20]), reads=[Bm0], writes=[B_decb])
        P.op("dve", lambda e: e.tensor_tensor(out=Chat[:], in0=Chat[:], in1=decb[:, :].unsqueeze(2).to_broadcast([128, 4, 129]), op=ALU.mult),
             reads=[B_Ch, B_decb], writes=[B_Ch])
        P.op("act", lambda e: e.copy(Csb[:], Chat[:]), reads=[B_Ch], writes=[B_Cs])
        P.op("dve", lambda e: e.tensor_tensor(out=vp[0:R], in0=vap, in1=et[0:R, 0:8].unsqueeze(2).to_broadcast([R, NH, 129]), op=ALU.mult),
             reads=[Bv, B_et], writes=[B_vp])
        for cb in range(4):
            P.op("pe", lambda e, cb=cb: e.transpose(pT[0:R, cb * 128:(cb + 1) * 128], kT[:, cb, tcol:tcol + R], identb[:, :]),
                 reads=[B_kT, B_identb], writes=[B_pT])
        P.op("act", lambda e: e.copy(ktok[0:R, :], pT[0:R, 0:512]), reads=[B_pT], writes=[B_kt])
        for par in range(2):
            pm, Bm = pM[1 + par], B_pM[1 + par]
            for hh in range(4):
                hd = hh * 2 + par
                cb, base = hd // 2, par * 64
                P.op("pe", lambda e, pm=pm, hh=hh, cb=cb, base=base: e.matmul(
                    pm[0:R, hh * 128:hh * 128 + R], lhsT=kT[base:base + 64, cb, tcol:tcol + R], rhs=qT[base:base + 64, cb, tcol:tcol + R],
                    start=True, stop=True), reads=[B_kT, B_qT], writes=[Bm])
            P.op("dve", lambda e, pm=pm, par=par: e.tensor_tensor(
                out=STb[0:R, par * 4:(par + 1) * 4, 0:R], in0=pm[0:R, :].rearrange("p (a b) -> p a b", a=4)[:, :, 0:R],
                in1=mask_ap.unsqueeze(1).to_broadcast([R, 4, R]), op=ALU.mult), reads=[Bm, B_maskT], writes=[B_ST[par]])
        for ph, hhs in enumerate([(0, 1, 2), (3,)]):
            for par in range(2):
                pm, Bm = pM[3 + par], B_pM[3 + par]
                for sl, hh in enumerate(hhs):
                    hd = hh * 2 + par
                    cb, base = hd // 2, par * 64
                    P.op("pe", lambda e, pm=pm, sl=sl, hd=hd, par=par, hh=hh: e.matmul(
                        pm[0:R, sl * 129:(sl + 1) * 129], lhsT=STb[0:R, par * 4 + hh, 0:R], rhs=vp[0:R, hd, :], start=True, stop=False),
                        reads=[B_ST[par], B_vp], writes=[Bm])
                    P.op("pe", lambda e, pm=pm, sl=sl, cb=cb, base=base, hd=hd: e.matmul(
                        pm[0:R, sl * 129:(sl + 1) * 129], lhsT=qT[base:base + 64, cb, tcol:tcol + R], rhs=Csb[base:base + 64, hd // 2, :],
                        start=False, stop=True), reads=[B_qT, B_Cs], writes=[Bm])
                for sl, hh in enumerate(hhs):
                    hd = hh * 2 + par
                    P.op("act", lambda e, pm=pm, sl=sl, hd=hd: e.copy(nd[0:R, hd, :], pm[0:R, sl * 129:(sl + 1) * 129]),
                         reads=[Bm], writes=[B_nd])
        for hp2 in range(2):
            pm, Bm = pM[1 + hp2], B_pM[1 + hp2]
            for hq in range(2):
                for par in range(2):
                    hd = (hp2 * 2 + hq) * 2 + par
                    P.op("pe", lambda e, pm=pm, hq=hq, par=par, hd=hd: e.matmul(
                        pm[par * 64:(par + 1) * 64, hq * 129:(hq + 1) * 129], lhsT=ktok[0:R, hd * 64:(hd + 1) * 64], rhs=vp[0:R, hd, :],
                        start=True, stop=True), reads=[B_kt, B_vp], writes=[Bm])
            P.op("dve", lambda e, pm=pm, hp2=hp2: e.tensor_tensor(
                out=Chat[:, hp2 * 2:hp2 * 2 + 2, :], in0=Chat[:, hp2 * 2:hp2 * 2 + 2, :],
                in1=pm[:, 0:258].rearrange("p (a b) -> p a b", a=2), op=ALU.add), reads=[Bm, B_Ch], writes=[B_Ch])
        mlstm_post(R, soap, Bso, hmT_dst, Bhm, et[0:R, 8:16], B_et)

    def mlstm_project(tiles, ntok):
        allx = B_xnT
        P.dma("sp", lambda e: e.dma_start(out=ghead_s[:], in_=g_head[0:1, :].partition_broadcast(128)), writes=[B_gh])
        P.op("pool", lambda e: e.memset(vaug[:, :, :, 128:129], 1.0), writes=B_v)
        for c in range(2):
            wv, Bw = wload(w_in_ml[:, c * 256:(c + 1) * 256], KC, 256)
            proj_fm(wv, Bw, KC, 256, xnT, allx, ntok,
                    lambda pa, Bp, cb, t0, n, c=c: P.op("act", lambda e: e.copy(qT[:, c * 2 + cb, t0:t0 + n], pa), reads=[Bp], writes=[B_qT]))
        for c in range(2):
            wv, Bw = wload(w_in_ml[:, 512 + c * 256:512 + (c + 1) * 256], KC, 256)
            proj_fm(wv, Bw, KC, 256, xnT, allx, ntok,
                    lambda pa, Bp, cb, t0, n, c=c: P.op("act", lambda e: e.mul(kT[:, c * 2 + cb, t0:t0 + n], pa, DQK ** -0.5), reads=[Bp], writes=[B_kT]))
        for c in range(4):
            wv, Bw = wload(w_in_ml[:, 1024 + c * 256:1024 + (c + 1) * 256], KC, 256)
            proj_tm(wv, Bw, KC, 256, xnT, allx, tiles,
                    lambda pa, Bp, ti, rows, c=c: P.op("act", lambda e: e.copy(
                        vaug[0:rows, ti, 2 * c:2 * c + 2, 0:128], pa.rearrange("p (a b) -> p a b", a=2)), reads=[Bp], writes=[B_v[ti]]))
        for c in range(4):
            wv, Bw = wload(w_in_ml[:, 2048 + c * 256:2048 + (c + 1) * 256], KC, 256)

            def ev(pa, Bp, ti, rows, c=c):
                P.op("act", lambda e: e.activation(out=junk[0:rows, 0:256], in_=pa, func=AF.Sigmoid), reads=[Bp], writes=[B_junk])
                P.op("dve", lambda e: e.tensor_tensor(out=so[0:rows, ti, c * 256:(c + 1) * 256], in0=junk[0:rows, 0:256],
                                                      in1=ghead_s[0:rows, c * 256:(c + 1) * 256], op=ALU.mult), reads=[B_junk, B_gh], writes=[B_so[ti]])
            proj_tm(wv, Bw, KC, 256, xnT, allx, tiles, ev)
        wv, Bw = wload(w_in_ml[:, 3072:3088], KC, 16)
        proj_tm(wv, Bw, KC, 16, xnT, allx, tiles,
                lambda pa, Bp, ti, rows: P.op("dve", lambda e: e.tensor_tensor(out=gates[0:rows, ti, :], in0=pa, in1=bg_s[0:rows, :], op=ALU.add),
                                              reads=[Bp, B_vec], writes=[B_g]))

    def mlstm_gate_rows(tiles, do_rows=True):
        nt = len(tiles)
        R = tiles[0][1]
        f = gates[0:R, 0:nt, 8:16]
        P.op("dve", lambda e: e.scalar_tensor_tensor(out=gtmp[0:R, 0:nt, :, 0], in0=f, scalar=-1.0, in1=f, op0=ALU.mult, op1=ALU.max), reads=[B_g], writes=[B_gt])
        P.op("act", lambda e: e.activation(out=gtmp[0:R, 0:nt, :, 1], in_=gtmp[0:R, 0:nt, :, 0], func=AF.Exp, scale=-1.0), reads=[B_gt], writes=[B_gt])
        P.op("act", lambda e: e.activation(out=gtmp[0:R, 0:nt, :, 2], in_=gtmp[0:R, 0:nt, :, 1], func=AF.Ln, bias=1.0, scale=1.0), reads=[B_gt], writes=[B_gt])
        P.op("dve", lambda e: e.tensor_scalar(gtmp[0:R, 0:nt, :, 3], f, 0.0, None, ALU.min), reads=[B_g, B_gt], writes=[B_gt])
        P.op("dve", lambda e: e.tensor_tensor(out=logf[0:R, 0:nt, :], in0=gtmp[0:R, 0:nt, :, 3], in1=gtmp[0:R, 0:nt, :, 2], op=ALU.subtract),
             reads=[B_gt], writes=[B_gt])
        if not do_rows:
            return
        for ti, (c0, rows) in enumerate(tiles):
            pm0, Bm0 = pM[0], B_pM[0]
            P.op("pe", lambda e, ti=ti, rows=rows: e.transpose(pm0[0:8, 0:rows], gates[0:rows, ti, 0:8], ident[0:rows, 0:rows]),
                 reads=[B_g, B_ident], writes=[Bm0])
            P.op("pe", lambda e, ti=ti, rows=rows: e.transpose(pm0[0:8, 128:128 + rows], logf[0:rows, ti, :], ident[0:rows, 0:rows]),
                 reads=[B_gt, B_ident], writes=[Bm0])
            P.op("act", lambda e, c0=c0, rows=rows: e.copy(irow[:, c0:c0 + rows], pm0[0:8, 0:rows]), reads=[Bm0], writes=[B_rows])
            P.op("act", lambda e, c0=c0, rows=rows: e.copy(frow[:, c0:c0 + rows], pm0[0:8, 128:128 + rows]), reads=[Bm0], writes=[B_rows])

    def w_out_apply(w2d, tiles, srcT, Bsrc, nkc=KC):
        for c in range(4):
            wv, Bw = wload(w2d[:, c * 256:(c + 1) * 256], nkc, 256)
            proj_tm(wv, Bw, nkc, 256, srcT, Bsrc, tiles,
                    lambda pa, Bp, ti, rows, c=c: P.op("dve", lambda e: e.tensor_tensor(
                        out=h[0:rows, ti, c * 256:(c + 1) * 256], in0=pa, in1=h[0:rows, ti, c * 256:(c + 1) * 256], op=ALU.add),
                        reads=[Bp, B_h[ti]], writes=[B_h[ti]]))

    actT = arena[:, 0:NF_ * GT].rearrange("p (f t) -> p f t", f=NF_); B_act = Buf("actT")
    sil = mscr[:, 0:512]; B_sil = Buf("sil")

    def ffn(layer, tiles, ntok):
        NF = DFF // 128
        for fc in range(NF):
            wg, Bwg = wload(w_gu[layer, :, fc * 128:(fc + 1) * 128], KC, 128)
            wu, Bwu = wload(w_gu[layer, :, DFF + fc * 128:DFF + (fc + 1) * 128], KC, 128)
            for t0 in range(0, ntok, 512):
                n = min(512, ntok - t0)
                pg, Bpg = next_pA()
                for k in range(KC):
                    P.op("pe", lambda e, k=k, pg=pg, t0=t0, n=n, wg=wg: e.matmul(pg[:, 0:n], lhsT=wg[:, k, :], rhs=xnT[:, k, t0:t0 + n],
                                                                                start=(k == 0), stop=(k == KC - 1)), reads=[Bwg] + B_xnT, writes=[Bpg])
                pu, Bpu = next_pA()
                for k in range(KC):
                    P.op("pe", lambda e, k=k, pu=pu, t0=t0, n=n, wu=wu: e.matmul(pu[:, 0:n], lhsT=wu[:, k, :], rhs=xnT[:, k, t0:t0 + n],
                                                                                start=(k == 0), stop=(k == KC - 1)), reads=[Bwu] + B_xnT, writes=[Bpu])
                P.op("act", lambda e, pg=pg, n=n: e.activation(out=sil[:, 0:n], in_=pg[:, 0:n], func=AF.Silu), reads=[Bpg], writes=[B_sil])
                P.op("dve", lambda e, pu=pu, n=n, fc=fc, t0=t0: e.tensor_tensor(out=actT[:, fc, t0:t0 + n], in0=pu[:, 0:n], in1=sil[:, 0:n], op=ALU.mult),
                     reads=[Bpu, B_sil], writes=[B_act])
        for half in range(2):
            wds = []
            for ti, (c0, rows) in enumerate(tiles):
                pass
            groups = [(k0, min(4, NF - k0)) for k0 in range(0, NF, 4)]
            accs = [(pM[i], B_pM[i]) for i in range(5)] + [(pA[0], B_pA[0]), (pA[1], B_pA[1])]
            for tb in range(0, len(tiles), len(accs)):
                tl = tiles[tb:tb + len(accs)]
                for (k0, nk) in groups:
                    wv, Bw = wload(w_dn[layer, k0 * 128:(k0 + nk) * 128, half * 512:(half + 1) * 512], nk, 512)
                    for j, (c0, rows) in enumerate(tl):
                        pa, Bp = accs[j]
                        for kk in range(nk):
                            P.op("pe", lambda e, pa=pa, kk=kk, k0=k0, c0=c0, rows=rows, wv=wv: e.matmul(
                                pa[0:rows, 0:512], lhsT=actT[:, k0 + kk, c0:c0 + rows], rhs=wv[:, kk, :],
                                start=(k0 + kk == 0), stop=(k0 + kk == NF - 1)), reads=[Bw, B_act], writes=[Bp])
                for j, (c0, rows) in enumerate(tl):
                    pa, Bp = accs[j]
                    ti = tb + j
                    P.op("dve", lambda e, pa=pa, ti=ti, rows=rows, half=half: e.tensor_tensor(
                        out=h[0:rows, ti, half * 512:(half + 1) * 512], in0=pa[0:rows, 0:512], in1=h[0:rows, ti, half * 512:(half + 1) * 512], op=ALU.add),
                        reads=[Bp, B_h[ti]], writes=[B_h[ti]])

    NKB = S // 128
    knT = sb("knT", [128, NH, S], BF16); B_kn = Buf("knT")
    krT = sb("krT", [64, S], BF16); B_kr = Buf("krT")
    Vh = sb("Vh", [128, NKB, NH, 129], BF16); B_Vh = Buf("Vh")
    P.op("pool", lambda e: e.memset(Vh[:, :, :, 128:129], 1.0), writes=[B_Vh])
    gqfm_s = sb("gqfm_s", [128, 3])
    P.dma("sp", lambda e: e.dma_start(out=gqfm_s[:], in_=gqfm[:, :]), writes=[B_gfm])
    gfin_s = ghead_s
    ckq = mscr[:, 0:NT * 704].rearrange("p (t c) -> p t c", t=NT); B_ckq = [Buf(f"ckq{t}") for t in range(NT)]
    a0 = 0
    cqnT = arena[:, a0:a0 + 3 * GT].rearrange("p (c t) -> p c t", c=3); B_cqn = [Buf(f"cqn{t}") for t in range(NT)]; a0 += 3 * GT
    ckvT = arena[:, a0:a0 + 2 * GT].rearrange("p (c t) -> p c t", c=2); B_ckvT = [Buf(f"ckvT{t}") for t in range(NT)]; a0 += 2 * GT
    qnT = arena[:, a0:a0 + NH * GT].rearrange("p (h t) -> p h t", h=NH); B_qn = Buf("qnT"); a0 += NH * GT
    qrT = arena[:, a0:a0 + NH * GT].rearrange("p (h t) -> p h t", h=NH); B_qr = Buf("qrT"); a0 += NH * GT
    PTb = arena[:, a0:a0 + 2 * NH * 128].rearrange("p (b h t) -> p b h t", b=2, h=NH); B_PT = [Buf("PT0"), Buf("PT1")]; a0 += 2 * NH * 128
    attb = arena[:, a0:a0 + D]; B_att = Buf("attb"); a0 += D
    wrot = arena[:, a0:a0 + 3 * 64].rearrange("p (c r) -> p c r", c=3); B_wrot = Buf("wrot"); a0 += 192
    assert a0 <= ASZ, (a0, ASZ)
    sarena = knT[:, :, :].rearrange("p h s -> p (h s)") if NH * S >= 13500 else sb("sarena", [128, 13500], BF16)
    arena_p, arena, a0 = arena, sarena, 0
    NKP = 16
    kp = [arena[:, a0 + i * 322:a0 + (i + 1) * 322] for i in range(NKP)]; B_kp = [Buf(f"kp{i}") for i in range(NKP)]; a0 += NKP * 322
    KTp = [arena[:, a0 + i * 384:a0 + (i + 1) * 384].rearrange("p (c k) -> p c k", c=3) for i in range(3)]; B_KTp = [Buf("KTp0"), Buf("KTp1"), Buf("KTp2")]; a0 += 1152
    PTs = [arena[:, a0 + i * 8:a0 + (i + 1) * 8] for i in range(2)]; B_PTs = [Buf("PTs0"), Buf("PTs1")]; a0 += 16
    wukT = arena[:, a0:a0 + 2048].rearrange("p (h k c) -> p h k c", h=NH, k=2); B_wukT = Buf("wukT"); a0 += 2048
    qlT = arena[:, a0:a0 + 2 * NS * NH].rearrange("p (k t h) -> p k t h", k=2, t=NS); B_qlT = Buf("qlT"); a0 += 2 * NS * NH
    cnew = arena[:, a0:a0 + 258]; B_cnew = Buf("cnew"); a0 += 258
    krTn = arena[:, a0:a0 + NS]; B_krTn = Buf("krTn"); a0 += NS
    olat = arena[:, a0:a0 + 256]; B_olat = Buf("olat"); a0 += 256
    olT = arena[:, a0:a0 + 2 * NH * NS].rearrange("p (k h t) -> p k h t", k=2, h=NH); B_olT = Buf("olT"); a0 += 2 * NH * NS
    pnew = arena[:, a0:a0 + 8]; B_pnew = Buf("pnew"); a0 += 8
    qs_b = arena[:, a0:a0 + 4 * NS].rearrange("p (c t) -> p c t", c=4); B_qs = Buf("qs"); a0 += 4 * NS
    qTm = arena[:, a0:a0 + 4 * NS * NS].rearrange("p (c a t) -> p c a t", c=4, a=NS); B_qTm = Buf("qTm"); a0 += 4 * NS * NS
    keb = arena[:, a0:a0 + 512]; B_keb = Buf("keb"); a0 += 512
    kmt = [arena[:, a0 + i * 512:a0 + (i + 1) * 512] for i in range(2)]; B_kmt = [Buf("kmt0"), Buf("kmt1")]; a0 += 1024
    c0b = [arena[:, a0 + i * 516:a0 + (i + 1) * 516].rearrange("p (a b) -> p a b", a=4) for i in range(2)]; B_c0b = [Buf("c0b0"), Buf("c0b1")]; a0 += 1032
    assert a0 <= 13500, a0
    arena = arena_p
    if USE_CACHE:
        idx_i = sb("idx_i", [128, NPG], I32); idx_f = sb("idx_f", [128, NPG]); B_idx = Buf("idx")
        iota_p = sb("iota_p", [128, 1]); B_iota = Buf("iota")
        P.op("pool", lambda e: e.iota(iota_p[:], pattern=[[0, 1]], base=0, channel_multiplier=1, allow_small_or_imprecise_dtypes=True), writes=[B_iota])
    oh16 = sb("oh16", [128, NS, NS]); B_oh = Buf("oh16")
    P.dma("sp", lambda e: e.dma_start(out=oh16[:], in_=oh16_d[:, :, :]), writes=[B_oh])
    sg = sb("sg", [NS, 12, 8]); B_sg = Buf("sg")
    d8 = sb("d8", [8, NS]); R8 = sb("R8", [8, 4, NS]); B_d8 = Buf("d8")
    decT = sb("decT", [128, 4, NS]); B_decT = Buf("decT")
    _c0 = mscr[:, 2056:2056 + 516].rearrange("p (a b) -> p a b", a=4)
    c0f = [_c0, _c0]; _B = Buf("c0f"); B_c0f = [_B, _B]

    def sample_attention(R):
        for c in range(4):
            wv, Bw = wload(w_uk[:, c * 256:(c + 1) * 256], 2, 256)
            for kc in range(2):
                for hh in range(2):
                    sl = kc * 2 + hh
                    P.op("pe", lambda e, wv=wv, kc=kc, hh=hh, sl=sl: e.transpose(pT[:, sl * 128:(sl + 1) * 128], wv[:, kc, hh * 128:(hh + 1) * 128], identb[:, :]),
                         reads=[Bw, B_identb], writes=[B_pT])
            P.op("act", lambda e, c=c: e.copy(wukT[:, 2 * c:2 * c + 2, :, :], pT[:, 0:512].rearrange("p (k h c) -> p h k c", k=2, h=2)), reads=[B_pT], writes=[B_wukT])
        pa, Bp = next_pA()
        for kc in range(2):
            for hd in range(NH):
                o = (kc * NH + hd) * R
                P.op("pe", lambda e, pa=pa, kc=kc, hd=hd, o=o: e.matmul(pa[:, o:o + R], lhsT=wukT[:, hd, kc, :], rhs=qnT[:, hd, 0:R], start=True, stop=True),
                     reads=[B_wukT, B_qn], writes=[Bp])
        P.op("act", lambda e, pa=pa: e.copy(qlT[:, :, 0:R, :].rearrange("p k t h -> p k h t"), pa[:, 0:2 * NH * R].rearrange("p (k h t) -> p k h t", k=2, h=NH)),
             reads=[Bp], writes=[B_qlT])
        P.op("pool", lambda e: e.memset(cnew[0:R, 0:1], 0.0), writes=[B_cnew])
        P.op("pool", lambda e: e.memset(cnew[0:R, 1:2], 1.0), writes=[B_cnew])
        for i in range(NKP):
            P.op("pool", lambda e, i=i: e.memset(kp[i][:, 0:1], 0.0), writes=[B_kp[i]])
            P.op("pool", lambda e, i=i: e.memset(kp[i][:, 1:2], 1.0), writes=[B_kp[i]])
        kctr = 0
        for t in range(R):
            if USE_CACHE:
                P.dma("sp", lambda e, t=t: e.dma_start(out=idx_i[:], in_=ptb[0:1, t * NPG:(t + 1) * NPG].partition_broadcast(128)), writes=[B_idx])
                P.op("pool", lambda e: e.tensor_copy(idx_f[:], idx_i[:]), reads=[B_idx], writes=[B_idx])
                P.op("pool", lambda e: e.tensor_scalar(idx_f[:], idx_f[:], 128.0, iota_p[:, 0:1], ALU.mult, ALU.add), reads=[B_idx, B_iota], writes=[B_idx])
                P.op("pool", lambda e: e.tensor_copy(idx_i[:], idx_f[:]), reads=[B_idx], writes=[B_idx])
            acc, Bacc = pM[0], B_pM[0]
            npg = NPG if USE_CACHE else 0
            def stA(j):
                nonlocal kctr
                ki = kctr % NKP; k3 = kctr % 3; tb = kctr % 2; kctr += 1
                pTx, BpTx = (pT, B_pT) if tb == 0 else (pT2, B_pM[4])
                P.dma("pool", lambda e, ki=ki, j=j: e.indirect_dma_start(out=kp[ki][:, 2:322], out_offset=None, in_=latkr[:, :],
                                                                    in_offset=bass.IndirectOffsetOnAxis(ap=idx_i[:, j:j + 1], axis=0)), reads=[B_idx], writes=[B_kp[ki]])
                for k in range(2):
                    P.op("pe", lambda e, ki=ki, k=k, pTx=pTx: e.transpose(pTx[:, k * 128:(k + 1) * 128], kp[ki][:, 2 + k * 128:2 + (k + 1) * 128], identb[:, :]),
                         reads=[B_kp[ki], B_identb], writes=[BpTx])
                P.op("pe", lambda e, ki=ki, pTx=pTx: e.transpose(pTx[0:64, 256:384], kp[ki][:, 258:322], identb[:, :]), reads=[B_kp[ki], B_identb], writes=[BpTx])
                P.op("act", lambda e, k3=k3, pTx=pTx: e.copy(KTp[k3][:, 0:2, :], pTx[:, 0:256].rearrange("p (c k) -> p c k", c=2)), reads=[BpTx], writes=[B_KTp[k3]])
                P.op("dve", lambda e, k3=k3, pTx=pTx: e.tensor_copy(KTp[k3][0:64, 2, :], pTx[0:64, 256:384]), reads=[BpTx], writes=[B_KTp[k3]])
                return (ki, k3)

            def stB(st, j):
                ki, k3 = st
                k2 = j % 2
                ps_, Bps = next_pA()
                for k in range(2):
                    P.op("pe", lambda e, ps_=ps_, k=k, k3=k3, t=t: e.matmul(ps_[:, 0:8], lhsT=KTp[k3][:, k, :], rhs=qlT[:, k, t, :], start=(k == 0), stop=False),
                         reads=[B_KTp[k3], B_qlT], writes=[Bps])
                P.op("pe", lambda e, ps_=ps_, k3=k3, t=t: e.matmul(ps_[:, 0:8], lhsT=KTp[k3][0:64, 2, :], rhs=qrT[0:64, :, t], start=False, stop=True),
                     reads=[B_KTp[k3], B_qr], writes=[Bps])
                P.op("act", lambda e, ps_=ps_, k2=k2: e.activation(out=PTs[k2][:, :], in_=ps_[:, 0:8], func=AF.Exp, scale=MLA_SCALE), reads=[Bps], writes=[B_PTs[k2]])

            def stC(st, j):
                ki, k3 = st
                k2 = j % 2
                P.op("pe", lambda e, k2=k2, ki=ki, j=j: e.matmul(acc[0:8, 0:258], lhsT=PTs[k2][:, :], rhs=kp[ki][:, 0:258], start=(j == 0), stop=False),
                     reads=[B_PTs[k2], B_kp[ki]], writes=[Bacc])
            sts = {}
            for i in range(npg + 2):
                if i < npg:
                    sts[i] = stA(i)
                if 0 <= i - 1 < npg:
                    stB(sts[i - 1], i - 1)
                if 0 <= i - 2 < npg:
                    stC(sts[i - 2], i - 2)
            pn, Bpn = pM[1], B_pM[1]
            for k in range(2):
                P.op("pe", lambda e, k=k, t=t: e.matmul(pn[0:R, 0:8], lhsT=ckvT[:, k, 0:R], rhs=qlT[:, k, t, :], start=(k == 0), stop=False),
                     reads=B_ckvT + [B_qlT], writes=[Bpn])
            P.op("pe", lambda e, t=t: e.matmul(pn[0:R, 0:8], lhsT=krTn[0:64, 0:R], rhs=qrT[0:64, :, t], start=False, stop=True), reads=[B_krTn, B_qr], writes=[Bpn])
            P.op("act", lambda e: e.activation(out=sg[0:R, 9, :], in_=pn[0:R, 0:8], func=AF.Exp, scale=MLA_SCALE), reads=[Bpn], writes=[B_sg])
            P.op("dve", lambda e, t=t: e.tensor_scalar(pnew[0:R, :], sg[0:R, 9, :], ident[0:R, t:t + 1], None, ALU.mult), reads=[B_sg, B_ident], writes=[B_pnew])
            P.op("pe", lambda e, npg=npg: e.matmul(acc[0:8, 0:258], lhsT=pnew[0:R, :], rhs=cnew[0:R, 0:258], start=(npg == 0), stop=True), reads=[B_pnew, B_cnew], writes=[Bacc])
            P.op("dve", lambda e: e.reciprocal(orc[0:8, 0:1], acc[0:8, 1:2]), reads=[Bacc], writes=[B_orc])
            P.op("dve", lambda e: e.tensor_scalar(olat[0:8, :], acc[0:8, 2:258], orc[0:8, 0:1], None, ALU.mult), reads=[Bacc, B_orc], writes=[B_olat])
            for k in range(2):
                P.op("pe", lambda e, k=k: e.transpose(pT[:, 512 + k * 8:512 + (k + 1) * 8], olat[0:8, k * 128:(k + 1) * 128], identb[0:8, 0:8]), reads=[B_olat, B_identb], writes=[B_pT])
            P.op("act", lambda e, t=t: e.copy(olT[:, :, :, t], pT[:, 512:528].rearrange("p (k h) -> p k h", k=2)), reads=[B_pT], writes=[B_olT])
        for c in range(2):
            wv, Bw = wload(w_uv[:, c * 512:(c + 1) * 512], 2, 512)
            pa, Bp = next_pA()
            for hh in range(4):
                hd = c * 4 + hh
                for k in range(2):
                    P.op("pe", lambda e, pa=pa, wv=wv, hh=hh, hd=hd, k=k: e.matmul(pa[:, hh * R:(hh + 1) * R], lhsT=wv[:, k, hh * 128:(hh + 1) * 128], rhs=olT[:, k, hd, 0:R],
                                                                            start=(k == 0), stop=(k == 1)), reads=[Bw, B_olT], writes=[Bp])
            P.op("act", lambda e, pa=pa, c=c: e.copy(hmT[:, 4 * c:4 * c + 4, 0:R], pa[:, 0:4 * R].rearrange("p (h t) -> p h t", h=4)), reads=[Bp], writes=[B_hm[0]])

    def sample_group():
        R = NS
        tiles = [(0, R)]
        P.dma("sp", lambda e: e.dma_start(out=h[0:R, 0, :], in_=xs[:, :]), writes=[B_h[0]])
        rmsnorm_T(h[0:R, 0, :], B_h[0], R, 0, xnT[:, :, 0:R], B_xnT[0])
        mlstm_project(tiles, R)
        mlstm_gate_rows(tiles, do_rows=False)
        ig = gates[0:R, 0, 0:8]; lf = logf[0:R, 0, :]
        P.dma("sp", lambda e: e.dma_start(out=sg[0:R, 0, :], in_=stm[:, :]), writes=[B_sg])
        P.op("dve", lambda e: e.tensor_tensor(out=sg[0:R, 1, :], in0=ig, in1=lf, op=ALU.subtract), reads=[B_g, B_gt, B_sg], writes=[B_sg])
        P.op("dve", lambda e: e.tensor_tensor(out=sg[0:R, 2, :], in0=sg[0:R, 0, :], in1=sg[0:R, 1, :], op=ALU.max), reads=[B_sg], writes=[B_sg])
        P.op("dve", lambda e: e.tensor_tensor(out=sg[0:R, 3, :], in0=lf, in1=sg[0:R, 2, :], op=ALU.add), reads=[B_gt, B_sg], writes=[B_sg])
        P.op("dve", lambda e: e.tensor_tensor(out=sg[0:R, 4, :], in0=sg[0:R, 0, :], in1=sg[0:R, 2, :], op=ALU.subtract), reads=[B_sg], writes=[B_sg])
        P.op("act", lambda e: e.activation(out=sg[0:R, 4, :], in_=sg[0:R, 4, :], func=AF.Exp), reads=[B_sg], writes=[B_sg])
        P.op("dve", lambda e: e.tensor_tensor(out=sg[0:R, 5, :], in0=sg[0:R, 1, :], in1=sg[0:R, 2, :], op=ALU.subtract), reads=[B_sg], writes=[B_sg])
        P.op("act", lambda e: e.activation(out=sg[0:R, 5, :], in_=sg[0:R, 5, :], func=AF.Exp), reads=[B_sg], writes=[B_sg])
        P.op("act", lambda e: e.activation(out=sg[0:R, 6, :], in_=sg[0:R, 3, :], func=AF.Exp, scale=-1.0), reads=[B_sg], writes=[B_sg])
        P.dma("pool", lambda e: e.dma_start(out=ms_[:, :], in_=sg[0:R, 3, :]), reads=[B_sg], writes=[B_out])
        for cb in range(4):
            P.op("pe", lambda e, cb=cb: e.transpose(pT[0:R, cb * 128:(cb + 1) * 128], qT[:, cb, 0:R], identb[:, :]), reads=[B_qT, B_identb], writes=[B_pT])
        P.op("act", lambda e: e.copy(xnb[0:R, 0:512], pT[0:R, 0:512]), reads=[B_pT], writes=[B_xnb])
        for cb in range(4):
            P.op("pe", lambda e, cb=cb: e.transpose(pT[0:R, cb * 128:(cb + 1) * 128], kT[:, cb, 0:R], identb[:, :]), reads=[B_kT, B_identb], writes=[B_pT])
        P.op("act", lambda e: e.copy(ktok[0:R, :], pT[0:R, 0:512]), reads=[B_pT], writes=[B_kt])
        sq64 = sqs[0:R, :, 0:64]
        P.op("dve", lambda e: e.tensor_tensor(out=sq64, in0=xnb[0:R, 0:512].rearrange("p (h d) -> p h d", h=NH), in1=ktok[0:R, :].rearrange("p (h d) -> p h d", h=NH), op=ALU.mult),
             reads=[B_xnb, B_kt], writes=[B_sq])
        P.op("dve", lambda e: e.tensor_reduce(out=sg[0:R, 7, :], in_=sq64, axis=AX.X, op=ALU.add), reads=[B_sq, B_sg], writes=[B_sg])
        P.op("dve", lambda e: e.tensor_tensor(out=sg[0:R, 8, :], in0=sg[0:R, 7, :], in1=sg[0:R, 5, :], op=ALU.mult), reads=[B_sg], writes=[B_sg])
        P.op("dve", lambda e: e.tensor_tensor(out=nd[0:R], in0=vaug[0:R, 0], in1=sg[0:R, 8, :].unsqueeze(2).to_broadcast([R, NH, 129]), op=ALU.mult),
             reads=[B_v[0], B_sg], writes=[B_nd])
        P.op("dve", lambda e: e.tensor_tensor(out=keb[0:R, :].rearrange("p (h d) -> p h d", h=NH), in0=ktok[0:R, :].rearrange("p (h d) -> p h d", h=NH),
                                              in1=sg[0:R, 5, :].unsqueeze(2).to_broadcast([R, NH, 64]), op=ALU.mult), reads=[B_kt, B_sg], writes=[B_keb])
        P.op("pe", lambda e: e.transpose(pM[4][0:8, 0:R], sg[0:R, 4, :], ident[0:R, 0:R]), reads=[B_sg, B_ident], writes=[B_pM[4]])
        P.op("act", lambda e: e.copy(d8[:, 0:R], pM[4][0:8, 0:R]), reads=[B_pM[4]], writes=[B_d8])
        P.op("dve", lambda e: e.tensor_tensor(out=R8[:, :, 0:R], in0=d8[:, 0:R].unsqueeze(1).to_broadcast([8, 4, R]), in1=pairsel[:, :].unsqueeze(2).to_broadcast([8, 4, R]), op=ALU.mult),
             reads=[B_d8, B_sel], writes=[B_d8])
        P.op("pe", lambda e: e.matmul(pM[4][:, 64:64 + 4 * R], lhsT=parsel[:, :], rhs=R8[:, :, 0:R].rearrange("p c t -> p (c t)"), start=True, stop=True), reads=[B_d8, B_sel], writes=[B_pM[4]])
        P.op("act", lambda e: e.copy(decT[:, :, 0:R], pM[4][:, 64:64 + 4 * R].rearrange("p (c t) -> p c t", c=4)), reads=[B_pM[4]], writes=[B_decT])
        P.op("dve", lambda e: e.tensor_tensor(out=qs_b[:, :, 0:R], in0=qT[:, :, 0:R], in1=decT[:, :, 0:R], op=ALU.mult), reads=[B_qT, B_decT], writes=[B_qs])
        P.op("dve", lambda e: e.tensor_tensor(out=qTm[:, :, 0:R, 0:R], in0=qs_b[:, :, 0:R].unsqueeze(2).to_broadcast([128, 4, R, R]),
                                              in1=oh16[:, 0:R, 0:R].unsqueeze(1).to_broadcast([128, 4, R, R]), op=ALU.mult), reads=[B_qs, B_oh], writes=[B_qTm])

        def ibank(par, hp):
            bi = par * 2 + (1 if hp == 3 else 0)
            return (pM[bi], B_pM[bi], 0 if hp == 3 else hp, bi)
        started = set()
        for t in range(R):
            ci = t % 2
            stC_v = stC[t].rearrange("(hp par) d e -> par d hp e", par=2)
            stn_v = stn[t].rearrange("(hp par) (d o) -> par d hp o", par=2, o=1)
            for par in range(2):
                P.dma("sp", lambda e, ci=ci, par=par, stC_v=stC_v: e.dma_start(out=c0f[ci][par * 64:(par + 1) * 64, :, 0:128], in_=stC_v[par]), writes=[B_c0f[ci]])
                P.dma("sp", lambda e, ci=ci, par=par, stn_v=stn_v: e.dma_start(out=c0f[ci][par * 64:(par + 1) * 64, :, 128:129], in_=stn_v[par], allow_slow_non_contiguous=True), writes=[B_c0f[ci]])
            P.op("pool", lambda e, ci=ci: e.tensor_copy(c0b[ci], c0f[ci][:]), reads=[B_c0f[ci]], writes=[B_c0b[ci]])
            for hd in range(NH):
                par, hp = hd % 2, hd // 2
                bank, Bb, sl, bi = ibank(par, hp)
                first = bi not in started
                started.add(bi)
                last = (t == R - 1) and (hp == 3 or hp == 2)
                P.op("pe", lambda e, bank=bank, sl=sl, par=par, hp=hp, t=t, ci=ci, first=first, last=last: e.matmul(
                    bank[0:R, sl * 129:(sl + 1) * 129], lhsT=qTm[par * 64:(par + 1) * 64, hp, t, 0:R], rhs=c0b[ci][par * 64:(par + 1) * 64, hp, :],
                    start=first, stop=last, skip_group_check=True), reads=[B_qTm, B_c0b[ci]], writes=[Bb])
            km = kmt[t % 2]; Bkm = B_kmt[t % 2]
            P.op("dve", lambda e, km=km, t=t: e.tensor_scalar(km[0:R, :], keb[0:R, :], ident[0:R, t:t + 1], None, ALU.mult), reads=[B_keb, B_ident], writes=[Bkm])
            pk0, Bk0 = pM[4], B_pM[4]
            pk1, Bk1 = next_pA()
            for hd in range(NH):
                par, hp = hd % 2, hd // 2
                pk, sl = (pk1, 0) if hp == 3 else (pk0, hp)
                Bk = Bk1 if hp == 3 else Bk0
                P.op("pe", lambda e, pk=pk, sl=sl, par=par, hd=hd, km=km: e.matmul(
                    pk[par * 64:(par + 1) * 64, sl * 129:(sl + 1) * 129], lhsT=km[0:R, hd * 64:(hd + 1) * 64], rhs=vaug[0:R, 0, hd, :], start=True, stop=True),
                    reads=[Bkm, B_v[0]], writes=[Bk])
            P.op("dve", lambda e, ci=ci, t=t: e.tensor_tensor(out=c0f[ci][:], in0=c0f[ci][:], in1=decT[:, :, t].unsqueeze(2).to_broadcast([128, 4, 129]), op=ALU.mult),
                 reads=[B_c0f[ci], B_decT, B_c0b[ci]], writes=[B_c0f[ci]])
            P.op("dve", lambda e, ci=ci, pk0=pk0: e.tensor_tensor(out=c0f[ci][:, 0:3, :], in0=c0f[ci][:, 0:3, :], in1=pk0[:, 0:387].rearrange("p (a b) -> p a b", a=3), op=ALU.add),
                 reads=[B_c0f[ci], Bk0], writes=[B_c0f[ci]])
            P.op("dve", lambda e, ci=ci, pk1=pk1: e.tensor_tensor(out=c0f[ci][:, 3, :], in0=c0f[ci][:, 3, :], in1=pk1[:, 0:129], op=ALU.add),
                 reads=[B_c0f[ci], Bk1], writes=[B_c0f[ci]])
            Cs_v = Cs[t].rearrange("(hp par) d e -> par d hp e", par=2)
            ns_v = ns_[t].rearrange("(hp par) (d o) -> par d hp o", par=2, o=1)
            for par in range(2):
                P.dma("pool", lambda e, ci=ci, par=par, Cs_v=Cs_v: e.dma_start(out=Cs_v[par], in_=c0f[ci][par * 64:(par + 1) * 64, :, 0:128]), reads=[B_c0f[ci]], writes=[B_out])
                P.dma("pool", lambda e, ci=ci, par=par, ns_v=ns_v: e.dma_start(out=ns_v[par], in_=c0f[ci][par * 64:(par + 1) * 64, :, 128:129], allow_slow_non_contiguous=True), reads=[B_c0f[ci]], writes=[B_out])
        for par in range(2):
            for (hps, bq) in [((0, 1, 2), 0), ((3,), 1)]:
                bank, Bb = pM[par * 2 + bq], B_pM[par * 2 + bq]
                for sl, hp in enumerate(hps):
                    hd = hp * 2 + par
                    P.op("dve", lambda e, bank=bank, sl=sl, hd=hd: e.tensor_tensor(out=nd[0:R, hd, :], in0=nd[0:R, hd, :], in1=bank[0:R, sl * 129:(sl + 1) * 129], op=ALU.add),
                         reads=[Bb, B_nd], writes=[B_nd])
        mlstm_post(R, so[0:R, 0, :], B_so[0], hmT[:, :, 0:R], B_hm[0], sg[0:R, 6, :], B_sg)
        w_out_apply(w_out_ml, tiles, hmT, B_hm)
        if stage >= 2:
            rmsnorm_T(h[0:R, 0, :], B_h[0], R, 1, xnT[:, :, 0:R], B_xnT[0])
            ffn(0, tiles, R)
        if stage >= 3:
            rmsnorm_T(h[0:R, 0, :], B_h[0], R, 2, xnT[:, :, 0:R], B_xnT[0])
            mla_mix(0, 0, tiles, sample=True)
        if stage >= 4:
            rmsnorm_T(h[0:R, 0, :], B_h[0], R, 3, xnT[:, :, 0:R], B_xnT[0])
            ffn(1, tiles, R)
            load_gfin()
            final_norm(h[0:R, 0, :], B_h[0], R)
        P.dma("pool", lambda e: e.dma_start(out=ys[:, :], in_=h[0:R, 0, :]), reads=[B_h[0]], writes=[B_out])

    assert 2 * GT <= 1024
    csf = mscr[0:64, 1024:1024 + 2 * GT].rearrange("p (a b) -> p a b", a=2); B_csf = Buf("csf")
    cst = sb("cst", [128, NT, 64]); B_cst = Buf("cst")
    lko = sb("lko", [128, 320]); B_lko = Buf("lko")
    rp4 = sb("rp4", [128, 4, 32]); B_rp4 = Buf("rp4")
    t64 = mscr[0:64, 0:1024].rearrange("p (a b) -> p a b", a=2); B_t64 = Buf("t64")
    orc = sb("orc", [128, 8]); B_orc = Buf("orc")

    def mla_mix(seq, g, tiles, sample=False):
        pos0 = 0 if sample else g * GT
        tok0 = 0 if sample else seq * S + pos0
        ntok = sum(r for _, r in tiles)
        lat_o, kr_o = (lats, krs) if sample else (latp, krp)
        if sample:
            P.dma("sp", lambda e: e.dma_start(out=cst[0:ntok, 0, :], in_=cs_tm[S:S + 1, :].partition_broadcast(ntok)), writes=[B_cst])
        else:
            P.dma("sp", lambda e: e.dma_start(out=cst[:], in_=cs_tm[pos0:pos0 + GT, :].rearrange("(t p) c -> p t c", p=128)), writes=[B_cst])
        for (c0, nc_) in [(0, 256), (256, 128), (384, 256), (640, 64)]:
            wv, Bw = wload(w_in_mla[:, c0:c0 + nc_], KC, nc_)
            proj_tm(wv, Bw, KC, nc_, xnT, B_xnT, tiles,
                    lambda pa, Bp, ti, rows, c0=c0, nc_=nc_: P.op("act", lambda e: e.copy(ckq[0:rows, ti, c0:c0 + nc_], pa), reads=[Bp], writes=[B_ckq[ti]] + ([B_c0f[0]] if sample else [])))
        for ti, (tc, rows) in enumerate(tiles):
            kb = (pos0 + tc) // 128
            rmsnorm_T(ckq[0:rows, ti, 0:QL], B_ckq[ti], rows, None, cqnT[:, :, tc:tc + rows], B_cqn[ti], width=QL, gap=gqfm_s[:, 0:3])
            ckv = ckq[0:rows, ti, QL:QL + KVL]
            P.op("dve", lambda e, ckv=ckv, rows=rows: e.scalar_tensor_tensor(out=junk[0:rows, 0:KVL], in0=ckv, scalar=1.0, in1=ckv, op0=ALU.mult, op1=ALU.mult,
                                                                     accum_out=st4[0:rows, 4:5]), reads=[B_ckq[ti]], writes=[B_junk, B_st4])
            P.op("dve", lambda e, rows=rows: e.tensor_scalar(st4[0:rows, 5:6], st4[0:rows, 4:5], 1.0 / KVL, EPS, ALU.mult, ALU.add), reads=[B_st4], writes=[B_st4])
            P.op("act", lambda e, rows=rows: e.activation(out=st4[0:rows, 6:7], in_=st4[0:rows, 5:6], func=AF.Ln), reads=[B_st4], writes=[B_st4])
            P.op("act", lambda e, rows=rows: e.activation(out=st4[0:rows, 7:8], in_=st4[0:rows, 6:7], func=AF.Exp, scale=-0.5), reads=[B_st4], writes=[B_st4])
            P.op("dve", lambda e, ckv=ckv, rows=rows: e.scalar_tensor_tensor(out=lko[0:rows, 0:KVL], in0=ckv, scalar=st4[0:rows, 7:8], in1=gkv_s[0:rows, :],
                                                                     op0=ALU.mult, op1=ALU.mult), reads=[B_ckq[ti], B_st4, B_vec], writes=[B_lko])
            x1 = ckq[0:rows, ti, 640:672]; x2 = ckq[0:rows, ti, 672:704]
            cos = cst[0:rows, ti, 0:32]; sin = cst[0:rows, ti, 32:64]
            P.op("dve", lambda e, x1=x1, cos=cos, rows=rows: e.tensor_tensor(out=rp4[0:rows, 0, :], in0=x1, in1=cos, op=ALU.mult), reads=[B_ckq[ti], B_cst], writes=[B_rp4])
            P.op("dve", lambda e, x2=x2, sin=sin, rows=rows: e.tensor_tensor(out=rp4[0:rows, 1, :], in0=x2, in1=sin, op=ALU.mult), reads=[B_ckq[ti], B_cst], writes=[B_rp4])
            P.op("dve", lambda e, x1=x1, sin=sin, rows=rows: e.tensor_tensor(out=rp4[0:rows, 2, :], in0=x1, in1=sin, op=ALU.mult), reads=[B_ckq[ti], B_cst], writes=[B_rp4])
            P.op("dve", lambda e, x2=x2, cos=cos, rows=rows: e.tensor_tensor(out=rp4[0:rows, 3, :], in0=x2, in1=cos, op=ALU.mult), reads=[B_ckq[ti], B_cst], writes=[B_rp4])
            P.op("dve", lambda e, rows=rows: e.tensor_tensor(out=lko[0:rows, 256:288], in0=rp4[0:rows, 0, :], in1=rp4[0:rows, 1, :], op=ALU.subtract), reads=[B_rp4, B_lko], writes=[B_lko])
            P.op("dve", lambda e, rows=rows: e.tensor_tensor(out=lko[0:rows, 288:320], in0=rp4[0:rows, 2, :], in1=rp4[0:rows, 3, :], op=ALU.add), reads=[B_rp4, B_lko], writes=[B_lko])
            P.dma("pool", lambda e, tc=tc, rows=rows: e.dma_start(out=lat_o[tok0 + tc:tok0 + tc + rows, :], in_=lko[0:rows, 0:KVL]), reads=[B_lko], writes=[B_out])
            P.dma("pool", lambda e, tc=tc, rows=rows: e.dma_start(out=kr_o[tok0 + tc:tok0 + tc + rows, :], in_=lko[0:rows, 256:320]), reads=[B_lko], writes=[B_out])
            P.op("act", lambda e, rows=rows: e.copy(xnb[0:rows, 0:KVL], lko[0:rows, 0:KVL]), reads=[B_lko], writes=[B_xnb])
            for k in range(2):
                P.op("pe", lambda e, k=k, rows=rows: e.transpose(pT[:, k * 128:k * 128 + rows], xnb[0:rows, k * 128:(k + 1) * 128], identb[0:rows, 0:rows]),
                     reads=[B_xnb, B_identb], writes=[B_pT])
            P.op("act", lambda e, tc=tc, rows=rows: e.copy(ckvT[:, :, tc:tc + rows], pT[:, 0:256].rearrange("p (k t) -> p k t", k=2)[:, :, 0:rows]), reads=[B_pT], writes=[B_ckvT[ti]])
            P.op("pe", lambda e, rows=rows: e.transpose(pM[4][0:64, 0:rows], lko[0:rows, 256:320], ident[0:rows, 0:rows]), reads=[B_lko, B_ident], writes=[B_pM[4]])
            if sample:
                P.op("act", lambda e, rows=rows: e.copy(krTn[0:64, 0:rows], pM[4][0:64, 0:rows]), reads=[B_pM[4]], writes=[B_krTn])
                P.op("act", lambda e, rows=rows: e.copy(cnew[0:rows, 2:2 + KVL], lko[0:rows, 0:KVL]), reads=[B_lko], writes=[B_cnew])
            else:
                P.op("act", lambda e, tc=tc, rows=rows: e.copy(krT[:, pos0 + tc:pos0 + tc + rows], pM[4][0:64, 0:rows]), reads=[B_pM[4]], writes=[B_kr])
        if sample:
            for t_ in range(ntok):
                P.dma("sp", lambda e, t_=t_: e.dma_start(out=csf[:, :, t_:t_ + 1], in_=cs_fm[0:64, :, S:S + 1], allow_slow_non_contiguous=True), writes=[B_csf] + B_ckq)
        else:
            P.dma("sp", lambda e: e.dma_start(out=csf[:], in_=cs_fm[0:64, :, pos0:pos0 + GT]), writes=[B_csf] + B_ckq)
        for c in range(0 if sample else 4):
            wv, Bw = wload(w_uk[:, c * 256:(c + 1) * 256], 2, 256)
            proj_fm(wv, Bw, 2, 256, ckvT, B_ckvT, ntok,
                    lambda pa, Bp, cb, t0, n, c=c: P.op("act", lambda e: e.copy(knT[:, 2 * c + cb, pos0 + t0:pos0 + t0 + n], pa), reads=[Bp], writes=[B_kn]))
        for c in range(0 if sample else 2):
            wv, Bw = wload(w_uv[:, c * 512:(c + 1) * 512], 2, 512)
            proj_tm(wv, Bw, 2, 512, ckvT, B_ckvT, tiles,
                    lambda pa, Bp, ti, rows, c=c: P.op("act", lambda e: e.copy(
                        Vh[0:rows, (pos0 + tiles[ti][0]) // 128, 4 * c:4 * c + 4, 0:128], pa.rearrange("p (a b) -> p a b", a=4)), reads=[Bp], writes=[B_Vh]))
        for hd in range(NH):
            wv, Bw = wload(w_uq[:, hd * 192:hd * 192 + 128], 3, 128)
            proj_fm(wv, Bw, 3, 128, cqnT, B_cqn, ntok,
                    lambda pa, Bp, cb, t0, n, hd=hd: P.op("act", lambda e: e.copy(qnT[:, hd, t0:t0 + n], pa), reads=[Bp], writes=[B_qn]))
        for hd in range(NH):
            wv, Bw = wload(w_uq[:, hd * 192 + 128:hd * 192 + 192], 3, 64)
            P.op("pool", lambda e, wv=wv: e.tensor_scalar(wrot[:, :, 0:32], wv[:, :, 32:64], -1.0, None, ALU.mult), reads=[Bw], writes=[B_wrot])
            P.op("pool", lambda e, wv=wv: e.tensor_copy(wrot[:, :, 32:64], wv[:, :, 0:32]), reads=[Bw], writes=[B_wrot])
            for t0 in range(0, ntok, 512):
                n = min(512, ntok - t0)
                px, Bpx = next_pA()
                for k in range(3):
                    P.op("pe", lambda e, k=k, px=px, wv=wv, t0=t0, n=n: e.matmul(px[0:64, 0:n], lhsT=wv[:, k, :], rhs=cqnT[:, k, t0:t0 + n], start=(k == 0), stop=(k == 2)),
                         reads=[Bw] + B_cqn, writes=[Bpx])
                pr, Bpr = next_pA()
                for k in range(3):
                    P.op("pe", lambda e, k=k, pr=pr, t0=t0, n=n: e.matmul(pr[0:64, 0:n], lhsT=wrot[:, k, :], rhs=cqnT[:, k, t0:t0 + n], start=(k == 0), stop=(k == 2)),
                         reads=[B_wrot] + B_cqn, writes=[Bpr])
                P.op("dve", lambda e, px=px, t0=t0, n=n: e.tensor_tensor(out=t64[:, 0, 0:n], in0=px[0:64, 0:n], in1=csf[:, 0, t0:t0 + n], op=ALU.mult), reads=[Bpx, B_csf], writes=[B_t64] + B_ckq)
                P.op("dve", lambda e, pr=pr, t0=t0, n=n: e.tensor_tensor(out=t64[:, 1, 0:n], in0=pr[0:64, 0:n], in1=csf[:, 1, t0:t0 + n], op=ALU.mult), reads=[Bpr, B_csf, B_t64], writes=[B_t64] + B_ckq)
                P.op("dve", lambda e, hd=hd, t0=t0, n=n: e.tensor_tensor(out=qrT[0:64, hd, t0:t0 + n], in0=t64[:, 0, 0:n], in1=t64[:, 1, 0:n], op=ALU.add), reads=[B_t64, B_csf] + B_ckq, writes=[B_qr])
        if sample:
            sample_attention(ntok)
            w_out_apply(w_o, tiles, hmT, B_hm)
            return
        accb = [(pM[0], B_pM[0], (0, 1, 2)), (pM[1], B_pM[1], (3, 4, 5)), (pM[2], B_pM[2], (6, 7))]
        pctr = 0
        for ti, (tc, rows) in enumerate(tiles):
            qi = (pos0 + tc) // 128
            for j in range(qi + 1):
                pb = pctr % 2; pctr += 1
                for half in range(2):
                    pa, Bp = next_pA()
                    for hh in range(4):
                        hd = half * 4 + hh
                        P.op("pe", lambda e, pa=pa, hh=hh, hd=hd, j=j, tc=tc: e.matmul(
                            pa[:, hh * 128:(hh + 1) * 128], lhsT=knT[:, hd, j * 128:(j + 1) * 128], rhs=qnT[:, hd, tc:tc + 128], start=True, stop=False),
                            reads=[B_kn, B_qn], writes=[Bp])
                        P.op("pe", lambda e, pa=pa, hh=hh, hd=hd, j=j, tc=tc: e.matmul(
                            pa[:, hh * 128:(hh + 1) * 128], lhsT=krT[0:64, j * 128:(j + 1) * 128], rhs=qrT[0:64, hd, tc:tc + 128], start=False, stop=True),
                            reads=[B_kr, B_qr], writes=[Bp])
                    P.op("act", lambda e, pa=pa, pb=pb, half=half: e.activation(
                        out=PTb[:, pb, half * 4:(half + 1) * 4, :], in_=pa[:, :].rearrange("p (a b) -> p a b", a=4), func=AF.Exp, scale=MLA_SCALE),
                        reads=[Bp], writes=[B_PT[pb]])
                if j == qi:
                    P.op("pool", lambda e, pb=pb: e.tensor_tensor(out=PTb[:, pb], in0=PTb[:, pb], in1=maskTb[:, :].unsqueeze(1).to_broadcast([128, NH, 128]), op=ALU.mult),
                         reads=[B_PT[pb], B_maskT], writes=[B_PT[pb]])
                for (pm, Bm, hds) in accb:
                    for si, hd in enumerate(hds):
                        P.op("pe", lambda e, pm=pm, si=si, hd=hd, pb=pb, j=j, first=(j == 0 and si == 0), last=(j == qi and si == len(hds) - 1): e.matmul(
                            pm[:, si * 129:(si + 1) * 129], lhsT=PTb[:, pb, hd, :], rhs=Vh[:, j, hd, :], start=first, stop=last, skip_group_check=True),
                            reads=[B_PT[pb], B_Vh], writes=[Bm])
            for (pm, Bm, hds) in accb:
                nh_ = len(hds); h0 = hds[0]
                pv = pm[:, 0:nh_ * 129].rearrange("p (a b) -> p a b", a=nh_)
                P.op("dve", lambda e, pv=pv, h0=h0, nh_=nh_: e.reciprocal(orc[:, h0:h0 + nh_], pv[:, :, 128]), reads=[Bm], writes=[B_orc])
                P.op("dve", lambda e, pv=pv, h0=h0, nh_=nh_: e.tensor_tensor(
                    out=attb[:, h0 * 128:(h0 + nh_) * 128].rearrange("p (a b) -> p a b", a=nh_), in0=pv[:, :, 0:128],
                    in1=orc[:, h0:h0 + nh_].unsqueeze(2).to_broadcast([128, nh_, 128]), op=ALU.mult), reads=[Bm, B_orc], writes=[B_att])
            for k in range(KC):
                P.op("pe", lambda e, k=k: e.transpose(pT[:, k * 128:(k + 1) * 128], attb[:, k * 128:(k + 1) * 128], identb[:, :]), reads=[B_att, B_identb], writes=[B_pT])
            P.op("act", lambda e, tc=tc: e.copy(hmT[:, :, tc:tc + 128], pT[:, :].rearrange("p (k t) -> p k t", k=KC)), reads=[B_pT], writes=[B_hm[ti]])
        w_out_apply(w_o, tiles, hmT, B_hm)

    B_out = Buf("out")
    if cfg.get("do_sample", True):
        sample_group()
    for seq in range(NSP if cfg.get("do_prompt", True) else 0):
        for g in range(NG):
            tok0 = seq * S + g * GT
            tiles = [(t * 128, 128) for t in range(NT)]
            for t in range(NT):
                P.dma("sp", lambda e, t=t, tok0=tok0: e.dma_start(out=h[:, t, :], in_=xp[tok0 + t * 128: tok0 + (t + 1) * 128, :]), writes=[B_h[t]])
            upto = cfg.get("upto", 99)
            for t in range(NT):
                if upto >= 1:
                    rmsnorm_T(h[:, t, :], B_h[t], 128, 0, xnT[:, :, t * 128:(t + 1) * 128], B_xnT[t])
            if upto >= 2:
                mlstm_project(tiles, GT)
            if upto >= 3:
                mlstm_gate_rows(tiles)
            if upto < 3:
                pass
            elif g == 0:
                P.op("pool", lambda e: e.memset(Bext[:, 0:1], 0.0), writes=[B_scan])
                P.op("pool", lambda e: e.memset(mext[:, 0:1], 0.0), writes=[B_scan])
                P.op("pool", lambda e: e.memset(mu[:, :], 0.0), writes=[B_mu])
                P.op("pool", lambda e: e.memset(Chat[:], 0.0), writes=[B_Ch])
            else:
                P.op("dve", lambda e: e.tensor_copy(Bext[:, 0:1], Bext[:, GT:GT + 1]), reads=[B_scan], writes=[B_scan])
                P.op("dve", lambda e: e.tensor_copy(mext[:, 0:1], mext[:, GT:GT + 1]), reads=[B_scan], writes=[B_scan])
            if upto >= 3:
              P.op("dve", lambda e: e.tensor_tensor_scan(out=Bext[:, 1:GT + 1], data0=frow[:, 0:GT], data1=zeros8[:, 0:GT], initial=Bext[:, 0:1],
                                                       op0=ALU.add, op1=ALU.add), reads=[B_rows, B_z8, B_scan], writes=[B_scan])
            if upto >= 3:
              P.op("dve", lambda e: e.tensor_tensor_scan(out=mext[:, 1:GT + 1], data0=frow[:, 0:GT], data1=irow[:, 0:GT], initial=mext[:, 0:1],
                                                       op0=ALU.add, op1=ALU.max), reads=[B_rows, B_scan], writes=[B_scan])
            for t in range(NT if upto >= 4 else 0):
                mlstm_chunk(128, t * 128, vaug[:, t], B_v[t], so[:, t, :], B_so[t], hmT[:, :, t * 128:(t + 1) * 128], B_hm[t],
                            Bext[:, t * 128:t * 128 + 1], Bext[:, t * 128 + 1:(t + 1) * 128 + 1], irow[:, t * 128:(t + 1) * 128], maskT[:, :], g == 0 and t == 0)
            if upto >= 5:
                w_out_apply(w_out_ml, tiles, hmT, B_hm)
            if g == NG - 1 and upto >= 4:
                P.op("dve", lambda e: e.tensor_tensor(out=rs8[:, 8:9], in0=mu[:, 0:1], in1=mext[:, GT:GT + 1], op=ALU.subtract), reads=[B_mu, B_scan, B_rs8], writes=[B_rs8])
                P.op("act", lambda e: e.activation(out=rs8[:, 9:10], in_=rs8[:, 8:9], func=AF.Exp), reads=[B_rs8], writes=[B_rs8])
                P.op("dve", lambda e: e.tensor_scalar(dg8[:, :], pairsel[:, :], rs8[:, 9:10], None, ALU.mult), reads=[B_rs8, B_sel, B_dg8], writes=[B_dg8])
                P.op("pe", lambda e: e.matmul(pM[0][:, 16:20], lhsT=parsel[:, :], rhs=dg8[:, :], start=True, stop=True), reads=[B_dg8, B_sel], writes=[B_pM[0]])
                P.op("act", lambda e: e.copy(decb[:, :], pM[0][:, 16:20]), reads=[B_pM[0]], writes=[B_decb])
                P.op("dve", lambda e: e.tensor_tensor(out=nd[:, 0:4, :], in0=Chat[:], in1=decb[:, :].unsqueeze(2).to_broadcast([128, 4, 129]), op=ALU.mult),
                     reads=[B_Ch, B_decb, B_nd], writes=[B_nd])
                for hp in range(4):
                    P.dma("pool", lambda e, hp=hp, seq=seq: e.dma_start(out=Cp[seq, 2 * hp:2 * hp + 2, :, :].rearrange("a d e -> (a d) e"), in_=nd[:, hp, 0:128]),
                          reads=[B_nd], writes=[B_out])
                    P.dma("pool", lambda e, hp=hp, seq=seq: e.dma_start(out=np_[seq, 2 * hp:2 * hp + 2, :].rearrange("a (d o) -> (a d) o", o=1), in_=nd[:, hp, 128:129]),
                          reads=[B_nd], writes=[B_out])
                P.dma("pool", lambda e, seq=seq: e.dma_start(out=mp[seq:seq + 1, :].rearrange("o h -> h o"), in_=mext[:, GT:GT + 1]), reads=[B_scan], writes=[B_out])
            if stage >= 2:
                for t in range(NT):
                    rmsnorm_T(h[:, t, :], B_h[t], 128, 1, xnT[:, :, t * 128:(t + 1) * 128], B_xnT[t])
                ffn(0, tiles, GT)
            if stage >= 3:
                for t in range(NT):
                    rmsnorm_T(h[:, t, :], B_h[t], 128, 2, xnT[:, :, t * 128:(t + 1) * 128], B_xnT[t])
                mla_mix(seq, g, tiles)
            if stage >= 4:
                for t in range(NT):
                    rmsnorm_T(h[:, t, :], B_h[t], 128, 3, xnT[:, :, t * 128:(t + 1) * 128], B_xnT[t])
                ffn(1, tiles, GT)
                load_gfin()
                for t in range(NT):
                    final_norm(h[:, t, :], B_h[t], 128)
            for t in range(NT):
                P.dma("pool", lambda e, t=t, tok0=tok0: e.dma_start(out=yp[tok0 + t * 128: tok0 + (t + 1) * 128, :], in_=h[:, t, :]), reads=[B_h[t]], writes=[B_out])

    P.emit()
    return nc, es


def host_consts(S, past_len):
    ident = np.eye(128, dtype=np.float32)
    maskT = np.triu(np.ones((128, 128), np.float32))
    parsel = np.zeros((8, 128), np.float32)
    for k in range(8):
        parsel[k, (k % 2) * 64:(k % 2) * 64 + 64] = 1.0
    pairsel = np.zeros((8, 4), np.float32)
    for k in range(8):
        pairsel[k, k // 2] = 1.0
    inv = (10000.0 ** (-np.arange(0, 64, 2, dtype=np.float32) / np.float32(64))).astype(np.float32)
    pos = np.concatenate([np.arange(S, dtype=np.float32), np.array([past_len], np.float32)])
    ang = (pos[:, None] * inv[None, :]).astype(np.float32)
    cs_tm = np.concatenate([np.cos(ang), np.sin(ang)], axis=1).astype(np.float32)
    cs_fm = np.zeros((128, 2, S + 1), np.float32)
    for p in range(128):
        cs_fm[p, 0] = np.cos(ang[:, p % 32])
        cs_fm[p, 1] = np.sin(ang[:, p % 32])
    oh16 = np.ascontiguousarray(np.broadcast_to(np.eye(16, dtype=np.float32)[None], (128, 16, 16)))
    return dict(oh16=oh16, ident=ident, maskT=maskT, parsel=parsel, pairsel=pairsel, cs_tm=cs_tm, cs_fm=cs_fm)


def make_in_maps(inputs, cfg, ncores):
    S, NSP, NS, NPG, NPHYS = cfg["S"], cfg["NSP"], cfg["NS"], cfg["NPG"], cfg["NPHYS"]
    f = lambda a: np.ascontiguousarray(np.asarray(a))
    consts = host_consts(S, NPG * 128)
    gfm = np.stack([f(inputs["norm_mix"])[0], f(inputs["norm_ffn"])[0], f(inputs["norm_mix"])[1], f(inputs["norm_ffn"])[1],
                    f(inputs["norm_final"])], axis=0)
    gfm = np.ascontiguousarray(gfm.reshape(5, KC, 128).transpose(2, 0, 1))
    shared = dict(
        gfm=gfm, gqfm=np.ascontiguousarray(f(inputs["mla_g_q"])[0].reshape(3, 128).T), g_fin=f(inputs["norm_final"]).reshape(1, D), w_in_ml=f(inputs["mlstm_w_in"])[0], b_gates=f(inputs["mlstm_b_gates"])[0].reshape(1, 16),
        g_head=f(inputs["mlstm_g_head"])[0].reshape(1, D), w_out_ml=f(inputs["mlstm_w_out"])[0],
        w_in_mla=f(inputs["mla_w_in"])[0], g_q=f(inputs["mla_g_q"])[0].reshape(1, QL), g_kv=f(inputs["mla_g_kv"])[0].reshape(1, KVL),
        w_uq=f(inputs["mla_w_uq"])[0], w_uk=f(inputs["mla_w_uk"])[0], w_uv=f(inputs["mla_w_uv"])[0], w_o=f(inputs["mla_w_o"])[0],
        w_gu=f(inputs["ffn_w_gate_up"]), w_dn=f(inputs["ffn_w_down"]), **consts)
    xp_all = f(inputs["x_prompt"]); xs_all = f(inputs["x_sample"])
    if cfg.get("use_cache", False):
        latkr_all = np.concatenate([f(inputs["cache_latent"])[0].reshape(NPHYS * 128, KVL), f(inputs["cache_k_rope"])[0].reshape(NPHYS * 128, RP)], axis=1)
    maps = []
    for c in range(ncores):
        m = dict(shared)
        m["xp"] = xp_all[c * NSP:(c + 1) * NSP].reshape(NSP * S, D)
        m["xs"] = xs_all[c * NS:(c + 1) * NS].reshape(NS, D)
        m["stC"] = f(inputs["state_mlstm_C"])[0, c * NS:(c + 1) * NS]
        m["stn"] = f(inputs["state_mlstm_n"])[0, c * NS:(c + 1) * NS]
        m["stm"] = f(inputs["state_mlstm_m"])[0, c * NS:(c + 1) * NS]
        if cfg.get("use_cache", False):
            m["latkr"] = latkr_all
            m["ptb"] = f(inputs["page_table"])[c * NS:(c + 1) * NS].reshape(1, NS * NPG).astype(np.int32)
        maps.append(m)
    return maps


def gather_outputs(res, cfg, ncores):
    S, NSP, NS = cfg["S"], cfg["NSP"], cfg["NS"]
    R = res.results
    cat = lambda k: np.concatenate([R[c][k] for c in range(ncores)], axis=0)
    y_p = cat("yp").reshape(ncores * NSP, S, D)
    y_s = cat("ys").reshape(ncores * NS, 1, D)
    return (y_p, y_s, cat("Cp")[None], cat("np")[None], cat("mp")[None], cat("Cs")[None], cat("ns")[None], cat("ms")[None],
            cat("latp").reshape(1, ncores * NSP, S, KVL), cat("krp").reshape(1, ncores * NSP, S, RP),
            cat("lats").reshape(1, ncores * NS, 1, KVL), cat("krs").reshape(1, ncores * NS, 1, RP))


FULL_CFG = dict(S=2048, NSP=2, GT=512, NS=16, NPG=128, NPHYS=20480, use_cache=True)


def kernel(**inputs):
    cfg = dict(FULL_CFG)
    ncores = 8
    nc, es = build(cfg)
    maps = make_in_maps(inputs, cfg, ncores)
    res = run_bass_kernel_spmd(nc, maps, core_ids=list(range(ncores)))
    return gather_outputs(res, cfg, ncores)
```
